# Optimizing a Trainium2 kernel written in Bass

```python
import math
import jax, jax.numpy as jnp
from jax import lax
import numpy as np


D_MODEL = 1024
BATCH = 16
SEQ = 4096
DEPTH = 1

SSD_EXPAND = 2
D_INNER = SSD_EXPAND * D_MODEL
SSD_HEAD_DIM = 64
SSD_HEADS = D_INNER // SSD_HEAD_DIM
SSD_GROUPS = 8
D_STATE = 128
SSD_CONV = 4
SSD_CHUNK = 128
SSD_CONV_DIM = D_INNER + 2 * SSD_GROUPS * D_STATE
SSD_NORM_GROUP = D_INNER // SSD_GROUPS
ATTN_WINDOWS = (128, 512, 2048)
ATTN_DILATIONS = (1, 4, 16)
ATTN_N_GROUPS = 3
ATTN_HEADS_PER_GROUP = 8
ATTN_HEAD_DIM = 64
ATTN_BLOCK = 128
ATTN_WIDTH = ATTN_N_GROUPS * ATTN_HEADS_PER_GROUP * ATTN_HEAD_DIM
ATTN_OUT = ATTN_HEADS_PER_GROUP * ATTN_HEAD_DIM
D_FF = 2816
FFN_CONV = 3
EPS = 1e-6
IN_WIDTHS = (D_INNER, SSD_CONV_DIM, SSD_HEADS, ATTN_WIDTH, ATTN_WIDTH, ATTN_WIDTH, D_MODEL, D_MODEL)
IN_SPLITS = tuple(sum(IN_WIDTHS[:i + 1]) for i in range(len(IN_WIDTHS) - 1))
D_IN_PROJ = sum(IN_WIDTHS)

kernel_name = 'hybrid_ssd_dilated_attn_block'


def rms_norm(x, g):
    xf = x.astype(jnp.float32)
    y = xf * lax.rsqrt(jnp.mean(xf * xf, axis=-1, keepdims=True) + EPS)
    return (y * g.astype(jnp.float32)).astype(x.dtype)


def causal_dwconv(x, w, bias):
    K = w.shape[0]
    s = x.shape[1]
    xp = jnp.pad(x, ((0, 0), (K - 1, 0), (0, 0)))
    out = bias
    for i in range(K):
        out = out + xp[:, i:i + s] * w[i]
    return out


def ssd_chunked(xs, dt, A, Bm, Cm):
    b, s, H, P = xs.shape
    G, N = Bm.shape[2], Bm.shape[3]
    K = H // G
    Q = SSD_CHUNK
    c = s // Q
    X = (xs * dt[..., None]).reshape(b, c, Q, G, K, P)
    a = (dt * A).reshape(b, c, Q, G, K).transpose(0, 1, 3, 4, 2)
    a_cs = jnp.cumsum(a, axis=-1)
    Bc = Bm.reshape(b, c, Q, G, N)
    Cc = Cm.reshape(b, c, Q, G, N)
    causal = jnp.tril(jnp.ones((Q, Q), dtype=bool))
    seg = a_cs[..., :, None] - a_cs[..., None, :]
    Ldec = jnp.exp(jnp.where(causal, seg, -jnp.inf))
    CB = jnp.einsum('bclgn,bcsgn->bcgls', Cc, Bc)
    y_diag = jnp.einsum('bcgkls,bcsgkp->bclgkp', CB[:, :, :, None] * Ldec, X)
    decay_states = jnp.exp(a_cs[..., -1:] - a_cs)
    states = jnp.einsum('bclgn,bcgkl,bclgkp->bcgkpn', Bc, decay_states, X)
    chunk_decay = jnp.exp(a_cs[..., -1])

    def step(carry, inp):
        st, dec = inp
        return carry * dec[..., None, None] + st, carry

    init = jnp.zeros((b, G, K, P, N), jnp.float32)
    _, prev = lax.scan(step, init, (jnp.moveaxis(states, 1, 0), jnp.moveaxis(chunk_decay, 1, 0)))
    prev = jnp.moveaxis(prev, 0, 1)
    y_off = jnp.einsum('bclgn,bcgkpn,bcgkl->bclgkp', Cc, prev, jnp.exp(a_cs))
    return (y_diag + y_off).reshape(b, s, H, P)


def ssd_branch(z, xbc, dt_raw, conv_w, conv_b, dt_bias, a_log, d_skip, norm_g):
    b, s, _ = z.shape
    f32 = jnp.float32
    xbc = jax.nn.silu(causal_dwconv(xbc, conv_w, conv_b))
    xs, Bm, Cm = jnp.split(xbc, (D_INNER, D_INNER + SSD_GROUPS * D_STATE), axis=-1)
    xs = xs.reshape(b, s, SSD_HEADS, SSD_HEAD_DIM).astype(f32)
    Bm = Bm.reshape(b, s, SSD_GROUPS, D_STATE).astype(f32)
    Cm = Cm.reshape(b, s, SSD_GROUPS, D_STATE).astype(f32)
    dt = jax.nn.softplus(dt_raw.astype(f32) + dt_bias.astype(f32))
    A = -jnp.exp(a_log.astype(f32))
    y = ssd_chunked(xs, dt, A, Bm, Cm) + d_skip.astype(f32)[:, None] * xs
    y = y.reshape(b, s, D_INNER) * jax.nn.silu(z.astype(f32))
    y = rms_norm(y.reshape(b, s, SSD_GROUPS, SSD_NORM_GROUP), norm_g.reshape(SSD_GROUPS, SSD_NORM_GROUP))
    return y.reshape(b, s, D_INNER).astype(z.dtype)


def dilated_window_attention(q, k, v, dil, n_back):
    b, s, h, hd = q.shape
    L = s // dil
    nb = -(-L // ATTN_BLOCK)
    Lp = nb * ATTN_BLOCK

    def to_sub(t):
        t = t.reshape(b, L, dil, h, hd).transpose(0, 2, 1, 3, 4).reshape(b * dil, L, h, hd)
        t = jnp.pad(t, ((0, 0), (0, Lp - L), (0, 0), (0, 0)))
        return t.reshape(b * dil, nb, ATTN_BLOCK, h, hd)

    def with_prev(t):
        prev = jnp.pad(t, ((0, 0), (1, 0), (0, 0), (0, 0), (0, 0)))[:, :-1]
        return jnp.concatenate([prev, t], axis=2)

    qb = to_sub(q)
    kc = with_prev(to_sub(k))
    vc = with_prev(to_sub(v))
    scores = jnp.einsum('znqhd,znkhd->znhqk', qb, kc) * (hd ** -0.5)
    qi = jnp.arange(ATTN_BLOCK)[:, None]
    ki = jnp.arange(2 * ATTN_BLOCK)[None, :]
    dist = ATTN_BLOCK + qi - ki
    band = (dist >= 0) & (dist <= n_back)
    key_pos = (jnp.arange(nb)[:, None, None] - 1) * ATTN_BLOCK + ki[None]
    mask = band[None] & (key_pos >= 0)
    scores = jnp.where(mask[None, :, None], scores, -jnp.inf)
    m = jnp.max(scores, axis=-1, keepdims=True)
    p = jnp.exp(scores - m)
    den = jnp.sum(p, axis=-1, keepdims=True)
    o = jnp.einsum('znhqk,znkhd->znqhd', p / den, vc)
    lse = (m + jnp.log(den))[..., 0]
    o = o.reshape(b, dil, Lp, h, hd)[:, :, :L].transpose(0, 2, 1, 3, 4).reshape(b, s, h, hd)
    lse = lse.transpose(0, 1, 3, 2).reshape(b, dil, Lp, h)[:, :, :L].transpose(0, 2, 1, 3).reshape(b, s, h)
    return o, lse


def attn_branch(q, k, v, q_norm_g, k_norm_g):
    b, s, _ = q.shape
    f32 = jnp.float32
    shp = (b, s, ATTN_N_GROUPS, ATTN_HEADS_PER_GROUP, ATTN_HEAD_DIM)
    qn = rms_norm(q.reshape(shp), q_norm_g).astype(f32)
    kn = rms_norm(k.reshape(shp), k_norm_g).astype(f32)
    vv = v.reshape(shp).astype(f32)
    outs, lses = [], []
    for gi in range(ATTN_N_GROUPS):
        o, lse = dilated_window_attention(qn[:, :, gi], kn[:, :, gi], vv[:, :, gi],
                                          ATTN_DILATIONS[gi], ATTN_WINDOWS[gi] // ATTN_DILATIONS[gi])
        outs.append(o)
        lses.append(lse)
    wts = jax.nn.softmax(jnp.stack(lses, axis=0), axis=0)
    o = jnp.einsum('gbsh,gbshd->bshd', wts, jnp.stack(outs, axis=0))
    return o.reshape(b, s, ATTN_OUT).astype(q.dtype)


def setup_inputs(seed: int = 0) -> dict:
    key = jax.random.key(seed)
    ks = jax.random.split(key, 20)
    f32 = jnp.float32

    def dense(k, shape, fan_in):
        return jax.random.normal(k, shape, f32) * fan_in ** -0.5

    def gain(k, shape):
        return 1.0 + 0.05 * jax.random.normal(k, shape, f32)

    x = jax.random.normal(ks[0], (BATCH, SEQ, D_MODEL), f32)
    norm1_g = gain(ks[1], (DEPTH, D_MODEL))
    w_in = dense(ks[2], (DEPTH, D_MODEL, D_IN_PROJ), D_MODEL)
    ssd_conv_w = dense(ks[3], (DEPTH, SSD_CONV, SSD_CONV_DIM), SSD_CONV)
    ssd_conv_b = 0.02 * jax.random.normal(ks[4], (DEPTH, SSD_CONV_DIM), f32)
    dt0 = jnp.exp(jax.random.uniform(ks[5], (DEPTH, SSD_HEADS), f32, math.log(1e-3), math.log(1e-1)))
    dt_bias = dt0 + jnp.log(-jnp.expm1(-dt0))
    a_log = jnp.log(jax.random.uniform(ks[6], (DEPTH, SSD_HEADS), f32, 1.0, 16.0))
    d_skip = 1.0 + 0.1 * jax.random.normal(ks[7], (DEPTH, SSD_HEADS), f32)
    ssd_norm_g = gain(ks[8], (DEPTH, D_INNER))
    w_ssd_proj = dense(ks[9], (DEPTH, D_INNER, D_MODEL), D_INNER)
    q_norm_g = gain(ks[10], (DEPTH, ATTN_HEAD_DIM))
    k_norm_g = gain(ks[11], (DEPTH, ATTN_HEAD_DIM))
    w_attn_proj = dense(ks[12], (DEPTH, ATTN_OUT, D_MODEL), ATTN_OUT)
    w_out = dense(ks[13], (DEPTH, D_MODEL, D_MODEL), D_MODEL)
    norm2_g = gain(ks[14], (DEPTH, D_MODEL))
    w_up = dense(ks[15], (DEPTH, D_MODEL, 2 * D_FF), D_MODEL)
    ffn_conv_w = dense(ks[16], (DEPTH, FFN_CONV, 2 * D_FF), FFN_CONV)
    ffn_conv_b = 0.02 * jax.random.normal(ks[17], (DEPTH, 2 * D_FF), f32)
    w_down = dense(ks[18], (DEPTH, D_FF, D_MODEL), D_FF)
    return {'x': x, 'norm1_g': norm1_g, 'w_in': w_in, 'ssd_conv_w': ssd_conv_w, 'ssd_conv_b': ssd_conv_b,
            'dt_bias': dt_bias, 'a_log': a_log, 'd_skip': d_skip, 'ssd_norm_g': ssd_norm_g,
            'w_ssd_proj': w_ssd_proj, 'q_norm_g': q_norm_g, 'k_norm_g': k_norm_g,
            'w_attn_proj': w_attn_proj, 'w_out': w_out, 'norm2_g': norm2_g, 'w_up': w_up,
            'ffn_conv_w': ffn_conv_w, 'ffn_conv_b': ffn_conv_b, 'w_down': w_down}


def reference(x, norm1_g, w_in, ssd_conv_w, ssd_conv_b, dt_bias, a_log, d_skip, ssd_norm_g,
              w_ssd_proj, q_norm_g, k_norm_g, w_attn_proj, w_out, norm2_g, w_up,
              ffn_conv_w, ffn_conv_b, w_down):
    for l in range(DEPTH):
        h = rms_norm(x, norm1_g[l])
        proj = h @ w_in[l]
        z, xbc, dt_raw, q, k, v, g_ssd, g_attn = jnp.split(proj, IN_SPLITS, axis=-1)
        y_ssd = ssd_branch(z, xbc, dt_raw, ssd_conv_w[l], ssd_conv_b[l], dt_bias[l], a_log[l],
                           d_skip[l], ssd_norm_g[l])
        y_attn = attn_branch(q, k, v, q_norm_g[l], k_norm_g[l])
        merged = (jax.nn.sigmoid(g_ssd) * (y_ssd @ w_ssd_proj[l])
                  + jax.nn.sigmoid(g_attn) * (y_attn @ w_attn_proj[l]))
        x = x + merged @ w_out[l]
        h2 = rms_norm(x, norm2_g[l])
        u = causal_dwconv(h2 @ w_up[l], ffn_conv_w[l], ffn_conv_b[l])
        u_gate, u_val = jnp.split(u, 2, axis=-1)
        x = x + (jax.nn.silu(u_gate) * u_val) @ w_down[l]
    return x
```

```python
import contextlib
import numpy as np
import ml_dtypes
import concourse.bass as bass
import concourse.mybir as mybir
from concourse.bass_utils import run_bass_kernel_spmd

F32 = mybir.dt.float32
BF16 = mybir.dt.bfloat16
AF = mybir.ActivationFunctionType
ALU = mybir.AluOpType

ENGS = ("pe", "act", "dve", "pool", "sp")
SMALL_N = 256


def region(ap):
    t = ap.tensor
    dims = ap.ap
    off = int(ap.offset)
    name = t.name
    cls = type(t).__name__
    if cls.startswith("DRam"):
        lo = off
        hi = off + sum((c - 1) * abs(s) for s, c in dims) + 1
        return (name, 0, 1, lo, hi)
    if cls.startswith("PSum"):
        return (name, 0, 128, 0, 1 << 30)
    ps, npart = dims[0]
    p0 = off // ps
    f0 = off % ps
    f1 = f0 + sum((c - 1) * abs(s) for s, c in dims[1:]) + 1
    return (name, p0, p0 + npart, f0, f1)


class Op:
    __slots__ = ("eng", "fn", "idx", "deps", "signaled", "dma_slot", "dma_cnt", "nfree", "sig_no")

    def __init__(self, eng, fn):
        self.eng = eng
        self.fn = fn
        self.deps = {}
        self.signaled = False
        self.dma_slot = None
        self.dma_cnt = 0
        self.nfree = 1 << 30
        self.sig_no = 0


class Prog:
    def __init__(self, nc):
        self.nc = nc
        self.ops = []
        self.recs = {}
        self.slot_last = {}
        self.slot_cnt = {}
        self.barrier_idx = {}

    def _track(self, op, reads, writes):
        deps = op.deps
        ekey = op.eng if op.dma_slot is None else ("dma", op.dma_slot)

        def add_dep(r):
            k = r[5]
            if k == ekey and op.dma_slot is None:
                if k == "pe":
                    return
            if deps.get(k, -1) < r[6]:
                deps[k] = r[6]

        psum_reads = [ap for ap in reads if type(ap.tensor).__name__.startswith("PSum")]
        if psum_reads:
            reads = [ap for ap in reads if not type(ap.tensor).__name__.startswith("PSum")]
            writes = list(writes) + psum_reads
        for ap in reads:
            name, p0, p1, f0, f1 = region(ap)
            lst = self.recs.setdefault(name, [])
            found = None
            for r in lst:
                if r[4] == "w":
                    if r[0] < p1 and p0 < r[1] and r[2] < f1 and f0 < r[3]:
                        add_dep(r)
                elif r[5] == ekey and r[0] == p0 and r[1] == p1 and r[2] == f0 and r[3] == f1:
                    found = r
            if found is not None:
                found[6] = op.idx
            else:
                lst.append([p0, p1, f0, f1, "r", ekey, op.idx, op.nfree])
        for ap in writes:
            name, p0, p1, f0, f1 = region(ap)
            lst = self.recs.setdefault(name, [])
            keep = []
            for r in lst:
                if r[0] < p1 and p0 < r[1] and r[2] < f1 and f0 < r[3]:
                    if r[6] != op.idx:
                        add_dep(r)
                    if r[0] >= p0 and r[1] <= p1 and r[2] >= f0 and r[3] <= f1 and r[6] != op.idx:
                        continue
                keep.append(r)
            keep.append([p0, p1, f0, f1, "w", ekey, op.idx, op.nfree])
            self.recs[name] = keep

    def add(self, eng, fn, reads=(), writes=(), nfree=None):
        op = Op(eng, fn)
        op.idx = len(self.ops)
        if nfree is None:
            nfree = 1 << 30
            for ap in writes:
                d = ap.ap
                n = 1
                for s, c in d[1:]:
                    n *= c
                nfree = min(nfree, n)
        op.nfree = nfree
        self.ops.append(op)
        self._track(op, reads, writes)
        return op

    def dma(self, out, in_, slot, eng="sp", **kw):
        def fn(e, out=out, in_=in_, kw=kw):
            return e.dma_start(out=out, in_=in_, **kw)

        op = Op(eng, fn)
        op.idx = len(self.ops)
        op.dma_slot = slot
        self.slot_cnt[slot] = self.slot_cnt.get(slot, 0) + 16
        op.dma_cnt = self.slot_cnt[slot]
        self.ops.append(op)
        prev = self.slot_last.get(slot)
        if prev is not None:
            op.deps[("dma", slot)] = prev
        self.slot_last[slot] = op.idx
        self._track(op, [in_], [out])
        return op

    def mm(self, out, lhsT, rhs, start=True, stop=True, **kw):
        if not start:
            kw.setdefault("skip_group_check", True)
        return self.add("pe", lambda e: e.matmul(out, lhsT, rhs, start=start, stop=stop, **kw),
                        [lhsT, rhs], [out])

    def tr(self, out, in_, ident):
        return self.add("pe", lambda e: e.transpose(out, in_, ident), [in_, ident], [out])

    def act(self, out, in_, func, bias=None, scale=None, accum_out=None, eng="act"):
        reads = [in_]
        kw = {}
        if bias is not None:
            kw["bias"] = bias
            if not isinstance(bias, (int, float)):
                reads.append(bias)
        if scale is not None:
            kw["scale"] = scale
            if not isinstance(scale, (int, float)):
                reads.append(scale)
        writes = [out]
        if accum_out is not None:
            kw["accum_out"] = accum_out
            writes.append(accum_out)
        return self.add(eng, lambda e: e.activation(out, in_, func, **kw), reads, writes)

    def tt(self, eng, out, in0, in1, op):
        return self.add(eng, lambda e: e.tensor_tensor(out, in0, in1, op), [in0, in1], [out])

    def ts(self, eng, out, in0, s1, s2, op0, op1=None, accum_out=None):
        reads = [in0]
        if not isinstance(s1, (int, float)):
            reads.append(s1)
        if s2 is not None and not isinstance(s2, (int, float)):
            reads.append(s2)
        kw = {}
        writes = [out]
        if accum_out is not None:
            kw["accum_out"] = accum_out
            writes.append(accum_out)
        if op1 is None:
            return self.add(eng, lambda e: e.tensor_scalar(out, in0, s1, None, op0, **kw), reads, writes)
        return self.add(eng, lambda e: e.tensor_scalar(out, in0, s1, s2, op0, op1, **kw), reads, writes)

    def stt(self, eng, out, in0, scalar, in1, op0, op1):
        reads = [in0, in1]
        if not isinstance(scalar, (int, float)):
            reads.append(scalar)
        return self.add(eng, lambda e: e.scalar_tensor_tensor(out, in0, scalar, in1, op0, op1), reads, [out])

    def scale(self, eng, out, in_, sc):
        if eng == "act":
            return self.act(out, in_, AF.Copy, scale=sc)
        return self.ts(eng, out, in_, sc, None, ALU.mult)

    def copy(self, eng, out, in_):
        if eng == "act":
            return self.add(eng, lambda e: e.copy(out, in_), [in_], [out])
        return self.add(eng, lambda e: e.tensor_copy(out, in_), [in_], [out])

    def memset(self, eng, ap, val):
        return self.add(eng, lambda e: e.memset(ap, val), [], [ap])

    def barrier(self, final=False):
        op = Op("sp", None)
        op.idx = len(self.ops)
        self.ops.append(op)
        for slot, i in self.slot_last.items():
            op.deps[("dma", slot)] = i
        if not final:
            m = Op("marker", None)
            m.idx = len(self.ops)
            self.ops.append(m)
        self.recs = {}

    def emit(self):
        nc = self.nc
        ops = self.ops
        for op in ops:
            for k, i in op.deps.items():
                ops[i].signaled = True
        cnt = {e: 0 for e in ENGS}
        for op in ops:
            if op.eng == "marker":
                continue
            if op.dma_slot is None and op.signaled:
                cnt[op.eng] += 1
                op.sig_no = cnt[op.eng]
        slots = sorted(self.slot_cnt.keys())
        import contextlib

        with contextlib.ExitStack() as st:
            esem = {e: st.enter_context(nc.semaphore("s_" + e)) for e in ENGS}
            ssem = {s: st.enter_context(nc.semaphore("d_" + s)) for s in slots}
            segs = [[]]
            for op in ops:
                if op.eng == "marker":
                    segs.append([])
                else:
                    segs[-1].append(op)
            stats = {e: [0, 0] for e in ENGS}
            waited_all = {e: {} for e in ENGS}

            def run(eng_name, e, seg):
                waited = waited_all[eng_name]
                for op in seg:
                    if op.eng != eng_name:
                        continue
                    for k, i in op.deps.items():
                        p = ops[i]
                        if isinstance(k, tuple):
                            sem, val = ssem[k[1]], p.dma_cnt
                        else:
                            sem, val = esem[k], p.sig_no
                        if waited.get(k, 0) >= val:
                            continue
                        waited[k] = val
                        e.wait_ge(sem, val)
                        stats[eng_name][1] += 1
                    if op.fn is None:
                        continue
                    ins = op.fn(e)
                    stats[eng_name][0] += 1
                    if op.dma_slot is not None:
                        ins.then_inc(ssem[op.dma_slot], 16)
                    elif op.signaled:
                        ins.then_inc(esem[eng_name], 1)

            for seg in segs:
                with nc.Block() as block:
                    @block.tensor
                    def _(e, seg=seg):
                        run("pe", e, seg)

                    @block.scalar
                    def _(e, seg=seg):
                        run("act", e, seg)

                    @block.vector
                    def _(e, seg=seg):
                        run("dve", e, seg)

                    @block.gpsimd
                    def _(e, seg=seg):
                        run("pool", e, seg)

                    @block.sync
                    def _(e, seg=seg):
                        run("sp", e, seg)

            self.stats = stats
            self.sig_counts = cnt


D = 1024
KC = 8
S_FULL = 4096
C_Z, C_X, C_B, C_C, C_DT, C_Q, C_K, C_V, C_GS, C_GA = 0, 2048, 4096, 5120, 6144, 6176, 7712, 9248, 10784, 11808
DIL = (1, 4, 16)
EPS = 1e-6
DFF = 2816
NFC = 22
PV_G1, PV_G2, PV_GS, PV_CW, PV_CB, PV_D, PV_QG, PV_KG, PV_FW, PV_FB, NPV = 0, 8, 16, 32, 160, 192, 208, 209, 210, 342, 386
CM_ID, CM_U, CM_L, CM_GT, CM_ONE, CM_AM, CM_BD, NCM = 0, 128, 256, 384, 512, 640, 1152, 1280

_bf = ml_dtypes.bfloat16


def _fm(v):
    return np.ascontiguousarray(np.asarray(v).reshape(-1, 128).T)


def pack_pvec(inp):
    pv = np.zeros((128, NPV), np.float32)
    pv[:, PV_G1:PV_G1 + 8] = _fm(inp["norm1_g"][0])
    pv[:, PV_G2:PV_G2 + 8] = _fm(inp["norm2_g"][0])
    pv[:, PV_GS:PV_GS + 16] = _fm(inp["ssd_norm_g"][0])
    for i in range(4):
        pv[:, PV_CW + i * 32:PV_CW + (i + 1) * 32] = _fm(inp["ssd_conv_w"][0, i])
    pv[:, PV_CB:PV_CB + 32] = _fm(inp["ssd_conv_b"][0])
    pv[:, PV_D:PV_D + 16] = _fm(np.repeat(inp["d_skip"][0], 64))
    pv[:, PV_QG] = np.tile(inp["q_norm_g"][0], 2)
    pv[:, PV_KG] = np.tile(inp["k_norm_g"][0], 2)
    for i in range(3):
        pv[:, PV_FW + i * 44:PV_FW + (i + 1) * 44] = _fm(inp["ffn_conv_w"][0, i])
    pv[:, PV_FB:PV_FB + 44] = _fm(inp["ffn_conv_b"][0])
    return pv


def make_cmask():
    cm = np.zeros((128, NCM), np.float32)
    i = np.arange(128)
    U = (i[:, None] <= i[None, :]).astype(np.float32)
    L = (i[:, None] >= i[None, :]).astype(np.float32)
    GT = (i[:, None] > i[None, :]).astype(np.float32)
    cm[:, CM_ID:CM_ID + 128] = np.eye(128)
    cm[:, CM_U:CM_U + 128] = U
    cm[:, CM_L:CM_L + 128] = L
    cm[:, CM_GT:CM_GT + 128] = GT
    cm[:, CM_ONE:CM_ONE + 128] = 1.0
    cm[:, CM_AM:CM_AM + 512] = np.concatenate([L, U, L, U], 1)
    bd = np.zeros((128, 128), np.float32)
    bd[:64, :64] = 1.0
    bd[64:, 64:] = 1.0
    cm[:, CM_BD:CM_BD + 128] = bd
    return cm.astype(_bf)


def build(NS=2, S=S_FULL, debug=False, phases=("norm", "attn", "ssd", "merge", "ffn")):
    nc = bass.Bass("TRN2", target_bir_lowering=False)
    NT = S // 512
    NB = S // 128

    def din(name, shape, dt=F32):
        return nc.dram_tensor(name, shape, dt, kind="ExternalInput").ap()

    x_d = din("x", [NS, S, D])
    w_in_d = din("w_in", [D, 12832])
    w_ssd_d = din("w_ssd_proj", [2048, D])
    w_att_d = din("w_attn_proj", [512, D])
    w_out_d = din("w_out", [D, D])
    w_up_d = din("w_up", [D, 2 * DFF])
    w_dn_d = din("w_down", [DFF, D])
    pv_d = din("pvec", [128, NPV])
    cm_d = din("cmask", [128, NCM], BF16)
    dtb_d = din("dt_bias", [1, 32])
    alog_d = din("a_log", [1, 32])
    skind = "ExternalOutput" if debug else "Internal"
    hTs_d = nc.dram_tensor("hTs", [NS, KC, 128, S], BF16, kind=skind).ap()
    yat_d = nc.dram_tensor("yattn", [NS, 4, 128, S], BF16, kind=skind).ap()
    ysd_d = nc.dram_tensor("yssd", [NS, 16, 128, S], BF16, kind=skind).ap()
    out_d = nc.dram_tensor("out", [NS, S, D], F32, kind="ExternalOutput").ap()

    P = Prog(nc)
    ES = contextlib.ExitStack

    with ES() as top:
        def sbt(st, name, shape, dt):
            return st.enter_context(nc.sbuf_tensor(name, shape, dt))

        def pst(st, name, dt=F32):
            return st.enter_context(nc.psum_tensor(name, [128, 512 if dt == F32 else 1024], dt))

        pv = sbt(top, "pv", [128, NPV], F32)
        pvh = sbt(top, "pvh", [128, NPV], F32)
        cm = sbt(top, "cm", [128, NCM], BF16)
        P.dma(pv[:], pv_d[:, :], "pv")
        P.dma(cm[:], cm_d[:, :], "cm")
        P.ts("dve", pvh[:], pv[:], 0.5, None, ALU.mult)
        ident = cm[:, CM_ID:CM_ID + 128]
        maskU = cm[:, CM_U:CM_U + 128]
        maskGT = cm[:, CM_GT:CM_GT + 128]
        ones = cm[:, CM_ONE:CM_ONE + 128]
        amask = cm[:, CM_AM:CM_AM + 512]
        bdones = cm[:, CM_BD:CM_BD + 128]
        stg = [sbt(top, f"stg{i}", [128, 512], F32) for i in range(2)]
        stg_i = [0, 0]

        def load_w(dst, src, kn, ncols, scale=None, engs=("act", "dve")):
            SW = 512
            cw = min(ncols, SW)
            kpp = max(1, SW // cw)
            for c0 in range(0, ncols, cw):
                cn = min(cw, ncols - c0)
                for k0 in range(0, kn, kpp):
                    kk = min(kpp, kn - k0)
                    i = stg_i[0] % 2
                    stg_i[0] += 1
                    sv = stg[i][:, 0:kk * cn].rearrange("p (k n) -> p k n", k=kk)
                    P.dma(sv, src[k0 * 128:(k0 + kk) * 128, c0:c0 + cn].rearrange("(k p) n -> p k n", p=128), f"stg{i}")
                    for k in range(kk):
                        d_ = dst[:, k0 + k, c0:c0 + cn]
                        eng = engs[stg_i[1] % len(engs)]
                        stg_i[1] += 1
                        if scale is None:
                            P.copy(eng, d_, sv[:, k, :])
                        elif isinstance(scale, float):
                            P.scale(eng, d_, sv[:, k, :], scale)
                        else:
                            P.scale(eng, d_, sv[:, k, :], scale[:, k0 + k:k0 + k + 1])

        g1 = pv[:, PV_G1:PV_G1 + 8]
        g1h = pvh[:, PV_G1:PV_G1 + 8]

        for b in range(NS):
            with ES() as seq:
                hT = sbt(seq, f"hT{b}", [128, KC, S], BF16)
                if "norm" in phases:
                    with ES() as st:
                        xt = [sbt(st, f"n_xt{b}_{i}", [128, 4, D], F32) for i in range(2)]
                        junk = sbt(st, f"n_junk{b}", [128, D], BF16)
                        h16 = [sbt(st, f"n_h16{b}_{i}", [128, 4, D], BF16) for i in range(2)]
                        ssq = [sbt(st, f"n_ss{b}_{i}", [128, 4], F32) for i in range(2)]
                        lnv = [sbt(st, f"n_ln{b}_{i}", [128, 4], F32) for i in range(2)]
                        rstd = [sbt(st, f"n_rs{b}_{i}", [128, 4], F32) for i in range(2)]
                        pb = [pst(st, f"n_pb{b}_{i}", BF16) for i in range(4)]
                        for j in range(NT):
                            i2 = j % 2
                            P.dma(xt[i2][:], x_d[b, 512 * j:512 * (j + 1), :].rearrange("(k p) d -> p k d", p=128), f"xt{i2}")
                            P.memset("pool", ssq[i2][:], 0.0)
                            for k in range(4):
                                P.act(junk[:], xt[i2][:, k, :], AF.Square, accum_out=ssq[i2][:, k:k + 1])
                            P.act(lnv[i2][:], ssq[i2][:], AF.Ln, scale=1.0 / D, bias=EPS)
                            P.act(rstd[i2][:], lnv[i2][:], AF.Exp, scale=-0.5)
                            for k in range(4):
                                P.scale("dve" if k % 2 == 0 else "act", h16[i2][:, k, :], xt[i2][:, k, :], rstd[i2][:, k:k + 1])
                            for pr in range(4):
                                bank = pb[pr]
                                for kk in range(2):
                                    kc = 2 * pr + kk
                                    for k in range(4):
                                        P.tr(bank[:, kk * 512 + k * 128:kk * 512 + (k + 1) * 128], h16[i2][:, k, kc * 128:(kc + 1) * 128], ident)
                                P.copy("act" if pr % 2 == 0 else "dve", hT[:, 2 * pr:2 * pr + 2, 512 * j:512 * (j + 1)],
                                       bank[:, :].rearrange("p (k t) -> p k t", k=2))
                            P.dma(hTs_d[b, :, :, 512 * j:512 * (j + 1)].rearrange("k p t -> p k t"), hT[:, :, 512 * j:512 * (j + 1)], f"hs{i2}")
                if "attn" in phases:
                    with ES() as st:
                        wq = [sbt(st, f"a_wq{b}_{i}", [128, KC, 128], BF16) for i in range(2)]
                        wk = [sbt(st, f"a_wk{b}_{i}", [128, KC, 128], BF16) for i in range(2)]
                        wv = [sbt(st, f"a_wv{b}_{i}", [128, KC, 128], BF16) for i in range(2)]
                        QT = sbt(st, f"a_QT{b}", [128, S], BF16)
                        KT = sbt(st, f"a_KT{b}", [128, S], BF16)
                        Vt = sbt(st, f"a_Vt{b}", [128, NB, 128], BF16)
                        lnb = sbt(st, f"a_ln{b}", [128, 2 * NT, 512], F32)
                        sq = [sbt(st, f"a_sq{b}_{i}", [128, 512], BF16) for i in range(2)]
                        pt = [sbt(st, f"a_pt{b}_{i}", [128, 512], BF16) for i in range(2)]
                        acc = sbt(st, f"a_acc{b}", [128, 2, S], F32)
                        rden = [sbt(st, f"a_rd{b}_{i}", [128, 512], F32) for i in range(2)]
                        yat = [sbt(st, f"a_yat{b}_{i}", [128, S], BF16) for i in range(2)]
                        bk = [pst(st, f"a_bk{b}_{i}") for i in range(8)]
                        ps_p = [bk[0], bk[1]]
                        ps_n = bk[7]
                        ps_v = bk[6]
                        ps_s = [(bk[0], bk[1]), (bk[2], bk[3])]
                        ps_o = [bk[4], bk[5]]
                        it = 0
                        for hp in range(4):
                            for g in range(3):
                                d = DIL[g]
                                L = S // d
                                nbl = L // 128
                                wi = it % 2
                                it += 1
                                for (wt, c0) in ((wq[wi], C_Q), (wk[wi], C_K), (wv[wi], C_V)):
                                    cc = c0 + g * 512 + hp * 128
                                    load_w(wt, w_in_d[:, cc:cc + 128], KC, 128, scale=g1)

                                def nat_view(T, j):
                                    if d == 1:
                                        return T[:, 512 * j:512 * (j + 1)]
                                    v = T[:, :].rearrange("p (r m) -> p r m", r=d)[:, :, (512 * j) // d:(512 * (j + 1)) // d]
                                    return v.rearrange("p r m -> p m r")

                                def nat_src(a):
                                    if d == 1:
                                        return a
                                    return a.rearrange("p (m r) -> p m r", r=d)

                                idx = 0
                                for (wt, T) in ((wq[wi], QT), (wk[wi], KT)):
                                    for j in range(NT):
                                        pp = ps_p[idx % 2]
                                        for kc in range(KC):
                                            P.mm(pp[:, 0:512], wt[:, kc, :], hT[:, kc, 512 * j:512 * (j + 1)], start=(kc == 0))
                                        P.act(sq[idx % 2][:], pp[:, 0:512], AF.Square)
                                        P.copy("dve", nat_view(T, j), nat_src(pp[:, 0:512]))
                                        P.mm(ps_n[:, 0:512], bdones, sq[idx % 2][:], start=True)
                                        P.act(lnb[:, idx, :], ps_n[:, 0:512], AF.Ln, scale=1.0 / 64, bias=EPS)
                                        idx += 1
                                idx = 0
                                for (T, gcol) in ((QT, PV_QG), (KT, PV_KG)):
                                    for j in range(NT):
                                        P.act(lnb[:, idx, :], lnb[:, idx, :], AF.Exp, scale=-0.5)
                                        P.stt("dve", nat_view(T, j), nat_view(T, j), pv[:, gcol:gcol + 1], nat_src(lnb[:, idx, :]), ALU.mult, ALU.mult)
                                        idx += 1
                                for bi in range(NB):
                                    r, n = bi // nbl, bi % nbl
                                    t0 = r + d * 128 * n
                                    for kc in range(KC):
                                        P.mm(ps_v[:, (bi % 4) * 128:(bi % 4 + 1) * 128], hT[:, kc, t0:t0 + d * 127 + 1:d], wv[wi][:, kc, :],
                                             start=(kc == 0 and bi % 4 == 0))
                                    if bi % 4 == 3:
                                        P.copy("act", Vt[:, bi - 3:bi + 1, :], ps_v[:, :].rearrange("p (k c) -> p k c", k=4))
                                for bi in range(NB):
                                    r, n = bi // nbl, bi % nbl
                                    c0 = r * L + 128 * n
                                    hp_ = n > 0
                                    ptb = pt[bi % 2]
                                    for h in range(2):
                                        pss = ps_s[bi % 2][h]
                                        rows = slice(64 * h, 64 * h + 64)
                                        if hp_:
                                            P.mm(pss[:, 0:128], KT[rows, c0 - 128:c0], QT[rows, c0:c0 + 128], start=True)
                                        P.mm(pss[:, 128:256], KT[rows, c0:c0 + 128], QT[rows, c0:c0 + 128], start=not hp_)
                                        lo = 0 if hp_ else 128
                                        P.act(ptb[:, h * 256 + lo:h * 256 + 256], pss[:, lo:256], AF.Exp, scale=0.125)
                                    if hp_:
                                        P.tt("pool", ptb[:, :], ptb[:, :], amask, ALU.mult)
                                    else:
                                        v3 = lambda a: a.rearrange("p (h c) -> p h c", h=2)[:, :, 128:256]
                                        P.tt("pool", v3(ptb[:, :]), v3(ptb[:, :]), v3(amask), ALU.mult)
                                    pso = ps_o[bi % 2]
                                    for h in range(2):
                                        rows = slice(64 * h, 64 * h + 64)
                                        kw = {} if h == 0 else {"tile_position": (0, 64)}
                                        first = True
                                        for (lh, ncol) in ((None, 0), (ones, 128)):
                                            for pc in ((0, 1) if hp_ else (1,)):
                                                vb = bi - 1 if pc == 0 else bi
                                                lhsT = Vt[:, vb, 64 * h:64 * h + 64] if lh is None else ones[:, 0:64]
                                                P.mm(pso[rows, ncol:ncol + 128], lhsT, ptb[:, h * 256 + pc * 128:h * 256 + pc * 128 + 128],
                                                     start=first, **kw)
                                                first = False
                                    t0 = r + d * 128 * n
                                    av = acc[:, :, t0:t0 + d * 127 + 1:d]
                                    pv3 = pso[:, 0:256].rearrange("p (c q) -> p c q", c=2)
                                    if g == 0:
                                        P.copy("dve", av, pv3)
                                    else:
                                        P.tt("dve", av, pv3, av, ALU.add)
                            yb = yat[hp % 2]
                            for j in range(NT):
                                sl = slice(512 * j, 512 * (j + 1))
                                P.add("dve", lambda e, o=rden[j % 2][:], i=acc[:, 1, sl]: e.reciprocal(o, i), [acc[:, 1, sl]], [rden[j % 2][:]])
                                P.tt("pool", yb[:, sl], acc[:, 0, sl], rden[j % 2][:], ALU.mult)
                            P.dma(yat_d[b, hp, :, :], yb[:, :], f"yat{hp % 2}")
                if "ssd" in phases:
                    with ES() as st:
                        wdt = sbt(st, f"s_wdt{b}", [128, KC, 32], BF16)
                        dtb = sbt(st, f"s_dtb{b}", [128, 32], F32)
                        alog = sbt(st, f"s_alog{b}", [128, 32], F32)
                        xdt = sbt(st, f"s_xdt{b}", [128, NB, 32], F32)
                        tmpa = sbt(st, f"s_tmpa{b}", [128, NB, 32], F32)
                        tmpb = sbt(st, f"s_tmpb{b}", [128, NB, 32], F32)
                        dt_all = sbt(st, f"s_dt{b}", [128, NB, 32], F32)
                        a32 = sbt(st, f"s_a32{b}", [128, NB, 32], F32)
                        a_hi = sbt(st, f"s_ahi{b}", [128, NB, 32], BF16)
                        a_lo = sbt(st, f"s_alo{b}", [128, NB, 32], BF16)
                        wg = [sbt(st, f"s_wg{b}_{i}", [128, KC, 768], BF16) for i in range(2)]
                        raw = [sbt(st, f"s_raw{b}_{i}", [128, 515], F32) for i in range(4)]
                        accb = [sbt(st, f"s_acc{b}_{i}", [128, 512], F32) for i in range(2)]
                        thb = [sbt(st, f"s_th{b}_{i}", [128, 512], F32) for i in range(2)]
                        xbcT = [sbt(st, f"s_xbcT{b}_{i}", [128, 4, 512], BF16) for i in range(2)]
                        zs = [sbt(st, f"s_zs{b}_{i}", [128, 2, 512], BF16) for i in range(2)]
                        xbtok = [sbt(st, f"s_xbtok{b}_{i}", [128, 4, 384], BF16) for i in range(2)]
                        Rh = [sbt(st, f"s_Rh{b}_{i}", [128, 512], BF16) for i in range(2)]
                        Rl = [sbt(st, f"s_Rl{b}_{i}", [128, 512], BF16) for i in range(2)]
                        Eb = [sbt(st, f"s_E{b}_{i}", [128, 512], BF16) for i in range(2)]
                        EAb = [sbt(st, f"s_EA{b}_{i}", [128, 512], BF16) for i in range(2)]
                        Mp = [sbt(st, f"s_Mp{b}_{i}", [128, 512], BF16) for i in range(2)]
                        Cs = [sbt(st, f"s_Cs{b}_{i}", [128, 512], BF16) for i in range(2)]
                        CBm = [sbt(st, f"s_CBm{b}_{i}", [128, 128], BF16) for i in range(2)]
                        Xd = [sbt(st, f"s_Xd{b}_{i}", [128, 256], BF16) for i in range(2)]
                        ddc = [sbt(st, f"s_dd{b}_{i}", [128, 12], F32) for i in range(2)]
                        Sst = sbt(st, f"s_S{b}", [128, 256], F32)
                        Sbf = [sbt(st, f"s_Sbf{b}_{i}", [128, 256], BF16) for i in range(2)]
                        ybuf = [sbt(st, f"s_yb{b}_{i}", [128, 2, 512], F32) for i in range(2)]
                        sqw = sbt(st, f"s_sq{b}", [128, 2, 512], BF16)
                        lnw = sbt(st, f"s_lnw{b}", [128, 512], F32)
                        yo = [sbt(st, f"s_yo{b}_{i}", [128, 2, 512], BF16) for i in range(2)]
                        ps_p = [pst(st, f"s_psp{b}_{i}") for i in range(2)]
                        ps_seg = pst(st, f"s_seg{b}")
                        ps_acs = pst(st, f"s_acs{b}")
                        ps_cb = pst(st, f"s_cb{b}")
                        ps_st = pst(st, f"s_st{b}")
                        ps_y = pst(st, f"s_y{b}")
                        ps_tr = pst(st, f"s_tr{b}", BF16)
                        load_w(wdt, w_in_d[:, C_DT:C_DT + 32], KC, 32, scale=g1)
                        P.dma(dtb[:], dtb_d[:, :].partition_broadcast(128), "dtb")
                        P.dma(alog[:], alog_d[:, :].partition_broadcast(128), "alog")
                        for half in range(NB // 16):
                            pp = ps_p[half % 2]
                            for bl in range(16):
                                blk = half * 16 + bl
                                for kc in range(KC):
                                    P.mm(pp[:, bl * 32:(bl + 1) * 32], hT[:, kc, blk * 128:(blk + 1) * 128], wdt[:, kc, :],
                                         start=(bl == 0 and kc == 0))
                            P.tt("dve", xdt[:, half * 16:(half + 1) * 16, :], pp[:, :].rearrange("p (k h) -> p k h", k=16),
                                 dtb[:, :].unsqueeze(1).broadcast_to([128, 16, 32]), ALU.add)
                        P.act(tmpa[:], xdt[:], AF.Abs)
                        P.act(tmpa[:], tmpa[:], AF.Exp, scale=-1.0)
                        P.act(alog[:], alog[:], AF.Exp)
                        P.act(tmpb[:], tmpa[:], AF.Ln, bias=1.0)
                        P.stt("dve", dt_all[:], xdt[:], 0.0, tmpb[:], ALU.max, ALU.add)
                        P.tt("dve", a32[:], dt_all[:], alog[:, :].unsqueeze(1).broadcast_to([128, NB, 32]), ALU.mult)
                        P.ts("dve", a32[:], a32[:], -1.0, None, ALU.mult)
                        P.copy("dve", a_hi[:], a32[:])
                        P.tt("dve", tmpa[:], a32[:], a_hi[:], ALU.subtract)
                        P.copy("dve", a_lo[:], tmpa[:])
                        cwh = lambda i, c: pvh[:, PV_CW + i * 32 + c:PV_CW + i * 32 + c + 1]
                        cbh = lambda c: pvh[:, PV_CB + c:PV_CB + c + 1]
                        cnt = 0
                        for g in range(8):
                            wgi = wg[g % 2]
                            load_w(wgi[:, :, 0:256], w_in_d[:, C_Z + g * 256:C_Z + (g + 1) * 256], KC, 256, scale=g1h)
                            load_w(wgi[:, :, 256:512], w_in_d[:, C_X + g * 256:C_X + (g + 1) * 256], KC, 256, scale=g1)
                            load_w(wgi[:, :, 512:640], w_in_d[:, C_B + g * 128:C_B + (g + 1) * 128], KC, 128, scale=g1)
                            load_w(wgi[:, :, 640:768], w_in_d[:, C_C + g * 128:C_C + (g + 1) * 128], KC, 128, scale=g1)
                            cchunk = (2 * g, 2 * g + 1, 16 + g, 24 + g)
                            for ct in range(4):
                                P.memset("pool", raw[ct][:, 0:3], 0.0)
                            for w in range(NT):
                                wi = w % 2
                                tsl = slice(512 * w, 512 * (w + 1))
                                for hc in range(2):
                                    pp = ps_p[cnt % 2]
                                    cnt += 1
                                    for kc in range(KC):
                                        P.mm(pp[:, 0:512], wgi[:, kc, hc * 128:(hc + 1) * 128], hT[:, kc, tsl], start=(kc == 0))
                                    tb = thb[hc]
                                    P.act(tb[:], pp[:, 0:512], AF.Tanh)
                                    P.stt("dve", zs[wi][:, hc, :], tb[:], 1.0, pp[:, 0:512], ALU.add, ALU.mult)
                                for ct in range(4):
                                    pp = ps_p[cnt % 2]
                                    cnt += 1
                                    for kc in range(KC):
                                        P.mm(pp[:, 0:512], wgi[:, kc, 256 + ct * 128:256 + (ct + 1) * 128], hT[:, kc, tsl], start=(kc == 0))
                                    cc = cchunk[ct]
                                    ab = accb[ct % 2]
                                    tb = thb[ct % 2]
                                    P.copy("act", raw[ct][:, 3:515], pp[:, 0:512])
                                    P.act(ab[:], pp[:, 0:512], AF.Identity, scale=cwh(3, cc), bias=cbh(cc))
                                    for i in range(3):
                                        P.stt("dve", ab[:], raw[ct][:, i:i + 512], cwh(i, cc), ab[:], ALU.mult, ALU.add)
                                    P.copy("pool", raw[ct][:, 0:3], raw[ct][:, 512:515])
                                    P.act(tb[:], ab[:], AF.Tanh)
                                    P.stt("dve", xbcT[wi][:, ct, :], tb[:], 1.0, ab[:], ALU.add, ALU.mult)
                                for bp in range(2):
                                    for k2 in range(2):
                                        blk = 2 * bp + k2
                                        for (ci, ct) in enumerate((0, 1, 2)):
                                            P.tr(ps_tr[:, k2 * 384 + ci * 128:k2 * 384 + (ci + 1) * 128],
                                                 xbcT[wi][:, ct, blk * 128:(blk + 1) * 128], ident)
                                    P.copy("act", xbtok[wi][:, 2 * bp:2 * bp + 2, :], ps_tr[:, 0:768].rearrange("p (k c) -> p k c", k=2))
                                for c4 in range(4):
                                    cidx = 4 * w + c4
                                    ci = cidx % 2
                                    tc = slice(128 * c4, 128 * (c4 + 1))
                                    BT = xbcT[wi][:, 2, tc]
                                    CT = xbcT[wi][:, 3, tc]
                                    v4 = lambda a: a.rearrange("p (k l) -> p k l", k=4)
                                    for (R_, asrc) in ((Rh[ci], a_hi), (Rl[ci], a_lo)):
                                        P.tt("pool", v4(R_[:, :]), maskU.unsqueeze(1).broadcast_to([128, 4, 128]),
                                             asrc[:, cidx, 4 * g:4 * g + 4].unsqueeze(2).broadcast_to([128, 4, 128]), ALU.mult)
                                    P.mm(ps_seg[:, 0:512], maskGT, Rh[ci][:, :], start=True)
                                    P.mm(ps_seg[:, 0:512], maskGT, Rl[ci][:, :], start=False)
                                    P.mm(ps_acs[:, 0:512], ones, Rh[ci][:, :], start=True)
                                    P.mm(ps_acs[:, 0:512], ones, Rl[ci][:, :], start=False)
                                    P.mm(ps_cb[:, 0:128], BT, CT, start=True)
                                    dd = ddc[ci]
                                    P.act(Eb[ci][:, :], ps_seg[:, 0:512], AF.Exp)
                                    P.act(dd[:, 0:4], ps_seg[:, 127:512:128], AF.Exp)
                                    P.act(EAb[ci][:, :], ps_acs[:, 0:512], AF.Exp)
                                    P.act(dd[:, 4:8], ps_acs[:, 127:512:128], AF.Exp)
                                    P.tt("dve", CBm[ci][:, :], ps_cb[:, 0:128], maskU, ALU.mult)
                                    for k in range(4):
                                        P.stt("dve", v4(Mp[ci][:, :])[:, k, :], v4(Eb[ci][:, :])[:, k, :],
                                              dt_all[:, cidx, 4 * g + k:4 * g + k + 1], CBm[ci][:, :], ALU.mult, ALU.mult)
                                    P.tt("pool", v4(Cs[ci][:, :]), CT.unsqueeze(1).broadcast_to([128, 4, 128]), v4(EAb[ci][:, :]), ALU.mult)
                                    P.tt("dve", dd[:, 8:12], dd[:, 0:4], dt_all[:, cidx, 4 * g:4 * g + 4], ALU.mult)
                                    P.tt("pool", Xd[ci][:, :].rearrange("p (k q) -> p k q", k=4),
                                         xbtok[wi][:, c4, 0:256].rearrange("p (k q) -> p k q", k=4),
                                         dd[:, 8:12].unsqueeze(2).broadcast_to([128, 4, 64]), ALU.mult)
                                    sb_old = Sbf[cidx % 2]
                                    sb_new = Sbf[(cidx + 1) % 2]
                                    firstq = [True, True]
                                    for hc in range(2):
                                        for hh in range(2):
                                            k = 2 * hc + hh
                                            kw = {} if hh == 0 else {"tile_position": (0, 64)}
                                            o = ps_y[64 * hh:64 * hh + 64, hc * 128:(hc + 1) * 128]
                                            P.mm(o, xbtok[wi][:, c4, k * 64:(k + 1) * 64], v4(Mp[ci][:, :])[:, k, :], start=firstq[hh], **kw)
                                            firstq[hh] = False
                                            if cidx > 0:
                                                P.mm(o, sb_old[:, k * 64:(k + 1) * 64], v4(Cs[ci][:, :])[:, k, :], start=False, **kw)
                                    P.mm(ps_st[:, 0:256], xbtok[wi][:, c4, 256:384], Xd[ci][:, :], start=True)
                                    if cidx == 0:
                                        P.copy("dve", Sst[:, :], ps_st[:, 0:256])
                                    else:
                                        s3 = Sst[:, :].rearrange("p (k q) -> p k q", k=4)
                                        P.tt("dve", s3, s3, dd[:, 4:8].unsqueeze(2).broadcast_to([128, 4, 64]), ALU.mult)
                                        P.tt("dve", Sst[:, :], Sst[:, :], ps_st[:, 0:256], ALU.add)
                                    P.copy("act", sb_new[:, :], Sst[:, :])
                                    for hc in range(2):
                                        P.stt("dve", ybuf[wi][:, hc, tc], xbcT[wi][:, hc, tc], pv[:, PV_D + 2 * g + hc:PV_D + 2 * g + hc + 1],
                                              ps_y[:, hc * 128:(hc + 1) * 128], ALU.mult, ALU.add)
                                P.tt("dve", ybuf[wi][:], ybuf[wi][:], zs[wi][:], ALU.mult)
                                P.act(sqw[:], ybuf[wi][:], AF.Square)
                                pp = ps_p[cnt % 2]
                                cnt += 1
                                P.mm(pp[:, 0:512], ones, sqw[:, 0, :], start=True)
                                P.mm(pp[:, 0:512], ones, sqw[:, 1, :], start=False)
                                P.act(lnw[:], pp[:, 0:512], AF.Ln, scale=1.0 / 256, bias=EPS)
                                P.act(lnw[:], lnw[:], AF.Exp, scale=-0.5)
                                P.tt("dve", yo[wi][:], ybuf[wi][:], lnw[:, :].unsqueeze(1).broadcast_to([128, 2, 512]), ALU.mult)
                                P.dma(ysd_d[b, 2 * g:2 * g + 2, :, tsl].rearrange("c p t -> p c t"), yo[wi][:], f"yo{wi}")
            P.barrier()

        tiles = [(b, j) for b in range(NS) for j in range(NT)]
        if "merge" in phases:
            with ES() as st:
                Wssd = sbt(st, "m_wssd", [128, 16, D], BF16)
                Watt = sbt(st, "m_watt", [128, 4, D], BF16)
                Wgs = sbt(st, "m_wgs", [128, KC, D], BF16)
                Wga = sbt(st, "m_wga", [128, KC, D], BF16)
                Wout = sbt(st, "m_wout", [128, KC, D], BF16)
                ys = [sbt(st, f"m_ys{i}", [128, 16, 512], BF16) for i in range(2)]
                ya = [sbt(st, f"m_ya{i}", [128, 4, 512], BF16) for i in range(2)]
                hTt = [sbt(st, f"m_ht{i}", [128, KC, 512], BF16) for i in range(2)]
                xt = [sbt(st, f"m_xt{i}", [128, 4, D], F32) for i in range(2)]
                th = [sbt(st, f"m_th{i}", [128, 512], F32) for i in range(2)]
                m1 = [sbt(st, f"m_m1{i}", [128, 512], F32) for i in range(3)]
                mT = sbt(st, "m_mT", [128, KC, 512], BF16)
                psm = [pst(st, f"m_ps{i}") for i in range(8)]
                pcm = [0]

                def m_loads(t):
                    b, j = tiles[t]
                    i2 = t % 2
                    tsl = slice(512 * j, 512 * (j + 1))
                    P.dma(hTt[i2][:], hTs_d[b, :, :, tsl].rearrange("c p t -> p c t"), f"mht{i2}")
                    P.dma(ys[i2][:], ysd_d[b, :, :, tsl].rearrange("c p t -> p c t"), f"mys{i2}")
                    P.dma(ya[i2][:], yat_d[b, :, :, tsl].rearrange("c p t -> p c t"), f"mya{i2}")
                    P.dma(xt[i2][:], x_d[b, tsl, :].rearrange("(k p) d -> p k d", p=128), f"mxt{i2}")

                def m_compute(t):
                    b, j = tiles[t]
                    i2 = t % 2
                    tsl = slice(512 * j, 512 * (j + 1))
                    for oc in range(KC):
                        osl = slice(oc * 128, (oc + 1) * 128)
                        pc = pcm[0]
                        p1, pg1, p2, pg2 = psm[pc % 8], psm[(pc + 1) % 8], psm[(pc + 2) % 8], psm[(pc + 3) % 8]
                        pcm[0] += 4
                        e2 = 0
                        for kc in range(KC):
                            P.mm(pg1[:, 0:512], Wgs[:, kc, osl], hTt[i2][:, kc, :], start=(kc == 0))
                        for kc in range(16):
                            P.mm(p1[:, 0:512], Wssd[:, kc, osl], ys[i2][:, kc, :], start=(kc == 0))
                        for kc in range(KC):
                            P.mm(pg2[:, 0:512], Wga[:, kc, osl], hTt[i2][:, kc, :], start=(kc == 0))
                        for kc in range(4):
                            P.mm(p2[:, 0:512], Watt[:, kc, osl], ya[i2][:, kc, :], start=(kc == 0))
                        P.act(th[e2][:], pg1[:, 0:512], AF.Tanh)
                        P.stt("dve", m1[e2][:], th[e2][:], 1.0, p1[:, 0:512], ALU.add, ALU.mult)
                        P.act(th[e2 + 1][:], pg2[:, 0:512], AF.Tanh)
                        P.stt("dve", m1[e2 + 1][:], th[e2 + 1][:], 1.0, p2[:, 0:512], ALU.add, ALU.mult)
                        P.tt("pool", mT[:, oc, :], m1[e2][:], m1[e2 + 1][:], ALU.add)
                    for blk in range(4):
                        for half in range(2):
                            pp = psm[pcm[0] % 8]
                            pcm[0] += 1
                            hs = slice(half * 512, (half + 1) * 512)
                            for kc in range(KC):
                                P.mm(pp[:, 0:512], mT[:, kc, blk * 128:(blk + 1) * 128], Wout[:, kc, hs], start=(kc == 0))
                            P.tt("dve", xt[i2][:, blk, hs], xt[i2][:, blk, hs], pp[:, 0:512], ALU.add)
                    P.dma(out_d[b, tsl, :].rearrange("(k p) d -> p k d", p=128), xt[i2][:], f"mo{i2}")

                m_loads(0)
                load_w(Wgs, w_in_d[:, C_GS:C_GS + D], KC, D, scale=g1h)
                load_w(Wssd, w_ssd_d, 16, D, scale=pv[:, PV_GS:PV_GS + 16])
                load_w(Wga, w_in_d[:, C_GA:C_GA + D], KC, D, scale=g1h)
                load_w(Watt, w_att_d, 4, D)
                load_w(Wout, w_out_d, KC, D, scale=0.5)
                for t in range(len(tiles)):
                    if t + 1 < len(tiles):
                        m_loads(t + 1)
                    m_compute(t)
            P.barrier()

        if "ffn" in phases:
            with ES() as st:
                Wup = sbt(st, "f_wup", [128, KC, 2 * DFF], BF16)
                Wdn = sbt(st, "f_wdn", [128, NFC, D], BF16)
                xb = [sbt(st, f"f_xb{i}", [128, D], F32) for i in range(2)]
                h16 = sbt(st, "f_h16", [128, 1, D], BF16)
                ssq = [sbt(st, f"f_ss{i}", [128, 1], F32) for i in range(2)]
                lnv = [sbt(st, f"f_ln{i}", [128, 1], F32) for i in range(2)]
                rstd = [sbt(st, f"f_rs{i}", [128, 1], F32) for i in range(2)]
                h2T = [sbt(st, f"f_h2T{i}", [128, KC, 512], BF16) for i in range(2)]
                aT = sbt(st, "f_aT", [128, NFC, 512], BF16)
                tail = sbt(st, "f_tail", [128, 2 * NFC, 2], F32)
                rawf = [sbt(st, f"f_raw{i}", [128, 514], F32) for i in range(2)]
                accf = [sbt(st, f"f_acc{i}", [128, 512], F32) for i in range(4)]
                ostg = [sbt(st, f"f_os{i}", [128, 512], F32) for i in range(2)]
                pbf = [pst(st, f"f_pb{i}", BF16) for i in range(2)]
                psf = [pst(st, f"f_ps{i}") for i in range(6)]
                pcf = [0]
                blkc = [0]
                def fwc(i, ci):
                    src = pvh if ci < NFC else pv
                    return src[:, PV_FW + i * 44 + ci:PV_FW + i * 44 + ci + 1]

                def fbc(ci):
                    src = pvh if ci < NFC else pv
                    return src[:, PV_FB + ci:PV_FB + ci + 1]

                def f_norm(t):
                    b, j = tiles[t]
                    hb = h2T[t % 2]
                    for k in range(4):
                        q = blkc[0] % 2
                        blkc[0] += 1
                        r0 = 512 * j + 128 * k
                        P.dma(xb[q][:], out_d[b, r0:r0 + 128, :], f"fxb{q}")
                        P.memset("pool", ssq[q][:], 0.0)
                        P.act(h16[:, 0, :], xb[q][:], AF.Square, accum_out=ssq[q][:, 0:1])
                        P.act(lnv[q][:], ssq[q][:], AF.Ln, scale=1.0 / D, bias=EPS)
                        P.act(rstd[q][:], lnv[q][:], AF.Exp, scale=-0.5)
                        P.scale("dve" if k % 2 == 0 else "act", h16[:, 0, :], xb[q][:], rstd[q][:, 0:1])
                        bank = pbf[q]
                        for kc in range(KC):
                            P.tr(bank[:, kc * 128:(kc + 1) * 128], h16[:, 0, kc * 128:(kc + 1) * 128], ident)
                        P.copy("act" if k % 2 == 0 else "dve", hb[:, :, k * 128:(k + 1) * 128], bank[:, :].rearrange("p (c t) -> p c t", c=KC))

                def f_up(t):
                    b, j = tiles[t]
                    hb = h2T[t % 2]
                    if j == 0:
                        P.memset("pool", tail[:], 0.0)
                    for c in range(NFC):
                        res = []
                        for half in range(2):
                            ci = half * NFC + c
                            pp = psf[pcf[0] % 6]
                            pcf[0] += 1
                            col = half * DFF + c * 128
                            for kc in range(KC):
                                P.mm(pp[:, 0:512], Wup[:, kc, col:col + 128], hb[:, kc, :], start=(kc == 0))
                            rw = rawf[half]
                            ab = accf[(2 * c + half) % 4]
                            P.copy("pool", rw[:, 0:2], tail[:, ci, :])
                            P.copy("act", rw[:, 2:514], pp[:, 0:512])
                            P.act(ab[:], pp[:, 0:512], AF.Identity, scale=fwc(2, ci), bias=fbc(ci))
                            P.copy("pool", tail[:, ci, :], rw[:, 512:514])
                            for i in range(2):
                                P.stt("dve", ab[:], rw[:, i:i + 512], fwc(i, ci), ab[:], ALU.mult, ALU.add)
                            res.append(ab)
                        tf = rawf[0][:, 2:514]
                        P.act(tf, res[0][:], AF.Tanh)
                        P.stt("dve", res[0][:], tf, 1.0, res[0][:], ALU.add, ALU.mult)
                        P.tt("pool", aT[:, c, :], res[0][:], res[1][:], ALU.mult)

                def f_down(t):
                    b, j = tiles[t]
                    for blk in range(4):
                        r0 = 512 * j + 128 * blk
                        for half in range(2):
                            pp = psf[pcf[0] % 6]
                            q = pcf[0] % 2
                            pcf[0] += 1
                            hs = slice(half * 512, (half + 1) * 512)
                            for c in range(NFC):
                                P.mm(pp[:, 0:512], aT[:, c, blk * 128:(blk + 1) * 128], Wdn[:, c, hs], start=(c == 0))
                            P.copy("act", ostg[q][:], pp[:, 0:512])
                            P.dma(out_d[b, r0:r0 + 128, hs], ostg[q][:], f"fo{q}", eng="pool", accum_op=ALU.add)

                f_norm(0)
                load_w(Wup, w_up_d, KC, 2 * DFF, scale=pv[:, PV_G2:PV_G2 + 8])
                load_w(Wdn, w_dn_d, NFC, D)
                nt_ = len(tiles)
                f_up(0)
                for t in range(nt_):
                    if t + 1 < nt_:
                        f_norm(t + 1)
                    f_down(t)
                    if t + 1 < nt_:
                        f_up(t + 1)
        P.barrier(final=True)
        P.emit()
    return nc, P


_CACHE = {}


def _core_inputs(inp, c, NS):
    x = np.ascontiguousarray(np.asarray(inp["x"], dtype=np.float32)[c * NS:(c + 1) * NS])
    return x


def kernel(**inputs):
    NCORES = 8
    NS = 2
    if "nc" not in _CACHE:
        _CACHE["nc"] = build(NS=NS)[0]
    nc = _CACHE["nc"]
    f = lambda k: np.ascontiguousarray(np.asarray(inputs[k], dtype=np.float32)[0])
    shared = {
        "w_in": f("w_in"), "w_ssd_proj": f("w_ssd_proj"), "w_attn_proj": f("w_attn_proj"), "w_out": f("w_out"),
        "w_up": f("w_up"), "w_down": f("w_down"), "pvec": pack_pvec(inputs), "cmask": make_cmask(),
        "dt_bias": np.asarray(inputs["dt_bias"], np.float32).reshape(1, 32),
        "a_log": np.asarray(inputs["a_log"], np.float32).reshape(1, 32),
    }
    in_maps = []
    for c in range(NCORES):
        m = dict(shared)
        m["x"] = _core_inputs(inputs, c, NS)
        in_maps.append(m)
    res = run_bass_kernel_spmd(nc, in_maps, core_ids=list(range(NCORES)))
    out = np.concatenate([np.asarray(r["out"]) for r in res.results], axis=0)
    return out.astype(np.float32)
```

```python
import contextlib
import numpy as np
import ml_dtypes
import concourse.bass as bass
import concourse.mybir as mybir
from concourse.bass_utils import run_bass_kernel_spmd

F32 = mybir.dt.float32
BF16 = mybir.dt.bfloat16
AF = mybir.ActivationFunctionType
ALU = mybir.AluOpType

ENGS = ("pe", "act", "dve", "pool", "sp")
SMALL_N = 256


def region(ap):
    t = ap.tensor
    dims = ap.ap
    off = int(ap.offset)
    name = t.name
    cls = type(t).__name__
    if cls.startswith("DRam"):
        lo = off
        hi = off + sum((c - 1) * abs(s) for s, c in dims) + 1
        return (name, 0, 1, lo, hi)
    if cls.startswith("PSum"):
        return (name, 0, 128, 0, 1 << 30)
    ps, npart = dims[0]
    p0 = off // ps
    f0 = off % ps
    f1 = f0 + sum((c - 1) * abs(s) for s, c in dims[1:]) + 1
    return (name, p0, p0 + npart, f0, f1)


class Op:
    __slots__ = ("eng", "fn", "idx", "deps", "signaled", "dma_slot", "dma_cnt", "nfree", "sig_no")

    def __init__(self, eng, fn):
        self.eng = eng
        self.fn = fn
        self.deps = {}
        self.signaled = False
        self.dma_slot = None
        self.dma_cnt = 0
        self.nfree = 1 << 30
        self.sig_no = 0


class Prog:
    def __init__(self, nc):
        self.nc = nc
        self.ops = []
        self.recs = {}
        self.slot_last = {}
        self.slot_cnt = {}
        self.barrier_idx = {}

    def _track(self, op, reads, writes):
        deps = op.deps
        ekey = op.eng if op.dma_slot is None else ("dma", op.dma_slot)

        def add_dep(r):
            k = r[5]
            if k == ekey and op.dma_slot is None:
                if k == "pe":
                    return
            if deps.get(k, -1) < r[6]:
                deps[k] = r[6]

        psum_reads = [ap for ap in reads if type(ap.tensor).__name__.startswith("PSum")]
        if psum_reads:
            reads = [ap for ap in reads if not type(ap.tensor).__name__.startswith("PSum")]
            writes = list(writes) + psum_reads
        for ap in reads:
            name, p0, p1, f0, f1 = region(ap)
            lst = self.recs.setdefault(name, [])
            found = None
            for r in lst:
                if r[4] == "w":
                    if r[0] < p1 and p0 < r[1] and r[2] < f1 and f0 < r[3]:
                        add_dep(r)
                elif r[5] == ekey and r[0] == p0 and r[1] == p1 and r[2] == f0 and r[3] == f1:
                    found = r
            if found is not None:
                found[6] = op.idx
            else:
                lst.append([p0, p1, f0, f1, "r", ekey, op.idx, op.nfree])
        for ap in writes:
            name, p0, p1, f0, f1 = region(ap)
            lst = self.recs.setdefault(name, [])
            keep = []
            for r in lst:
                if r[0] < p1 and p0 < r[1] and r[2] < f1 and f0 < r[3]:
                    if r[6] != op.idx:
                        add_dep(r)
                    if r[0] >= p0 and r[1] <= p1 and r[2] >= f0 and r[3] <= f1 and r[6] != op.idx:
                        continue
                keep.append(r)
            keep.append([p0, p1, f0, f1, "w", ekey, op.idx, op.nfree])
            self.recs[name] = keep

    def add(self, eng, fn, reads=(), writes=(), nfree=None):
        op = Op(eng, fn)
        op.idx = len(self.ops)
        if nfree is None:
            nfree = 1 << 30
            for ap in writes:
                d = ap.ap
                n = 1
                for s, c in d[1:]:
                    n *= c
                nfree = min(nfree, n)
        op.nfree = nfree
        self.ops.append(op)
        self._track(op, reads, writes)
        return op

    def dma(self, out, in_, slot, eng="sp", **kw):
        def fn(e, out=out, in_=in_, kw=kw):
            return e.dma_start(out=out, in_=in_, **kw)

        op = Op(eng, fn)
        op.idx = len(self.ops)
        op.dma_slot = slot
        self.slot_cnt[slot] = self.slot_cnt.get(slot, 0) + 16
        op.dma_cnt = self.slot_cnt[slot]
        self.ops.append(op)
        prev = self.slot_last.get(slot)
        if prev is not None:
            op.deps[("dma", slot)] = prev
        self.slot_last[slot] = op.idx
        self._track(op, [in_], [out])
        return op

    def mm(self, out, lhsT, rhs, start=True, stop=True, **kw):
        if not start:
            kw.setdefault("skip_group_check", True)
        return self.add("pe", lambda e: e.matmul(out, lhsT, rhs, start=start, stop=stop, **kw),
                        [lhsT, rhs], [out])

    def tr(self, out, in_, ident):
        return self.add("pe", lambda e: e.transpose(out, in_, ident), [in_, ident], [out])

    def act(self, out, in_, func, bias=None, scale=None, accum_out=None, eng="act"):
        reads = [in_]
        kw = {}
        if bias is not None:
            kw["bias"] = bias
            if not isinstance(bias, (int, float)):
                reads.append(bias)
        if scale is not None:
            kw["scale"] = scale
            if not isinstance(scale, (int, float)):
                reads.append(scale)
        writes = [out]
        if accum_out is not None:
            kw["accum_out"] = accum_out
            writes.append(accum_out)
        return self.add(eng, lambda e: e.activation(out, in_, func, **kw), reads, writes)

    def tt(self, eng, out, in0, in1, op):
        return self.add(eng, lambda e: e.tensor_tensor(out, in0, in1, op), [in0, in1], [out])

    def ts(self, eng, out, in0, s1, s2, op0, op1=None, accum_out=None):
        reads = [in0]
        if not isinstance(s1, (int, float)):
            reads.append(s1)
        if s2 is not None and not isinstance(s2, (int, float)):
            reads.append(s2)
        kw = {}
        writes = [out]
        if accum_out is not None:
            kw["accum_out"] = accum_out
            writes.append(accum_out)
        if op1 is None:
            return self.add(eng, lambda e: e.tensor_scalar(out, in0, s1, None, op0, **kw), reads, writes)
        return self.add(eng, lambda e: e.tensor_scalar(out, in0, s1, s2, op0, op1, **kw), reads, writes)

    def stt(self, eng, out, in0, scalar, in1, op0, op1):
        reads = [in0, in1]
        if not isinstance(scalar, (int, float)):
            reads.append(scalar)
        return self.add(eng, lambda e: e.scalar_tensor_tensor(out, in0, scalar, in1, op0, op1), reads, [out])

    def scale(self, eng, out, in_, sc):
        if eng == "act":
            return self.act(out, in_, AF.Copy, scale=sc)
        return self.ts(eng, out, in_, sc, None, ALU.mult)

    def copy(self, eng, out, in_):
        if eng == "act":
            return self.add(eng, lambda e: e.copy(out, in_), [in_], [out])
        return self.add(eng, lambda e: e.tensor_copy(out, in_), [in_], [out])

    def memset(self, eng, ap, val):
        return self.add(eng, lambda e: e.memset(ap, val), [], [ap])

    def barrier(self, final=False):
        op = Op("sp", None)
        op.idx = len(self.ops)
        self.ops.append(op)
        for slot, i in self.slot_last.items():
            op.deps[("dma", slot)] = i
        if not final:
            m = Op("marker", None)
            m.idx = len(self.ops)
            self.ops.append(m)
        self.recs = {}

    def emit(self):
        nc = self.nc
        ops = self.ops
        for op in ops:
            for k, i in op.deps.items():
                ops[i].signaled = True
        cnt = {e: 0 for e in ENGS}
        for op in ops:
            if op.eng == "marker":
                continue
            if op.dma_slot is None and op.signaled:
                cnt[op.eng] += 1
                op.sig_no = cnt[op.eng]
        slots = sorted(self.slot_cnt.keys())
        import contextlib

        with contextlib.ExitStack() as st:
            esem = {e: st.enter_context(nc.semaphore("s_" + e)) for e in ENGS}
            ssem = {s: st.enter_context(nc.semaphore("d_" + s)) for s in slots}
            segs = [[]]
            for op in ops:
                if op.eng == "marker":
                    segs.append([])
                else:
                    segs[-1].append(op)
            stats = {e: [0, 0] for e in ENGS}
            waited_all = {e: {} for e in ENGS}

            def run(eng_name, e, seg):
                waited = waited_all[eng_name]
                for op in seg:
                    if op.eng != eng_name:
                        continue
                    for k, i in op.deps.items():
                        p = ops[i]
                        if isinstance(k, tuple):
                            sem, val = ssem[k[1]], p.dma_cnt
                        else:
                            sem, val = esem[k], p.sig_no
                        if waited.get(k, 0) >= val:
                            continue
                        waited[k] = val
                        e.wait_ge(sem, val)
                        stats[eng_name][1] += 1
                    if op.fn is None:
                        continue
                    ins = op.fn(e)
                    stats[eng_name][0] += 1
                    if op.dma_slot is not None:
                        ins.then_inc(ssem[op.dma_slot], 16)
                    elif op.signaled:
                        ins.then_inc(esem[eng_name], 1)

            for seg in segs:
                with nc.Block() as block:
                    @block.tensor
                    def _(e, seg=seg):
                        run("pe", e, seg)

                    @block.scalar
                    def _(e, seg=seg):
                        run("act", e, seg)

                    @block.vector
                    def _(e, seg=seg):
                        run("dve", e, seg)

                    @block.gpsimd
                    def _(e, seg=seg):
                        run("pool", e, seg)

                    @block.sync
                    def _(e, seg=seg):
                        run("sp", e, seg)

            self.stats = stats
            self.sig_counts = cnt


D = 1024
KC = 8
S_FULL = 4096
C_Z, C_X, C_B, C_C, C_DT, C_Q, C_K, C_V, C_GS, C_GA = 0, 2048, 4096, 5120, 6144, 6176, 7712, 9248, 10784, 11808
DIL = (1, 4, 16)
EPS = 1e-6
DFF = 2816
NFC = 22
PV_G1, PV_G2, PV_GS, PV_CW, PV_CB, PV_D, PV_QG, PV_KG, PV_FW, PV_FB, NPV = 0, 8, 16, 32, 160, 192, 208, 209, 210, 342, 386
CM_ID, CM_U, CM_L, CM_GT, CM_ONE, CM_AM, CM_BD, NCM = 0, 128, 256, 384, 512, 640, 1152, 1280

_bf = ml_dtypes.bfloat16


def _fm(v):
    return np.ascontiguousarray(np.asarray(v).reshape(-1, 128).T)


def pack_pvec(inp):
    pv = np.zeros((128, NPV), np.float32)
    pv[:, PV_G1:PV_G1 + 8] = _fm(inp["norm1_g"][0])
    pv[:, PV_G2:PV_G2 + 8] = _fm(inp["norm2_g"][0])
    pv[:, PV_GS:PV_GS + 16] = _fm(inp["ssd_norm_g"][0])
    for i in range(4):
        pv[:, PV_CW + i * 32:PV_CW + (i + 1) * 32] = _fm(inp["ssd_conv_w"][0, i])
    pv[:, PV_CB:PV_CB + 32] = _fm(inp["ssd_conv_b"][0])
    pv[:, PV_D:PV_D + 16] = _fm(np.repeat(inp["d_skip"][0], 64))
    pv[:, PV_QG] = np.tile(inp["q_norm_g"][0], 2)
    pv[:, PV_KG] = np.tile(inp["k_norm_g"][0], 2)
    for i in range(3):
        pv[:, PV_FW + i * 44:PV_FW + (i + 1) * 44] = _fm(inp["ffn_conv_w"][0, i])
    pv[:, PV_FB:PV_FB + 44] = _fm(inp["ffn_conv_b"][0])
    return pv


def make_cmask():
    cm = np.zeros((128, NCM), np.float32)
    i = np.arange(128)
    U = (i[:, None] <= i[None, :]).astype(np.float32)
    L = (i[:, None] >= i[None, :]).astype(np.float32)
    GT = (i[:, None] > i[None, :]).astype(np.float32)
    cm[:, CM_ID:CM_ID + 128] = np.eye(128)
    cm[:, CM_U:CM_U + 128] = U
    cm[:, CM_L:CM_L + 128] = L
    cm[:, CM_GT:CM_GT + 128] = GT
    cm[:, CM_ONE:CM_ONE + 128] = 1.0
    cm[:, CM_AM:CM_AM + 512] = np.concatenate([L, U, L, U], 1)
    bd = np.zeros((128, 128), np.float32)
    bd[:64, :64] = 1.0
    bd[64:, 64:] = 1.0
    cm[:, CM_BD:CM_BD + 128] = bd
    return cm.astype(_bf)


def build(NS=2, S=S_FULL, debug=False, phases=("norm", "attn", "ssd", "merge", "ffn")):
    nc = bass.Bass("TRN2", target_bir_lowering=False)
    NT = S // 512
    NB = S // 128

    def din(name, shape, dt=F32):
        return nc.dram_tensor(name, shape, dt, kind="ExternalInput").ap()

    x_d = din("x", [NS, S, D])
    w_in_d = din("w_in", [D, 12832])
    w_ssd_d = din("w_ssd_proj", [2048, D])
    w_att_d = din("w_attn_proj", [512, D])
    w_out_d = din("w_out", [D, D])
    w_up_d = din("w_up", [D, 2 * DFF])
    w_dn_d = din("w_down", [DFF, D])
    pv_d = din("pvec", [128, NPV])
    cm_d = din("cmask", [128, NCM], BF16)
    dtb_d = din("dt_bias", [1, 32])
    alog_d = din("a_log", [1, 32])
    skind = "ExternalOutput" if debug else "Internal"
    hTs_d = nc.dram_tensor("hTs", [NS, KC, 128, S], BF16, kind=skind).ap()
    yat_d = nc.dram_tensor("yattn", [NS, 4, 128, S], BF16, kind=skind).ap()
    ysd_d = nc.dram_tensor("yssd", [NS, 16, 128, S], BF16, kind=skind).ap()
    out_d = nc.dram_tensor("out", [NS, S, D], F32, kind="ExternalOutput").ap()

    P = Prog(nc)
    ES = contextlib.ExitStack

    with ES() as top:
        def sbt(st, name, shape, dt):
            return st.enter_context(nc.sbuf_tensor(name, shape, dt))

        def pst(st, name, dt=F32):
            return st.enter_context(nc.psum_tensor(name, [128, 512 if dt == F32 else 1024], dt))

        pv = sbt(top, "pv", [128, NPV], F32)
        pvh = sbt(top, "pvh", [128, NPV], F32)
        cm = sbt(top, "cm", [128, NCM], BF16)
        P.dma(pv[:], pv_d[:, :], "pv")
        P.dma(cm[:], cm_d[:, :], "cm")
        P.ts("dve", pvh[:], pv[:], 0.5, None, ALU.mult)
        ident = cm[:, CM_ID:CM_ID + 128]
        maskU = cm[:, CM_U:CM_U + 128]
        maskGT = cm[:, CM_GT:CM_GT + 128]
        ones = cm[:, CM_ONE:CM_ONE + 128]
        amask = cm[:, CM_AM:CM_AM + 512]
        bdones = cm[:, CM_BD:CM_BD + 128]
        stg = [sbt(top, f"stg{i}", [128, 512], F32) for i in range(2)]
        stg_i = [0, 0]

        def load_w(dst, src, kn, ncols, scale=None, engs=("act", "dve")):
            SW = 512
            cw = min(ncols, SW)
            kpp = max(1, SW // cw)
            for c0 in range(0, ncols, cw):
                cn = min(cw, ncols - c0)
                for k0 in range(0, kn, kpp):
                    kk = min(kpp, kn - k0)
                    i = stg_i[0] % 2
                    stg_i[0] += 1
                    sv = stg[i][:, 0:kk * cn].rearrange("p (k n) -> p k n", k=kk)
                    P.dma(sv, src[k0 * 128:(k0 + kk) * 128, c0:c0 + cn].rearrange("(k p) n -> p k n", p=128), f"stg{i}")
                    for k in range(kk):
                        d_ = dst[:, k0 + k, c0:c0 + cn]
                        eng = engs[stg_i[1] % len(engs)]
                        stg_i[1] += 1
                        if scale is None:
                            P.copy(eng, d_, sv[:, k, :])
                        elif isinstance(scale, float):
                            P.scale(eng, d_, sv[:, k, :], scale)
                        else:
                            P.scale(eng, d_, sv[:, k, :], scale[:, k0 + k:k0 + k + 1])

        g1 = pv[:, PV_G1:PV_G1 + 8]
        g1h = pvh[:, PV_G1:PV_G1 + 8]

        for b in range(NS):
            with ES() as seq:
                hT = sbt(seq, f"hT{b}", [128, KC, S], BF16)
                if "norm" in phases:
                    with ES() as st:
                        xt = [sbt(st, f"n_xt{b}_{i}", [128, 4, D], F32) for i in range(2)]
                        junk = sbt(st, f"n_junk{b}", [128, D], BF16)
                        h16 = [sbt(st, f"n_h16{b}_{i}", [128, 4, D], BF16) for i in range(2)]
                        ssq = [sbt(st, f"n_ss{b}_{i}", [128, 4], F32) for i in range(2)]
                        lnv = [sbt(st, f"n_ln{b}_{i}", [128, 4], F32) for i in range(2)]
                        rstd = [sbt(st, f"n_rs{b}_{i}", [128, 4], F32) for i in range(2)]
                        pb = [pst(st, f"n_pb{b}_{i}", BF16) for i in range(4)]
                        for j in range(NT):
                            i2 = j % 2
                            P.dma(xt[i2][:], x_d[b, 512 * j:512 * (j + 1), :].rearrange("(k p) d -> p k d", p=128), f"xt{i2}")
                            P.memset("pool", ssq[i2][:], 0.0)
                            for k in range(4):
                                P.act(junk[:], xt[i2][:, k, :], AF.Square, accum_out=ssq[i2][:, k:k + 1])
                            P.act(lnv[i2][:], ssq[i2][:], AF.Ln, scale=1.0 / D, bias=EPS)
                            P.act(rstd[i2][:], lnv[i2][:], AF.Exp, scale=-0.5)
                            for k in range(4):
                                P.scale("dve" if k % 2 == 0 else "act", h16[i2][:, k, :], xt[i2][:, k, :], rstd[i2][:, k:k + 1])
                            for pr in range(4):
                                bank = pb[pr]
                                for kk in range(2):
                                    kc = 2 * pr + kk
                                    for k in range(4):
                                        P.tr(bank[:, kk * 512 + k * 128:kk * 512 + (k + 1) * 128], h16[i2][:, k, kc * 128:(kc + 1) * 128], ident)
                                P.copy("act" if pr % 2 == 0 else "dve", hT[:, 2 * pr:2 * pr + 2, 512 * j:512 * (j + 1)],
                                       bank[:, :].rearrange("p (k t) -> p k t", k=2))
                            P.dma(hTs_d[b, :, :, 512 * j:512 * (j + 1)].rearrange("k p t -> p k t"), hT[:, :, 512 * j:512 * (j + 1)], f"hs{i2}")
                P.barrier()
                if "attn" in phases:
                    with ES() as st:
                        wq = [sbt(st, f"a_wq{b}_{i}", [128, KC, 128], BF16) for i in range(2)]
                        wk = [sbt(st, f"a_wk{b}_{i}", [128, KC, 128], BF16) for i in range(2)]
                        wv = [sbt(st, f"a_wv{b}_{i}", [128, KC, 128], BF16) for i in range(2)]
                        QT = sbt(st, f"a_QT{b}", [128, S], BF16)
                        KT = sbt(st, f"a_KT{b}", [128, S], BF16)
                        Vt = sbt(st, f"a_Vt{b}", [128, NB, 128], BF16)
                        lnb = sbt(st, f"a_ln{b}", [128, 2 * NT, 512], F32)
                        sq = [sbt(st, f"a_sq{b}_{i}", [128, 512], BF16) for i in range(2)]
                        pt = [sbt(st, f"a_pt{b}_{i}", [128, 512], BF16) for i in range(2)]
                        acc = sbt(st, f"a_acc{b}", [128, 2, S], F32)
                        rden = [sbt(st, f"a_rd{b}_{i}", [128, 512], F32) for i in range(2)]
                        yat = [sbt(st, f"a_yat{b}_{i}", [128, S], BF16) for i in range(2)]
                        bk = [pst(st, f"a_bk{b}_{i}") for i in range(8)]
                        ps_p = [bk[0], bk[1]]
                        ps_n = bk[7]
                        ps_v = bk[6]
                        ps_s = [(bk[0], bk[1]), (bk[2], bk[3])]
                        ps_o = [bk[4], bk[5]]
                        it = 0
                        for hp in range(4):
                            for g in range(3):
                                d = DIL[g]
                                L = S // d
                                nbl = L // 128
                                wi = it % 2
                                it += 1
                                for (wt, c0) in ((wq[wi], C_Q), (wk[wi], C_K), (wv[wi], C_V)):
                                    cc = c0 + g * 512 + hp * 128
                                    load_w(wt, w_in_d[:, cc:cc + 128], KC, 128, scale=g1)

                                def nat_view(T, j):
                                    if d == 1:
                                        return T[:, 512 * j:512 * (j + 1)]
                                    return T[:, :].rearrange("p (r m) -> p r m", r=d)[:, :, (512 * j) // d:(512 * (j + 1)) // d]

                                def nat_src(a):
                                    if d == 1:
                                        return a
                                    return a.rearrange("p (m r) -> p r m", r=d)

                                idx = 0
                                for (wt, T) in ((wq[wi], QT), (wk[wi], KT)):
                                    for j in range(NT):
                                        pp = ps_p[idx % 2]
                                        for kc in range(KC):
                                            P.mm(pp[:, 0:512], wt[:, kc, :], hT[:, kc, 512 * j:512 * (j + 1)], start=(kc == 0))
                                        P.act(sq[idx % 2][:], pp[:, 0:512], AF.Square)
                                        P.copy("dve", nat_view(T, j), nat_src(pp[:, 0:512]))
                                        P.mm(ps_n[:, 0:512], bdones, sq[idx % 2][:], start=True)
                                        P.act(lnb[:, idx, :], ps_n[:, 0:512], AF.Ln, scale=1.0 / 64, bias=EPS)
                                        idx += 1
                                idx = 0
                                for (T, gcol) in ((QT, PV_QG), (KT, PV_KG)):
                                    for j in range(NT):
                                        P.act(lnb[:, idx, :], lnb[:, idx, :], AF.Exp, scale=-0.5)
                                        P.stt("dve", nat_view(T, j), nat_view(T, j), pv[:, gcol:gcol + 1], nat_src(lnb[:, idx, :]), ALU.mult, ALU.mult)
                                        idx += 1
                                for bi in range(NB):
                                    r, n = bi // nbl, bi % nbl
                                    t0 = r + d * 128 * n
                                    for kc in range(KC):
                                        P.mm(ps_v[:, (bi % 4) * 128:(bi % 4 + 1) * 128], hT[:, kc, t0:t0 + d * 127 + 1:d], wv[wi][:, kc, :],
                                             start=(kc == 0 and bi % 4 == 0))
                                    if bi % 4 == 3:
                                        P.copy("act", Vt[:, bi - 3:bi + 1, :], ps_v[:, :].rearrange("p (k c) -> p k c", k=4))
                                def Sblk(bi):
                                    r, n = bi // nbl, bi % nbl
                                    c0 = r * L + 128 * n
                                    hp_ = n > 0
                                    ptb = pt[bi % 2]
                                    for h in range(2):
                                        pss = ps_s[bi % 2][h]
                                        rows = slice(64 * h, 64 * h + 64)
                                        if hp_:
                                            P.mm(pss[:, 0:128], KT[rows, c0 - 128:c0], QT[rows, c0:c0 + 128], start=True)
                                        P.mm(pss[:, 128:256], KT[rows, c0:c0 + 128], QT[rows, c0:c0 + 128], start=not hp_)
                                        lo = 0 if hp_ else 128
                                        P.act(ptb[:, h * 256 + lo:h * 256 + 256], pss[:, lo:256], AF.Exp, scale=0.125)
                                    if hp_:
                                        P.tt("pool", ptb[:, :], ptb[:, :], amask, ALU.mult)
                                    else:
                                        v3 = lambda a: a.rearrange("p (h c) -> p h c", h=2)[:, :, 128:256]
                                        P.tt("pool", v3(ptb[:, :]), v3(ptb[:, :]), v3(amask), ALU.mult)

                                def Vblk(bi):
                                    r, n = bi // nbl, bi % nbl
                                    hp_ = n > 0
                                    ptb = pt[bi % 2]
                                    pso = ps_o[bi % 2]
                                    for h in range(2):
                                        rows = slice(64 * h, 64 * h + 64)
                                        kw = {} if h == 0 else {"tile_position": (0, 64)}
                                        first = True
                                        for (lh, ncol) in ((None, 0), (ones, 128)):
                                            for pc in ((0, 1) if hp_ else (1,)):
                                                vb = bi - 1 if pc == 0 else bi
                                                lhsT = Vt[:, vb, 64 * h:64 * h + 64] if lh is None else ones[:, 0:64]
                                                P.mm(pso[rows, ncol:ncol + 128], lhsT, ptb[:, h * 256 + pc * 128:h * 256 + pc * 128 + 128],
                                                     start=first, **kw)
                                                first = False
                                    t0 = r + d * 128 * n
                                    av = acc[:, :, t0:t0 + d * 127 + 1:d]
                                    pv3 = pso[:, 0:256].rearrange("p (c q) -> p c q", c=2)
                                    if g == 0:
                                        P.copy("dve", av, pv3)
                                    else:
                                        P.tt("dve", av, pv3, av, ALU.add)

                                Sblk(0)
                                for bi in range(NB):
                                    if bi + 1 < NB:
                                        Sblk(bi + 1)
                                    Vblk(bi)
                            yb = yat[hp % 2]
                            for j in range(NT):
                                sl = slice(512 * j, 512 * (j + 1))
                                P.add("dve", lambda e, o=rden[j % 2][:], i=acc[:, 1, sl]: e.reciprocal(o, i), [acc[:, 1, sl]], [rden[j % 2][:]])
                                P.tt("pool", yb[:, sl], acc[:, 0, sl], rden[j % 2][:], ALU.mult)
                            P.dma(yat_d[b, hp, :, :], yb[:, :], f"yat{hp % 2}")
                P.barrier()
                if "ssd" in phases:
                    with ES() as st:
                        wdt = sbt(st, f"s_wdt{b}", [128, KC, 32], BF16)
                        dtb = sbt(st, f"s_dtb{b}", [128, 32], F32)
                        alog = sbt(st, f"s_alog{b}", [128, 32], F32)
                        xdt = sbt(st, f"s_xdt{b}", [128, NB, 32], F32)
                        tmpa = sbt(st, f"s_tmpa{b}", [128, NB, 32], F32)
                        tmpb = sbt(st, f"s_tmpb{b}", [128, NB, 32], F32)
                        dt_all = sbt(st, f"s_dt{b}", [128, NB, 32], F32)
                        a32 = sbt(st, f"s_a32{b}", [128, NB, 32], F32)
                        a_hi = sbt(st, f"s_ahi{b}", [128, NB, 32], BF16)
                        a_lo = sbt(st, f"s_alo{b}", [128, NB, 32], BF16)
                        wg = [sbt(st, f"s_wg{b}_{i}", [128, KC, 768], BF16) for i in range(2)]
                        raw = [sbt(st, f"s_raw{b}_{i}", [128, 515], F32) for i in range(4)]
                        accb = [sbt(st, f"s_acc{b}_{i}", [128, 512], F32) for i in range(2)]
                        thb = [sbt(st, f"s_th{b}_{i}", [128, 512], F32) for i in range(2)]
                        xbcT = [sbt(st, f"s_xbcT{b}_{i}", [128, 4, 512], BF16) for i in range(2)]
                        zs = [sbt(st, f"s_zs{b}_{i}", [128, 2, 512], BF16) for i in range(2)]
                        xbtok = [sbt(st, f"s_xbtok{b}_{i}", [128, 4, 384], BF16) for i in range(2)]
                        Rh = [sbt(st, f"s_Rh{b}_{i}", [128, 512], BF16) for i in range(4)]
                        Rl = [sbt(st, f"s_Rl{b}_{i}", [128, 512], BF16) for i in range(4)]
                        Eb = [sbt(st, f"s_E{b}_{i}", [128, 512], BF16) for i in range(2)]
                        EAb = [sbt(st, f"s_EA{b}_{i}", [128, 512], BF16) for i in range(2)]
                        Mp = [sbt(st, f"s_Mp{b}_{i}", [128, 512], BF16) for i in range(2)]
                        Cs = [sbt(st, f"s_Cs{b}_{i}", [128, 512], BF16) for i in range(2)]
                        CBm = [sbt(st, f"s_CBm{b}_{i}", [128, 128], BF16) for i in range(2)]
                        Xd = [sbt(st, f"s_Xd{b}_{i}", [128, 256], BF16) for i in range(2)]
                        ddc = [sbt(st, f"s_dd{b}_{i}", [128, 12], F32) for i in range(2)]
                        Sst = sbt(st, f"s_S{b}", [128, 256], F32)
                        Sbf = [sbt(st, f"s_Sbf{b}_{i}", [128, 256], BF16) for i in range(2)]
                        ybuf = [sbt(st, f"s_yb{b}_{i}", [128, 2, 512], F32) for i in range(2)]
                        sqw = sbt(st, f"s_sq{b}", [128, 2, 512], BF16)
                        lnw = sbt(st, f"s_lnw{b}", [128, 512], F32)
                        yo = [sbt(st, f"s_yo{b}_{i}", [128, 2, 512], BF16) for i in range(2)]
                        ps_p = [pst(st, f"s_psp{b}_{i}") for i in range(2)]
                        ps_seg = pst(st, f"s_seg{b}")
                        ps_acs = pst(st, f"s_acs{b}")
                        ps_cb = pst(st, f"s_cb{b}")
                        ps_st = pst(st, f"s_st{b}")
                        ps_y = pst(st, f"s_y{b}")
                        ps_tr = pst(st, f"s_tr{b}", BF16)
                        load_w(wdt, w_in_d[:, C_DT:C_DT + 32], KC, 32, scale=g1)
                        P.dma(dtb[:], dtb_d[:, :].partition_broadcast(128), "dtb")
                        P.dma(alog[:], alog_d[:, :].partition_broadcast(128), "alog")
                        for half in range(NB // 16):
                            pp = ps_p[half % 2]
                            for bl in range(16):
                                blk = half * 16 + bl
                                for kc in range(KC):
                                    P.mm(pp[:, bl * 32:(bl + 1) * 32], hT[:, kc, blk * 128:(blk + 1) * 128], wdt[:, kc, :],
                                         start=(bl == 0 and kc == 0))
                            P.tt("dve", xdt[:, half * 16:(half + 1) * 16, :], pp[:, :].rearrange("p (k h) -> p k h", k=16),
                                 dtb[:, :].unsqueeze(1).broadcast_to([128, 16, 32]), ALU.add)
                        P.act(tmpa[:], xdt[:], AF.Abs)
                        P.act(tmpa[:], tmpa[:], AF.Exp, scale=-1.0)
                        P.act(alog[:], alog[:], AF.Exp)
                        P.act(tmpb[:], tmpa[:], AF.Ln, bias=1.0)
                        P.stt("dve", dt_all[:], xdt[:], 0.0, tmpb[:], ALU.max, ALU.add)
                        P.tt("dve", a32[:], dt_all[:], alog[:, :].unsqueeze(1).broadcast_to([128, NB, 32]), ALU.mult)
                        P.ts("dve", a32[:], a32[:], -1.0, None, ALU.mult)
                        P.copy("dve", a_hi[:], a32[:])
                        P.tt("dve", tmpa[:], a32[:], a_hi[:], ALU.subtract)
                        P.copy("dve", a_lo[:], tmpa[:])
                        cwh = lambda i, c: pvh[:, PV_CW + i * 32 + c:PV_CW + i * 32 + c + 1]
                        cbh = lambda c: pvh[:, PV_CB + c:PV_CB + c + 1]
                        cnt = [0]
                        v4 = lambda a: a.rearrange("p (k l) -> p k l", k=4)

                        def load_group(g):
                            wgi = wg[g % 2]
                            load_w(wgi[:, :, 0:256], w_in_d[:, C_Z + g * 256:C_Z + (g + 1) * 256], KC, 256, scale=g1h)
                            load_w(wgi[:, :, 256:512], w_in_d[:, C_X + g * 256:C_X + (g + 1) * 256], KC, 256, scale=g1)
                            load_w(wgi[:, :, 512:640], w_in_d[:, C_B + g * 128:C_B + (g + 1) * 128], KC, 128, scale=g1)
                            load_w(wgi[:, :, 640:768], w_in_d[:, C_C + g * 128:C_C + (g + 1) * 128], KC, 128, scale=g1)

                        def PC(W):
                            g, w = divmod(W, NT)
                            wgi = wg[g % 2]
                            wi = W % 2
                            tsl = slice(512 * w, 512 * (w + 1))
                            cchunk = (2 * g, 2 * g + 1, 16 + g, 24 + g)
                            if w == 0:
                                for ct in range(4):
                                    P.memset("pool", raw[ct][:, 0:3], 0.0)
                            for hc in range(2):
                                pp = ps_p[cnt[0] % 2]
                                cnt[0] += 1
                                for kc in range(KC):
                                    P.mm(pp[:, 0:512], wgi[:, kc, hc * 128:(hc + 1) * 128], hT[:, kc, tsl], start=(kc == 0))
                                tb = thb[hc]
                                P.act(tb[:], pp[:, 0:512], AF.Tanh)
                                P.stt("dve", zs[wi][:, hc, :], tb[:], 1.0, pp[:, 0:512], ALU.add, ALU.mult)
                            for ct in range(4):
                                pp = ps_p[cnt[0] % 2]
                                cnt[0] += 1
                                for kc in range(KC):
                                    P.mm(pp[:, 0:512], wgi[:, kc, 256 + ct * 128:256 + (ct + 1) * 128], hT[:, kc, tsl], start=(kc == 0))
                                cc = cchunk[ct]
                                ab = accb[ct % 2]
                                tb = thb[ct % 2]
                                P.copy("act", raw[ct][:, 3:515], pp[:, 0:512])
                                P.act(ab[:], pp[:, 0:512], AF.Identity, scale=cwh(3, cc), bias=cbh(cc))
                                for i in range(3):
                                    P.stt("dve", ab[:], raw[ct][:, i:i + 512], cwh(i, cc), ab[:], ALU.mult, ALU.add)
                                P.copy("pool", raw[ct][:, 0:3], raw[ct][:, 512:515])
                                P.act(tb[:], ab[:], AF.Tanh)
                                P.stt("dve", xbcT[wi][:, ct, :], tb[:], 1.0, ab[:], ALU.add, ALU.mult)
                            for bp in range(2):
                                for k2 in range(2):
                                    blk = 2 * bp + k2
                                    for (ci_, ct) in enumerate((0, 1, 2)):
                                        P.tr(ps_tr[:, k2 * 384 + ci_ * 128:k2 * 384 + (ci_ + 1) * 128],
                                             xbcT[wi][:, ct, blk * 128:(blk + 1) * 128], ident)
                                P.copy("act", xbtok[wi][:, 2 * bp:2 * bp + 2, :], ps_tr[:, 0:768].rearrange("p (k c) -> p k c", k=2))

                        def RG(n):
                            g, cidx = divmod(n, NB)
                            for (R_, asrc) in ((Rh[n % 4], a_hi), (Rl[n % 4], a_lo)):
                                P.tt("pool", v4(R_[:, :]), maskU.unsqueeze(1).broadcast_to([128, 4, 128]),
                                     asrc[:, cidx, 4 * g:4 * g + 4].unsqueeze(2).broadcast_to([128, 4, 128]), ALU.mult)

                        def A(n):
                            g, cidx = divmod(n, NB)
                            W = n // 4
                            wi = W % 2
                            c4 = cidx % 4
                            ci = n % 2
                            tc = slice(128 * c4, 128 * (c4 + 1))
                            BT = xbcT[wi][:, 2, tc]
                            CT = xbcT[wi][:, 3, tc]
                            P.mm(ps_seg[:, 0:512], maskGT, Rh[n % 4][:, :], start=True)
                            P.mm(ps_seg[:, 0:512], maskGT, Rl[n % 4][:, :], start=False)
                            P.mm(ps_acs[:, 0:512], ones, Rh[n % 4][:, :], start=True)
                            P.mm(ps_acs[:, 0:512], ones, Rl[n % 4][:, :], start=False)
                            P.mm(ps_cb[:, 0:128], BT, CT, start=True)
                            dd = ddc[ci]
                            P.act(Eb[ci][:, :], ps_seg[:, 0:512], AF.Exp)
                            P.act(dd[:, 0:4], ps_seg[:, 127:512:128], AF.Exp)
                            P.act(EAb[ci][:, :], ps_acs[:, 0:512], AF.Exp)
                            P.act(dd[:, 4:8], ps_acs[:, 127:512:128], AF.Exp)
                            P.tt("dve", CBm[ci][:, :], ps_cb[:, 0:128], maskU, ALU.mult)
                            for k in range(4):
                                P.stt("dve", v4(Mp[ci][:, :])[:, k, :], v4(Eb[ci][:, :])[:, k, :],
                                      dt_all[:, cidx, 4 * g + k:4 * g + k + 1], CBm[ci][:, :], ALU.mult, ALU.mult)
                            P.tt("pool", v4(Cs[ci][:, :]), CT.unsqueeze(1).broadcast_to([128, 4, 128]), v4(EAb[ci][:, :]), ALU.mult)
                            P.tt("dve", dd[:, 8:12], dd[:, 0:4], dt_all[:, cidx, 4 * g:4 * g + 4], ALU.mult)
                            P.tt("dve", Xd[ci][:, :].rearrange("p (k q) -> p k q", k=4),
                                 xbtok[wi][:, c4, 0:256].rearrange("p (k q) -> p k q", k=4),
                                 dd[:, 8:12].unsqueeze(2).broadcast_to([128, 4, 64]), ALU.mult)

                        def Bk(n):
                            g, cidx = divmod(n, NB)
                            W = n // 4
                            wi = W % 2
                            c4 = cidx % 4
                            ci = n % 2
                            tc = slice(128 * c4, 128 * (c4 + 1))
                            dd = ddc[ci]
                            sb_old = Sbf[n % 2]
                            sb_new = Sbf[(n + 1) % 2]
                            firstq = [True, True]
                            for hc in range(2):
                                for hh in range(2):
                                    k = 2 * hc + hh
                                    kw = {} if hh == 0 else {"tile_position": (0, 64)}
                                    o = ps_y[64 * hh:64 * hh + 64, hc * 128:(hc + 1) * 128]
                                    P.mm(o, xbtok[wi][:, c4, k * 64:(k + 1) * 64], v4(Mp[ci][:, :])[:, k, :], start=firstq[hh], **kw)
                                    firstq[hh] = False
                                    if cidx > 0:
                                        P.mm(o, sb_old[:, k * 64:(k + 1) * 64], v4(Cs[ci][:, :])[:, k, :], start=False, **kw)
                            P.mm(ps_st[:, 0:256], xbtok[wi][:, c4, 256:384], Xd[ci][:, :], start=True)
                            if cidx == 0:
                                P.copy("dve", Sst[:, :], ps_st[:, 0:256])
                            else:
                                s3 = Sst[:, :].rearrange("p (k q) -> p k q", k=4)
                                P.tt("dve", s3, s3, dd[:, 4:8].unsqueeze(2).broadcast_to([128, 4, 64]), ALU.mult)
                                P.tt("dve", Sst[:, :], Sst[:, :], ps_st[:, 0:256], ALU.add)
                            P.copy("act", sb_new[:, :], Sst[:, :])
                            for hc in range(2):
                                P.stt("dve", ybuf[wi][:, hc, tc], xbcT[wi][:, hc, tc], pv[:, PV_D + 2 * g + hc:PV_D + 2 * g + hc + 1],
                                      ps_y[:, hc * 128:(hc + 1) * 128], ALU.mult, ALU.add)

                        def WN(W):
                            g, w = divmod(W, NT)
                            wi = W % 2
                            tsl = slice(512 * w, 512 * (w + 1))
                            P.tt("dve", ybuf[wi][:], ybuf[wi][:], zs[wi][:], ALU.mult)
                            P.act(sqw[:], ybuf[wi][:], AF.Square)
                            pp = ps_p[cnt[0] % 2]
                            cnt[0] += 1
                            P.mm(pp[:, 0:512], ones, sqw[:, 0, :], start=True)
                            P.mm(pp[:, 0:512], ones, sqw[:, 1, :], start=False)
                            P.act(lnw[:], pp[:, 0:512], AF.Ln, scale=1.0 / 256, bias=EPS)
                            P.act(lnw[:], lnw[:], AF.Exp, scale=-0.5)
                            P.tt("dve", yo[wi][:], ybuf[wi][:], lnw[:, :].unsqueeze(1).broadcast_to([128, 2, 512]), ALU.mult)
                            P.dma(ysd_d[b, 2 * g:2 * g + 2, :, tsl].rearrange("c p t -> p c t"), yo[wi][:], f"yo{wi}")

                        NCH = 8 * NB
                        NW = 8 * NT
                        load_group(0)
                        PC(0)
                        RG(0)
                        RG(1)
                        for n in range(NCH):
                            g, cidx = divmod(n, NB)
                            W = n // 4
                            c4 = n % 4
                            if n + 2 < NCH:
                                RG(n + 2)
                            A(n)
                            if n > 0:
                                Bk(n - 1)
                                if c4 == 0:
                                    WN(W - 1)
                            if cidx == 12 and g + 1 < 8:
                                load_group(g + 1)
                            if c4 == 1 and W + 1 < NW:
                                PC(W + 1)
                        Bk(NCH - 1)
                        WN(NW - 1)
            P.barrier()

        tiles = [(b, j) for b in range(NS) for j in range(NT)]
        if "merge" in phases:
            with ES() as st:
                Wssd = sbt(st, "m_wssd", [128, 16, D], BF16)
                Watt = sbt(st, "m_watt", [128, 4, D], BF16)
                Wgs = sbt(st, "m_wgs", [128, KC, D], BF16)
                Wga = sbt(st, "m_wga", [128, KC, D], BF16)
                Wout = sbt(st, "m_wout", [128, KC, D], BF16)
                ys = [sbt(st, f"m_ys{i}", [128, 16, 512], BF16) for i in range(2)]
                ya = [sbt(st, f"m_ya{i}", [128, 4, 512], BF16) for i in range(2)]
                hTt = [sbt(st, f"m_ht{i}", [128, KC, 512], BF16) for i in range(2)]
                xt = [sbt(st, f"m_xt{i}", [128, 4, D], F32) for i in range(2)]
                th = [sbt(st, f"m_th{i}", [128, 512], F32) for i in range(2)]
                m1 = [sbt(st, f"m_m1{i}", [128, 512], F32) for i in range(3)]
                mT = sbt(st, "m_mT", [128, KC, 512], BF16)
                psm = [pst(st, f"m_ps{i}") for i in range(8)]
                pcm = [0]

                def m_loads(t):
                    b, j = tiles[t]
                    i2 = t % 2
                    tsl = slice(512 * j, 512 * (j + 1))
                    P.dma(hTt[i2][:], hTs_d[b, :, :, tsl].rearrange("c p t -> p c t"), f"mht{i2}")
                    P.dma(ys[i2][:], ysd_d[b, :, :, tsl].rearrange("c p t -> p c t"), f"mys{i2}")
                    P.dma(ya[i2][:], yat_d[b, :, :, tsl].rearrange("c p t -> p c t"), f"mya{i2}")
                    P.dma(xt[i2][:], x_d[b, tsl, :].rearrange("(k p) d -> p k d", p=128), f"mxt{i2}")

                def m_compute(t):
                    b, j = tiles[t]
                    i2 = t % 2
                    tsl = slice(512 * j, 512 * (j + 1))
                    for oc in range(KC):
                        osl = slice(oc * 128, (oc + 1) * 128)
                        pc = pcm[0]
                        p1, pg1, p2, pg2 = psm[pc % 8], psm[(pc + 1) % 8], psm[(pc + 2) % 8], psm[(pc + 3) % 8]
                        pcm[0] += 4
                        e2 = 0
                        for kc in range(KC):
                            P.mm(pg1[:, 0:512], Wgs[:, kc, osl], hTt[i2][:, kc, :], start=(kc == 0))
                        for kc in range(16):
                            P.mm(p1[:, 0:512], Wssd[:, kc, osl], ys[i2][:, kc, :], start=(kc == 0))
                        for kc in range(KC):
                            P.mm(pg2[:, 0:512], Wga[:, kc, osl], hTt[i2][:, kc, :], start=(kc == 0))
                        for kc in range(4):
                            P.mm(p2[:, 0:512], Watt[:, kc, osl], ya[i2][:, kc, :], start=(kc == 0))
                        P.act(th[e2][:], pg1[:, 0:512], AF.Tanh)
                        P.stt("dve", m1[e2][:], th[e2][:], 1.0, p1[:, 0:512], ALU.add, ALU.mult)
                        P.act(th[e2 + 1][:], pg2[:, 0:512], AF.Tanh)
                        P.stt("dve", m1[e2 + 1][:], th[e2 + 1][:], 1.0, p2[:, 0:512], ALU.add, ALU.mult)
                        P.tt("pool", mT[:, oc, :], m1[e2][:], m1[e2 + 1][:], ALU.add)
                    for blk in range(4):
                        for half in range(2):
                            pp = psm[pcm[0] % 8]
                            pcm[0] += 1
                            hs = slice(half * 512, (half + 1) * 512)
                            for kc in range(KC):
                                P.mm(pp[:, 0:512], mT[:, kc, blk * 128:(blk + 1) * 128], Wout[:, kc, hs], start=(kc == 0))
                            P.tt("dve", xt[i2][:, blk, hs], xt[i2][:, blk, hs], pp[:, 0:512], ALU.add)
                    P.dma(out_d[b, tsl, :].rearrange("(k p) d -> p k d", p=128), xt[i2][:], f"mo{i2}")

                m_loads(0)
                load_w(Wgs, w_in_d[:, C_GS:C_GS + D], KC, D, scale=g1h)
                load_w(Wssd, w_ssd_d, 16, D, scale=pv[:, PV_GS:PV_GS + 16])
                load_w(Wga, w_in_d[:, C_GA:C_GA + D], KC, D, scale=g1h)
                load_w(Watt, w_att_d, 4, D)
                load_w(Wout, w_out_d, KC, D, scale=0.5)
                for t in range(len(tiles)):
                    if t + 1 < len(tiles):
                        m_loads(t + 1)
                    m_compute(t)
            P.barrier()

        if "ffn" in phases:
            with ES() as st:
                Wup = sbt(st, "f_wup", [128, KC, 2 * DFF], BF16)
                Wdn = sbt(st, "f_wdn", [128, NFC, D], BF16)
                xb = [sbt(st, f"f_xb{i}", [128, D], F32) for i in range(2)]
                h16 = sbt(st, "f_h16", [128, 1, D], BF16)
                ssq = [sbt(st, f"f_ss{i}", [128, 1], F32) for i in range(2)]
                lnv = [sbt(st, f"f_ln{i}", [128, 1], F32) for i in range(2)]
                rstd = [sbt(st, f"f_rs{i}", [128, 1], F32) for i in range(2)]
                h2T = [sbt(st, f"f_h2T{i}", [128, KC, 512], BF16) for i in range(2)]
                aT = sbt(st, "f_aT", [128, NFC, 512], BF16)
                tail = sbt(st, "f_tail", [128, 2 * NFC, 2], F32)
                rawf = [sbt(st, f"f_raw{i}", [128, 514], F32) for i in range(2)]
                accf = [sbt(st, f"f_acc{i}", [128, 512], F32) for i in range(4)]
                ostg = [sbt(st, f"f_os{i}", [128, 512], F32) for i in range(2)]
                pbf = [pst(st, f"f_pb{i}", BF16) for i in range(2)]
                psf = [pst(st, f"f_ps{i}") for i in range(6)]
                pcf = [0]
                blkc = [0]
                def fwc(i, ci):
                    src = pvh if ci < NFC else pv
                    return src[:, PV_FW + i * 44 + ci:PV_FW + i * 44 + ci + 1]

                def fbc(ci):
                    src = pvh if ci < NFC else pv
                    return src[:, PV_FB + ci:PV_FB + ci + 1]

                def f_norm(t):
                    b, j = tiles[t]
                    hb = h2T[t % 2]
                    for k in range(4):
                        q = blkc[0] % 2
                        blkc[0] += 1
                        r0 = 512 * j + 128 * k
                        P.dma(xb[q][:], out_d[b, r0:r0 + 128, :], f"fxb{q}")
                        P.memset("pool", ssq[q][:], 0.0)
                        P.act(h16[:, 0, :], xb[q][:], AF.Square, accum_out=ssq[q][:, 0:1])
                        P.act(lnv[q][:], ssq[q][:], AF.Ln, scale=1.0 / D, bias=EPS)
                        P.act(rstd[q][:], lnv[q][:], AF.Exp, scale=-0.5)
                        P.scale("dve" if k % 2 == 0 else "act", h16[:, 0, :], xb[q][:], rstd[q][:, 0:1])
                        bank = pbf[q]
                        for kc in range(KC):
                            P.tr(bank[:, kc * 128:(kc + 1) * 128], h16[:, 0, kc * 128:(kc + 1) * 128], ident)
                        P.copy("act" if k % 2 == 0 else "dve", hb[:, :, k * 128:(k + 1) * 128], bank[:, :].rearrange("p (c t) -> p c t", c=KC))

                def f_up(t):
                    b, j = tiles[t]
                    hb = h2T[t % 2]
                    if j == 0:
                        P.memset("pool", tail[:], 0.0)
                    for c in range(NFC):
                        res = []
                        for half in range(2):
                            ci = half * NFC + c
                            pp = psf[pcf[0] % 6]
                            pcf[0] += 1
                            col = half * DFF + c * 128
                            for kc in range(KC):
                                P.mm(pp[:, 0:512], Wup[:, kc, col:col + 128], hb[:, kc, :], start=(kc == 0))
                            rw = rawf[half]
                            ab = accf[(2 * c + half) % 4]
                            P.copy("pool", rw[:, 0:2], tail[:, ci, :])
                            P.copy("act", rw[:, 2:514], pp[:, 0:512])
                            P.act(ab[:], pp[:, 0:512], AF.Identity, scale=fwc(2, ci), bias=fbc(ci))
                            P.copy("pool", tail[:, ci, :], rw[:, 512:514])
                            for i in range(2):
                                P.stt("dve", ab[:], rw[:, i:i + 512], fwc(i, ci), ab[:], ALU.mult, ALU.add)
                            res.append(ab)
                        tf = rawf[0][:, 2:514]
                        P.act(tf, res[0][:], AF.Tanh)
                        P.stt("dve", res[0][:], tf, 1.0, res[0][:], ALU.add, ALU.mult)
                        P.tt("pool", aT[:, c, :], res[0][:], res[1][:], ALU.mult)

                def f_down(t):
                    b, j = tiles[t]
                    for blk in range(4):
                        r0 = 512 * j + 128 * blk
                        for half in range(2):
                            pp = psf[pcf[0] % 6]
                            q = pcf[0] % 2
                            pcf[0] += 1
                            hs = slice(half * 512, (half + 1) * 512)
                            for c in range(NFC):
                                P.mm(pp[:, 0:512], aT[:, c, blk * 128:(blk + 1) * 128], Wdn[:, c, hs], start=(c == 0))
                            P.copy("act", ostg[q][:], pp[:, 0:512])
                            P.dma(out_d[b, r0:r0 + 128, hs], ostg[q][:], f"fo{q}", eng="pool", accum_op=ALU.add)

                f_norm(0)
                load_w(Wup, w_up_d, KC, 2 * DFF, scale=pv[:, PV_G2:PV_G2 + 8])
                load_w(Wdn, w_dn_d, NFC, D)
                nt_ = len(tiles)
                f_up(0)
                for t in range(nt_):
                    if t + 1 < nt_:
                        f_norm(t + 1)
                    f_down(t)
                    if t + 1 < nt_:
                        f_up(t + 1)
        P.barrier(final=True)
        P.emit()
    return nc, P


_CACHE = {}


def _core_inputs(inp, c, NS):
    x = np.ascontiguousarray(np.asarray(inp["x"], dtype=np.float32)[c * NS:(c + 1) * NS])
    return x


def kernel(**inputs):
    NCORES = 8
    NS = 2
    if "nc" not in _CACHE:
        _CACHE["nc"] = build(NS=NS)[0]
    nc = _CACHE["nc"]
    f = lambda k: np.ascontiguousarray(np.asarray(inputs[k], dtype=np.float32)[0])
    shared = {
        "w_in": f("w_in"), "w_ssd_proj": f("w_ssd_proj"), "w_attn_proj": f("w_attn_proj"), "w_out": f("w_out"),
        "w_up": f("w_up"), "w_down": f("w_down"), "pvec": pack_pvec(inputs), "cmask": make_cmask(),
        "dt_bias": np.asarray(inputs["dt_bias"], np.float32).reshape(1, 32),
        "a_log": np.asarray(inputs["a_log"], np.float32).reshape(1, 32),
    }
    in_maps = []
    for c in range(NCORES):
        m = dict(shared)
        m["x"] = _core_inputs(inputs, c, NS)
        in_maps.append(m)
    res = run_bass_kernel_spmd(nc, in_maps, core_ids=list(range(NCORES)))
    out = np.concatenate([np.asarray(r["out"]) for r in res.results], axis=0)
    return out.astype(np.float32)
```

```python
import contextlib
import numpy as np
import ml_dtypes
import concourse.bass as bass
import concourse.mybir as mybir
from concourse.bass_utils import run_bass_kernel_spmd

F32 = mybir.dt.float32
BF16 = mybir.dt.bfloat16
AF = mybir.ActivationFunctionType
ALU = mybir.AluOpType

ENGS = ("pe", "act", "dve", "pool", "sp")
SMALL_N = 256


def region(ap):
    t = ap.tensor
    dims = ap.ap
    off = int(ap.offset)
    name = t.name
    cls = type(t).__name__
    if cls.startswith("DRam"):
        lo = off
        hi = off + sum((c - 1) * abs(s) for s, c in dims) + 1
        return (name, 0, 1, lo, hi)
    if cls.startswith("PSum"):
        return (name, 0, 128, 0, 1 << 30)
    ps, npart = dims[0]
    p0 = off // ps
    f0 = off % ps
    f1 = f0 + sum((c - 1) * abs(s) for s, c in dims[1:]) + 1
    return (name, p0, p0 + npart, f0, f1)


class Op:
    __slots__ = ("eng", "fn", "idx", "deps", "signaled", "dma_slot", "dma_cnt", "nfree", "sig_no")

    def __init__(self, eng, fn):
        self.eng = eng
        self.fn = fn
        self.deps = {}
        self.signaled = False
        self.dma_slot = None
        self.dma_cnt = 0
        self.nfree = 1 << 30
        self.sig_no = 0


class Prog:
    def __init__(self, nc):
        self.nc = nc
        self.ops = []
        self.recs = {}
        self.slot_last = {}
        self.slot_cnt = {}
        self.barrier_idx = {}

    def _track(self, op, reads, writes):
        deps = op.deps
        ekey = op.eng if op.dma_slot is None else ("dma", op.dma_slot)

        def add_dep(r):
            k = r[5]
            if k == ekey and op.dma_slot is None:
                if k == "pe":
                    return
            if deps.get(k, -1) < r[6]:
                deps[k] = r[6]

        psum_reads = [ap for ap in reads if type(ap.tensor).__name__.startswith("PSum")]
        if psum_reads:
            reads = [ap for ap in reads if not type(ap.tensor).__name__.startswith("PSum")]
            writes = list(writes) + psum_reads
        for ap in reads:
            name, p0, p1, f0, f1 = region(ap)
            lst = self.recs.setdefault(name, [])
            found = None
            for r in lst:
                if r[4] == "w":
                    if r[0] < p1 and p0 < r[1] and r[2] < f1 and f0 < r[3]:
                        add_dep(r)
                elif r[5] == ekey and r[0] == p0 and r[1] == p1 and r[2] == f0 and r[3] == f1:
                    found = r
            if found is not None:
                found[6] = op.idx
            else:
                lst.append([p0, p1, f0, f1, "r", ekey, op.idx, op.nfree])
        for ap in writes:
            name, p0, p1, f0, f1 = region(ap)
            lst = self.recs.setdefault(name, [])
            keep = []
            for r in lst:
                if r[0] < p1 and p0 < r[1] and r[2] < f1 and f0 < r[3]:
                    if r[6] != op.idx:
                        add_dep(r)
                    if r[0] >= p0 and r[1] <= p1 and r[2] >= f0 and r[3] <= f1 and r[6] != op.idx:
                        continue
                keep.append(r)
            keep.append([p0, p1, f0, f1, "w", ekey, op.idx, op.nfree])
            self.recs[name] = keep

    def add(self, eng, fn, reads=(), writes=(), nfree=None):
        op = Op(eng, fn)
        op.idx = len(self.ops)
        if nfree is None:
            nfree = 1 << 30
            for ap in writes:
                d = ap.ap
                n = 1
                for s, c in d[1:]:
                    n *= c
                nfree = min(nfree, n)
        op.nfree = nfree
        self.ops.append(op)
        self._track(op, reads, writes)
        return op

    def dma(self, out, in_, slot, eng="sp", **kw):
        def fn(e, out=out, in_=in_, kw=kw):
            return e.dma_start(out=out, in_=in_, **kw)

        op = Op(eng, fn)
        op.idx = len(self.ops)
        op.dma_slot = slot
        self.slot_cnt[slot] = self.slot_cnt.get(slot, 0) + 16
        op.dma_cnt = self.slot_cnt[slot]
        self.ops.append(op)
        prev = self.slot_last.get(slot)
        if prev is not None:
            op.deps[("dma", slot)] = prev
        self.slot_last[slot] = op.idx
        self._track(op, [in_], [out])
        return op

    def mm(self, out, lhsT, rhs, start=True, stop=True, **kw):
        if not start:
            kw.setdefault("skip_group_check", True)
        return self.add("pe", lambda e: e.matmul(out, lhsT, rhs, start=start, stop=stop, **kw),
                        [lhsT, rhs], [out])

    def tr(self, out, in_, ident):
        return self.add("pe", lambda e: e.transpose(out, in_, ident), [in_, ident], [out])

    def act(self, out, in_, func, bias=None, scale=None, accum_out=None, eng="act"):
        reads = [in_]
        kw = {}
        if bias is not None:
            kw["bias"] = bias
            if not isinstance(bias, (int, float)):
                reads.append(bias)
        if scale is not None:
            kw["scale"] = scale
            if not isinstance(scale, (int, float)):
                reads.append(scale)
        writes = [out]
        if accum_out is not None:
            kw["accum_out"] = accum_out
            writes.append(accum_out)
        return self.add(eng, lambda e: e.activation(out, in_, func, **kw), reads, writes)

    def tt(self, eng, out, in0, in1, op):
        return self.add(eng, lambda e: e.tensor_tensor(out, in0, in1, op), [in0, in1], [out])

    def ts(self, eng, out, in0, s1, s2, op0, op1=None, accum_out=None):
        reads = [in0]
        if not isinstance(s1, (int, float)):
            reads.append(s1)
        if s2 is not None and not isinstance(s2, (int, float)):
            reads.append(s2)
        kw = {}
        writes = [out]
        if accum_out is not None:
            kw["accum_out"] = accum_out
            writes.append(accum_out)
        if op1 is None:
            return self.add(eng, lambda e: e.tensor_scalar(out, in0, s1, None, op0, **kw), reads, writes)
        return self.add(eng, lambda e: e.tensor_scalar(out, in0, s1, s2, op0, op1, **kw), reads, writes)

    def stt(self, eng, out, in0, scalar, in1, op0, op1):
        reads = [in0, in1]
        if not isinstance(scalar, (int, float)):
            reads.append(scalar)
        return self.add(eng, lambda e: e.scalar_tensor_tensor(out, in0, scalar, in1, op0, op1), reads, [out])

    def scale(self, eng, out, in_, sc):
        if eng == "act":
            return self.act(out, in_, AF.Copy, scale=sc)
        return self.ts(eng, out, in_, sc, None, ALU.mult)

    def copy(self, eng, out, in_):
        if eng == "act":
            return self.add(eng, lambda e: e.copy(out, in_), [in_], [out])
        return self.add(eng, lambda e: e.tensor_copy(out, in_), [in_], [out])

    def memset(self, eng, ap, val):
        return self.add(eng, lambda e: e.memset(ap, val), [], [ap])

    def barrier(self, final=False):
        op = Op("sp", None)
        op.idx = len(self.ops)
        self.ops.append(op)
        for slot, i in self.slot_last.items():
            op.deps[("dma", slot)] = i
        if not final:
            m = Op("marker", None)
            m.idx = len(self.ops)
            self.ops.append(m)
        self.recs = {}

    def emit(self):
        nc = self.nc
        ops = self.ops
        for op in ops:
            for k, i in op.deps.items():
                ops[i].signaled = True
        cnt = {e: 0 for e in ENGS}
        for op in ops:
            if op.eng == "marker":
                continue
            if op.dma_slot is None and op.signaled:
                cnt[op.eng] += 1
                op.sig_no = cnt[op.eng]
        slots = sorted(self.slot_cnt.keys())
        import contextlib

        with contextlib.ExitStack() as st:
            esem = {e: st.enter_context(nc.semaphore("s_" + e)) for e in ENGS}
            ssem = {s: st.enter_context(nc.semaphore("d_" + s)) for s in slots}
            segs = [[]]
            for op in ops:
                if op.eng == "marker":
                    segs.append([])
                else:
                    segs[-1].append(op)
            stats = {e: [0, 0] for e in ENGS}
            waited_all = {e: {} for e in ENGS}

            def run(eng_name, e, seg):
                waited = waited_all[eng_name]
                for op in seg:
                    if op.eng != eng_name:
                        continue
                    for k, i in op.deps.items():
                        p = ops[i]
                        if isinstance(k, tuple):
                            sem, val = ssem[k[1]], p.dma_cnt
                        else:
                            sem, val = esem[k], p.sig_no
                        if waited.get(k, 0) >= val:
                            continue
                        waited[k] = val
                        e.wait_ge(sem, val)
                        stats[eng_name][1] += 1
                    if op.fn is None:
                        continue
                    ins = op.fn(e)
                    stats[eng_name][0] += 1
                    if op.dma_slot is not None:
                        ins.then_inc(ssem[op.dma_slot], 16)
                    elif op.signaled:
                        ins.then_inc(esem[eng_name], 1)

            for seg in segs:
                with nc.Block() as block:
                    @block.tensor
                    def _(e, seg=seg):
                        run("pe", e, seg)

                    @block.scalar
                    def _(e, seg=seg):
                        run("act", e, seg)

                    @block.vector
                    def _(e, seg=seg):
                        run("dve", e, seg)

                    @block.gpsimd
                    def _(e, seg=seg):
                        run("pool", e, seg)

                    @block.sync
                    def _(e, seg=seg):
                        run("sp", e, seg)

            self.stats = stats
            self.sig_counts = cnt


D = 1024
KC = 8
S_FULL = 4096
C_Z, C_X, C_B, C_C, C_DT, C_Q, C_K, C_V, C_GS, C_GA = 0, 2048, 4096, 5120, 6144, 6176, 7712, 9248, 10784, 11808
DIL = (1, 4, 16)
EPS = 1e-6
DFF = 2816
NFC = 22
PV_G1, PV_G2, PV_GS, PV_CW, PV_CB, PV_D, PV_QG, PV_KG, PV_FW, PV_FB, NPV = 0, 8, 16, 32, 160, 192, 208, 209, 210, 342, 386
CM_ID, CM_U, CM_L, CM_GT, CM_ONE, CM_AM, CM_BD, NCM = 0, 128, 256, 384, 512, 640, 1152, 1280

_bf = ml_dtypes.bfloat16


def _fm(v):
    return np.ascontiguousarray(np.asarray(v).reshape(-1, 128).T)


def pack_pvec(inp):
    pv = np.zeros((128, NPV), np.float32)
    pv[:, PV_G1:PV_G1 + 8] = _fm(inp["norm1_g"][0])
    pv[:, PV_G2:PV_G2 + 8] = _fm(inp["norm2_g"][0])
    pv[:, PV_GS:PV_GS + 16] = _fm(inp["ssd_norm_g"][0])
    for i in range(4):
        pv[:, PV_CW + i * 32:PV_CW + (i + 1) * 32] = _fm(inp["ssd_conv_w"][0, i])
    pv[:, PV_CB:PV_CB + 32] = _fm(inp["ssd_conv_b"][0])
    pv[:, PV_D:PV_D + 16] = _fm(np.repeat(inp["d_skip"][0], 64))
    pv[:, PV_QG] = np.tile(inp["q_norm_g"][0], 2)
    pv[:, PV_KG] = np.tile(inp["k_norm_g"][0], 2)
    for i in range(3):
        pv[:, PV_FW + i * 44:PV_FW + (i + 1) * 44] = _fm(inp["ffn_conv_w"][0, i])
    pv[:, PV_FB:PV_FB + 44] = _fm(inp["ffn_conv_b"][0])
    return pv


def make_cmask():
    cm = np.zeros((128, NCM), np.float32)
    i = np.arange(128)
    U = (i[:, None] <= i[None, :]).astype(np.float32)
    L = (i[:, None] >= i[None, :]).astype(np.float32)
    GT = (i[:, None] > i[None, :]).astype(np.float32)
    cm[:, CM_ID:CM_ID + 128] = np.eye(128)
    cm[:, CM_U:CM_U + 128] = U
    cm[:, CM_L:CM_L + 128] = L
    cm[:, CM_GT:CM_GT + 128] = GT
    cm[:, CM_ONE:CM_ONE + 128] = 1.0
    cm[:, CM_AM:CM_AM + 512] = np.concatenate([L, U, L, U], 1)
    bd = np.zeros((128, 128), np.float32)
    bd[:64, :64] = 1.0
    bd[64:, 64:] = 1.0
    cm[:, CM_BD:CM_BD + 128] = bd
    return cm.astype(_bf)


def build(NS=2, S=S_FULL, debug=False, phases=("norm", "attn", "ssd", "merge", "ffn")):
    nc = bass.Bass("TRN2", target_bir_lowering=False)
    NT = S // 512
    NB = S // 128

    def din(name, shape, dt=F32):
        return nc.dram_tensor(name, shape, dt, kind="ExternalInput").ap()

    x_d = din("x", [NS, S, D])
    w_in_d = din("w_in", [D, 12832])
    w_ssd_d = din("w_ssd_proj", [2048, D])
    w_att_d = din("w_attn_proj", [512, D])
    w_out_d = din("w_out", [D, D])
    w_up_d = din("w_up", [D, 2 * DFF])
    w_dn_d = din("w_down", [DFF, D])
    pv_d = din("pvec", [128, NPV])
    cm_d = din("cmask", [128, NCM], BF16)
    dtb_d = din("dt_bias", [1, 32])
    alog_d = din("a_log", [1, 32])
    skind = "ExternalOutput" if debug else "Internal"
    hTs_d = nc.dram_tensor("hTs", [NS, KC, 128, S], BF16, kind=skind).ap()
    yat_d = nc.dram_tensor("yattn", [NS, 4, 128, S], BF16, kind=skind).ap()
    ysd_d = nc.dram_tensor("yssd", [NS, 16, 128, S], BF16, kind=skind).ap()
    out_d = nc.dram_tensor("out", [NS, S, D], F32, kind="ExternalOutput").ap()

    P = Prog(nc)
    ES = contextlib.ExitStack

    with ES() as top:
        def sbt(st, name, shape, dt):
            return st.enter_context(nc.sbuf_tensor(name, shape, dt))

        def pst(st, name, dt=F32):
            return st.enter_context(nc.psum_tensor(name, [128, 512 if dt == F32 else 1024], dt))

        pv = sbt(top, "pv", [128, NPV], F32)
        pvh = sbt(top, "pvh", [128, NPV], F32)
        cm = sbt(top, "cm", [128, NCM], BF16)
        P.dma(pv[:], pv_d[:, :], "pv")
        P.dma(cm[:], cm_d[:, :], "cm")
        P.ts("dve", pvh[:], pv[:], 0.5, None, ALU.mult)
        ident = cm[:, CM_ID:CM_ID + 128]
        maskU = cm[:, CM_U:CM_U + 128]
        maskGT = cm[:, CM_GT:CM_GT + 128]
        ones = cm[:, CM_ONE:CM_ONE + 128]
        amask = cm[:, CM_AM:CM_AM + 512]
        bdones = cm[:, CM_BD:CM_BD + 128]
        stg = [sbt(top, f"stg{i}", [128, 512], F32) for i in range(2)]
        stg_i = [0, 0]

        def load_w(dst, src, kn, ncols, scale=None, engs=("act", "dve")):
            SW = 512
            cw = min(ncols, SW)
            kpp = max(1, SW // cw)
            for c0 in range(0, ncols, cw):
                cn = min(cw, ncols - c0)
                for k0 in range(0, kn, kpp):
                    kk = min(kpp, kn - k0)
                    i = stg_i[0] % 2
                    stg_i[0] += 1
                    sv = stg[i][:, 0:kk * cn].rearrange("p (k n) -> p k n", k=kk)
                    P.dma(sv, src[k0 * 128:(k0 + kk) * 128, c0:c0 + cn].rearrange("(k p) n -> p k n", p=128), f"stg{i}")
                    for k in range(kk):
                        d_ = dst[:, k0 + k, c0:c0 + cn]
                        eng = engs[stg_i[1] % len(engs)]
                        stg_i[1] += 1
                        if scale is None:
                            P.copy(eng, d_, sv[:, k, :])
                        elif isinstance(scale, float):
                            P.scale(eng, d_, sv[:, k, :], scale)
                        else:
                            P.scale(eng, d_, sv[:, k, :], scale[:, k0 + k:k0 + k + 1])

        g1 = pv[:, PV_G1:PV_G1 + 8]
        g1h = pvh[:, PV_G1:PV_G1 + 8]

        for b in range(NS):
            with ES() as seq:
                hT = sbt(seq, f"hT{b}", [128, KC, S], BF16)
                if "norm" in phases:
                    with ES() as st:
                        xt = [sbt(st, f"n_xt{b}_{i}", [128, 4, D], F32) for i in range(2)]
                        junk = sbt(st, f"n_junk{b}", [128, D], BF16)
                        h16 = [sbt(st, f"n_h16{b}_{i}", [128, 4, D], BF16) for i in range(2)]
                        ssq = [sbt(st, f"n_ss{b}_{i}", [128, 4], F32) for i in range(2)]
                        lnv = [sbt(st, f"n_ln{b}_{i}", [128, 4], F32) for i in range(2)]
                        rstd = [sbt(st, f"n_rs{b}_{i}", [128, 4], F32) for i in range(2)]
                        pb = [pst(st, f"n_pb{b}_{i}", BF16) for i in range(4)]
                        for j in range(NT):
                            i2 = j % 2
                            P.dma(xt[i2][:], x_d[b, 512 * j:512 * (j + 1), :].rearrange("(k p) d -> p k d", p=128), f"xt{i2}")
                            P.memset("pool", ssq[i2][:], 0.0)
                            for k in range(4):
                                P.act(junk[:], xt[i2][:, k, :], AF.Square, accum_out=ssq[i2][:, k:k + 1])
                            P.act(lnv[i2][:], ssq[i2][:], AF.Ln, scale=1.0 / D, bias=EPS)
                            P.act(rstd[i2][:], lnv[i2][:], AF.Exp, scale=-0.5)
                            for k in range(4):
                                P.scale("dve" if k % 2 == 0 else "act", h16[i2][:, k, :], xt[i2][:, k, :], rstd[i2][:, k:k + 1])
                            for pr in range(4):
                                bank = pb[pr]
                                for kk in range(2):
                                    kc = 2 * pr + kk
                                    for k in range(4):
                                        P.tr(bank[:, kk * 512 + k * 128:kk * 512 + (k + 1) * 128], h16[i2][:, k, kc * 128:(kc + 1) * 128], ident)
                                P.copy("act" if pr % 2 == 0 else "dve", hT[:, 2 * pr:2 * pr + 2, 512 * j:512 * (j + 1)],
                                       bank[:, :].rearrange("p (k t) -> p k t", k=2))
                            P.dma(hTs_d[b, :, :, 512 * j:512 * (j + 1)].rearrange("k p t -> p k t"), hT[:, :, 512 * j:512 * (j + 1)], f"hs{i2}")
                P.barrier()
                if "attn" in phases:
                    with ES() as st:
                        wq = [sbt(st, f"a_wq{b}_{i}", [128, KC, 128], BF16) for i in range(2)]
                        wk = [sbt(st, f"a_wk{b}_{i}", [128, KC, 128], BF16) for i in range(2)]
                        wv = [sbt(st, f"a_wv{b}_{i}", [128, KC, 128], BF16) for i in range(2)]
                        QT = sbt(st, f"a_QT{b}", [128, S], BF16)
                        KT = sbt(st, f"a_KT{b}", [128, S], BF16)
                        Vt = sbt(st, f"a_Vt{b}", [128, NB, 128], BF16)
                        lnb = sbt(st, f"a_ln{b}", [128, 2 * NT, 512], F32)
                        sq = [sbt(st, f"a_sq{b}_{i}", [128, 512], BF16) for i in range(2)]
                        pt = [sbt(st, f"a_pt{b}_{i}", [128, 512], BF16) for i in range(2)]
                        acc = sbt(st, f"a_acc{b}", [128, 2, S], F32)
                        rden = [sbt(st, f"a_rd{b}_{i}", [128, 512], F32) for i in range(2)]
                        yat = [sbt(st, f"a_yat{b}_{i}", [128, S], BF16) for i in range(2)]
                        bk = [pst(st, f"a_bk{b}_{i}") for i in range(8)]
                        ps_p = [bk[0], bk[1]]
                        ps_n = bk[7]
                        ps_v = bk[6]
                        ps_s = [(bk[0], bk[1]), (bk[2], bk[3])]
                        ps_o = [bk[4], bk[5]]
                        it = 0
                        for hp in range(4):
                            for g in range(3):
                                d = DIL[g]
                                L = S // d
                                nbl = L // 128
                                wi = it % 2
                                it += 1
                                for (wt, c0) in ((wq[wi], C_Q), (wk[wi], C_K), (wv[wi], C_V)):
                                    cc = c0 + g * 512 + hp * 128
                                    load_w(wt, w_in_d[:, cc:cc + 128], KC, 128, scale=g1)

                                def nat_view(T, j):
                                    if d == 1:
                                        return T[:, 512 * j:512 * (j + 1)]
                                    return T[:, :].rearrange("p (r m) -> p r m", r=d)[:, :, (512 * j) // d:(512 * (j + 1)) // d]

                                def nat_src(a):
                                    if d == 1:
                                        return a
                                    return a.rearrange("p (m r) -> p r m", r=d)

                                idx = 0
                                for (wt, T) in ((wq[wi], QT), (wk[wi], KT)):
                                    for j in range(NT):
                                        pp = ps_p[idx % 2]
                                        for kc in range(KC):
                                            P.mm(pp[:, 0:512], wt[:, kc, :], hT[:, kc, 512 * j:512 * (j + 1)], start=(kc == 0))
                                        P.act(sq[idx % 2][:], pp[:, 0:512], AF.Square)
                                        P.copy("dve", nat_view(T, j), nat_src(pp[:, 0:512]))
                                        P.mm(ps_n[:, 0:512], bdones, sq[idx % 2][:], start=True)
                                        P.act(lnb[:, idx, :], ps_n[:, 0:512], AF.Ln, scale=1.0 / 64, bias=EPS)
                                        idx += 1
                                idx = 0
                                for (T, gcol) in ((QT, PV_QG), (KT, PV_KG)):
                                    for j in range(NT):
                                        P.act(lnb[:, idx, :], lnb[:, idx, :], AF.Exp, scale=-0.5)
                                        P.stt("dve", nat_view(T, j), nat_view(T, j), pv[:, gcol:gcol + 1], nat_src(lnb[:, idx, :]), ALU.mult, ALU.mult)
                                        idx += 1
                                for bi in range(NB):
                                    r, n = bi // nbl, bi % nbl
                                    t0 = r + d * 128 * n
                                    for kc in range(KC):
                                        P.mm(ps_v[:, (bi % 4) * 128:(bi % 4 + 1) * 128], hT[:, kc, t0:t0 + d * 127 + 1:d], wv[wi][:, kc, :],
                                             start=(kc == 0 and bi % 4 == 0))
                                    if bi % 4 == 3:
                                        P.copy("act", Vt[:, bi - 3:bi + 1, :], ps_v[:, :].rearrange("p (k c) -> p k c", k=4))
                                def Sblk(bi):
                                    r, n = bi // nbl, bi % nbl
                                    c0 = r * L + 128 * n
                                    hp_ = n > 0
                                    ptb = pt[bi % 2]
                                    for h in range(2):
                                        pss = ps_s[bi % 2][h]
                                        rows = slice(64 * h, 64 * h + 64)
                                        if hp_:
                                            P.mm(pss[:, 0:128], KT[rows, c0 - 128:c0], QT[rows, c0:c0 + 128], start=True)
                                        P.mm(pss[:, 128:256], KT[rows, c0:c0 + 128], QT[rows, c0:c0 + 128], start=not hp_)
                                        lo = 0 if hp_ else 128
                                        P.act(ptb[:, h * 256 + lo:h * 256 + 256], pss[:, lo:256], AF.Exp, scale=0.125)
                                    if hp_:
                                        P.tt("pool", ptb[:, :], ptb[:, :], amask, ALU.mult)
                                    else:
                                        v3 = lambda a: a.rearrange("p (h c) -> p h c", h=2)[:, :, 128:256]
                                        P.tt("pool", v3(ptb[:, :]), v3(ptb[:, :]), v3(amask), ALU.mult)

                                def Vblk(bi):
                                    r, n = bi // nbl, bi % nbl
                                    hp_ = n > 0
                                    ptb = pt[bi % 2]
                                    pso = ps_o[bi % 2]
                                    for h in range(2):
                                        rows = slice(64 * h, 64 * h + 64)
                                        kw = {} if h == 0 else {"tile_position": (0, 64)}
                                        first = True
                                        for (lh, ncol) in ((None, 0), (ones, 128)):
                                            for pc in ((0, 1) if hp_ else (1,)):
                                                vb = bi - 1 if pc == 0 else bi
                                                lhsT = Vt[:, vb, 64 * h:64 * h + 64] if lh is None else ones[:, 0:64]
                                                P.mm(pso[rows, ncol:ncol + 128], lhsT, ptb[:, h * 256 + pc * 128:h * 256 + pc * 128 + 128],
                                                     start=first, **kw)
                                                first = False
                                    t0 = r + d * 128 * n
                                    av = acc[:, :, t0:t0 + d * 127 + 1:d]
                                    pv3 = pso[:, 0:256].rearrange("p (c q) -> p c q", c=2)
                                    if g == 0:
                                        P.copy("dve", av, pv3)
                                    else:
                                        P.tt("dve", av, pv3, av, ALU.add)

                                Sblk(0)
                                for bi in range(NB):
                                    if bi + 1 < NB:
                                        Sblk(bi + 1)
                                    Vblk(bi)
                            yb = yat[hp % 2]
                            for j in range(NT):
                                sl = slice(512 * j, 512 * (j + 1))
                                P.add("dve", lambda e, o=rden[j % 2][:], i=acc[:, 1, sl]: e.reciprocal(o, i), [acc[:, 1, sl]], [rden[j % 2][:]])
                                P.tt("pool", yb[:, sl], acc[:, 0, sl], rden[j % 2][:], ALU.mult)
                            P.dma(yat_d[b, hp, :, :], yb[:, :], f"yat{hp % 2}")
                P.barrier()
                if "ssd" in phases:
                    with ES() as st:
                        wdt = sbt(st, f"s_wdt{b}", [128, KC, 32], BF16)
                        dtb = sbt(st, f"s_dtb{b}", [128, 32], F32)
                        alog = sbt(st, f"s_alog{b}", [128, 32], F32)
                        xdt = sbt(st, f"s_xdt{b}", [128, NB, 32], F32)
                        tmpa = sbt(st, f"s_tmpa{b}", [128, NB, 32], F32)
                        dt_all = sbt(st, f"s_dt{b}", [128, NB, 32], F32)
                        a32 = sbt(st, f"s_a32{b}", [128, NB, 32], F32)
                        a_hi = sbt(st, f"s_ahi{b}", [128, NB, 32], BF16)
                        a_lo = sbt(st, f"s_alo{b}", [128, NB, 32], BF16)
                        wg = [sbt(st, f"s_wg{b}_{i}", [128, KC, 768], BF16) for i in range(2)]
                        raw = [sbt(st, f"s_raw{b}_{i}", [128, 515], F32) for i in range(4)]
                        accb = [sbt(st, f"s_acc{b}_{i}", [128, 512], F32) for i in range(4)]
                        thb = [sbt(st, f"s_th{b}_{i}", [128, 512], F32) for i in range(2)]
                        xbcT = [sbt(st, f"s_xbcT{b}_{i}", [128, 4, 512], BF16) for i in range(2)]
                        zs = [sbt(st, f"s_zs{b}_{i}", [128, 2, 512], BF16) for i in range(2)]
                        xbtok = [sbt(st, f"s_xbtok{b}_{i}", [128, 4, 384], BF16) for i in range(2)]
                        Xdt = [sbt(st, f"s_Xdt{b}_{i}", [128, 4, 256], BF16) for i in range(2)]
                        xsD = [sbt(st, f"s_xsD{b}_{i}", [128, 2, 512], F32) for i in range(2)]
                        Rh = [sbt(st, f"s_Rh{b}_{i}", [128, 512], BF16) for i in range(4)]
                        Rl = [sbt(st, f"s_Rl{b}_{i}", [128, 512], BF16) for i in range(4)]
                        Eb = [sbt(st, f"s_E{b}_{i}", [128, 512], BF16) for i in range(2)]
                        EAb = [sbt(st, f"s_EA{b}_{i}", [128, 512], BF16) for i in range(2)]
                        Mp = [sbt(st, f"s_Mp{b}_{i}", [128, 512], BF16) for i in range(2)]
                        Cs = [sbt(st, f"s_Cs{b}_{i}", [128, 512], BF16) for i in range(2)]
                        CBm = [sbt(st, f"s_CBm{b}_{i}", [128, 128], BF16) for i in range(2)]
                        Xd = [sbt(st, f"s_Xd{b}_{i}", [128, 256], BF16) for i in range(2)]
                        ddc = [sbt(st, f"s_dd{b}_{i}", [128, 12], F32) for i in range(2)]
                        Sst = sbt(st, f"s_S{b}", [128, 256], F32)
                        Sbf = [sbt(st, f"s_Sbf{b}_{i}", [128, 256], BF16) for i in range(2)]
                        ybuf = [sbt(st, f"s_yb{b}_{i}", [128, 2, 512], F32) for i in range(2)]
                        sqw = sbt(st, f"s_sq{b}", [128, 2, 512], BF16)
                        lnw = sbt(st, f"s_lnw{b}", [128, 512], F32)
                        yo = [sbt(st, f"s_yo{b}_{i}", [128, 2, 512], BF16) for i in range(2)]
                        ps_p = [pst(st, f"s_psp{b}_{i}") for i in range(2)]
                        ps_seg = pst(st, f"s_seg{b}")
                        ps_acs = pst(st, f"s_acs{b}")
                        ps_cb = pst(st, f"s_cb{b}")
                        ps_st = pst(st, f"s_st{b}")
                        ps_y = pst(st, f"s_y{b}")
                        ps_tr = pst(st, f"s_tr{b}", BF16)
                        load_w(wdt, w_in_d[:, C_DT:C_DT + 32], KC, 32, scale=g1)
                        P.dma(dtb[:], dtb_d[:, :].partition_broadcast(128), "dtb")
                        P.dma(alog[:], alog_d[:, :].partition_broadcast(128), "alog")
                        for half in range(NB // 16):
                            pp = ps_p[half % 2]
                            for bl in range(16):
                                blk = half * 16 + bl
                                for kc in range(KC):
                                    P.mm(pp[:, bl * 32:(bl + 1) * 32], hT[:, kc, blk * 128:(blk + 1) * 128], wdt[:, kc, :],
                                         start=(bl == 0 and kc == 0))
                            P.tt("dve", xdt[:, half * 16:(half + 1) * 16, :], pp[:, :].rearrange("p (k h) -> p k h", k=16),
                                 dtb[:, :].unsqueeze(1).broadcast_to([128, 16, 32]), ALU.add)
                        P.act(tmpa[:], xdt[:], AF.Abs)
                        P.act(tmpa[:], tmpa[:], AF.Exp, scale=-1.0)
                        P.act(alog[:], alog[:], AF.Exp)
                        P.act(a32[:], tmpa[:], AF.Ln, bias=1.0)
                        P.stt("dve", dt_all[:], xdt[:], 0.0, a32[:], ALU.max, ALU.add)
                        P.tt("dve", a32[:], dt_all[:], alog[:, :].unsqueeze(1).broadcast_to([128, NB, 32]), ALU.mult)
                        P.ts("dve", a32[:], a32[:], -1.0, None, ALU.mult)
                        P.copy("dve", a_hi[:], a32[:])
                        P.tt("dve", tmpa[:], a32[:], a_hi[:], ALU.subtract)
                        P.copy("dve", a_lo[:], tmpa[:])
                        cwh = lambda i, c: pvh[:, PV_CW + i * 32 + c:PV_CW + i * 32 + c + 1]
                        cbh = lambda c: pvh[:, PV_CB + c:PV_CB + c + 1]
                        cnt = [0]
                        v4 = lambda a: a.rearrange("p (k l) -> p k l", k=4)

                        def load_group(g):
                            wgi = wg[g % 2]
                            load_w(wgi[:, :, 0:256], w_in_d[:, C_Z + g * 256:C_Z + (g + 1) * 256], KC, 256, scale=g1h)
                            load_w(wgi[:, :, 256:512], w_in_d[:, C_X + g * 256:C_X + (g + 1) * 256], KC, 256, scale=g1)
                            load_w(wgi[:, :, 512:640], w_in_d[:, C_B + g * 128:C_B + (g + 1) * 128], KC, 128, scale=g1)
                            load_w(wgi[:, :, 640:768], w_in_d[:, C_C + g * 128:C_C + (g + 1) * 128], KC, 128, scale=g1)

                        pc_pp = {}

                        def PCm(W, i):
                            g, w = divmod(W, NT)
                            wgi = wg[g % 2]
                            tsl = slice(512 * w, 512 * (w + 1))
                            pp = ps_p[cnt[0] % 2]
                            cnt[0] += 1
                            pc_pp[(W, i)] = pp
                            c0 = i * 128 if i < 2 else 256 + (i - 2) * 128
                            for kc in range(KC):
                                P.mm(pp[:, 0:512], wgi[:, kc, c0:c0 + 128], hT[:, kc, tsl], start=(kc == 0))

                        def PCe(W, i):
                            g, w = divmod(W, NT)
                            wi = W % 2
                            pp = pc_pp[(W, i)]
                            if i < 2:
                                P.act(thb[i][:], pp[:, 0:512], AF.Tanh)
                                P.stt("dve", zs[wi][:, i, :], thb[i][:], 1.0, pp[:, 0:512], ALU.add, ALU.mult)
                                return
                            ct = i - 2
                            cchunk = (2 * g, 2 * g + 1, 16 + g, 24 + g)
                            if w == 0:
                                P.memset("pool", raw[ct][:, 0:3], 0.0)
                            cc = cchunk[ct]
                            ab = accb[ct]
                            P.copy("act", raw[ct][:, 3:515], pp[:, 0:512])
                            P.act(ab[:], pp[:, 0:512], AF.Identity, scale=cwh(3, cc), bias=cbh(cc))
                            for k_ in range(3):
                                P.stt("dve", ab[:], raw[ct][:, k_:k_ + 512], cwh(k_, cc), ab[:], ALU.mult, ALU.add)
                            P.copy("pool", raw[ct][:, 0:3], raw[ct][:, 512:515])

                        def PCf(W, i):
                            wi = W % 2
                            pc_pp.pop((W, i))
                            if i < 2:
                                return
                            ct = i - 2
                            ab = accb[ct]
                            tb = thb[ct % 2]
                            P.act(tb[:], ab[:], AF.Tanh)
                            P.stt("dve", xbcT[wi][:, ct, :], tb[:], 1.0, ab[:], ALU.add, ALU.mult)

                        def PCt(W):
                            g, w = divmod(W, NT)
                            wi = W % 2
                            for bp in range(2):
                                for k2 in range(2):
                                    blk = 2 * bp + k2
                                    for (ci_, ct) in enumerate((0, 1, 2)):
                                        P.tr(ps_tr[:, k2 * 384 + ci_ * 128:k2 * 384 + (ci_ + 1) * 128],
                                             xbcT[wi][:, ct, blk * 128:(blk + 1) * 128], ident)
                                P.copy("act", xbtok[wi][:, 2 * bp:2 * bp + 2, :], ps_tr[:, 0:768].rearrange("p (k c) -> p k c", k=2))
                            P.tt("dve", Xdt[wi][:, :, :].rearrange("p b (k q) -> p b k q", k=4),
                                 xbtok[wi][:, :, 0:256].rearrange("p b (k q) -> p b k q", k=4),
                                 dt_all[:, 4 * w:4 * w + 4, 4 * g:4 * g + 4].unsqueeze(3).broadcast_to([128, 4, 4, 64]), ALU.mult)
                            for hc in range(2):
                                P.act(xsD[wi][:, hc, :], xbcT[wi][:, hc, :], AF.Copy, scale=pv[:, PV_D + 2 * g + hc:PV_D + 2 * g + hc + 1])

                        def PC(W):
                            for i in range(6):
                                PCm(W, i)
                                PCe(W, i)
                                PCf(W, i)
                            PCt(W)

                        def RG(n):
                            g, cidx = divmod(n, NB)
                            for (R_, asrc) in ((Rh[n % 4], a_hi), (Rl[n % 4], a_lo)):
                                P.tt("pool", v4(R_[:, :]), maskU.unsqueeze(1).broadcast_to([128, 4, 128]),
                                     asrc[:, cidx, 4 * g:4 * g + 4].unsqueeze(2).broadcast_to([128, 4, 128]), ALU.mult)

                        def A(n):
                            g, cidx = divmod(n, NB)
                            W = n // 4
                            wi = W % 2
                            c4 = cidx % 4
                            ci = n % 2
                            tc = slice(128 * c4, 128 * (c4 + 1))
                            BT = xbcT[wi][:, 2, tc]
                            CT = xbcT[wi][:, 3, tc]
                            P.mm(ps_seg[:, 0:512], maskGT, Rh[n % 4][:, :], start=True)
                            P.mm(ps_seg[:, 0:512], maskGT, Rl[n % 4][:, :], start=False)
                            P.mm(ps_acs[:, 0:512], ones, Rh[n % 4][:, :], start=True)
                            P.mm(ps_acs[:, 0:512], ones, Rl[n % 4][:, :], start=False)
                            P.mm(ps_cb[:, 0:128], BT, CT, start=True)
                            dd = ddc[ci]
                            P.act(Eb[ci][:, :], ps_seg[:, 0:512], AF.Exp)
                            P.act(dd[:, 0:4], ps_seg[:, 127:512:128], AF.Exp)
                            P.act(EAb[ci][:, :], ps_acs[:, 0:512], AF.Exp)
                            P.act(dd[:, 4:8], ps_acs[:, 127:512:128], AF.Exp)
                            P.tt("dve", CBm[ci][:, :], ps_cb[:, 0:128], maskU, ALU.mult)
                            P.tt("dve", v4(Mp[ci][:, :]), v4(Eb[ci][:, :]), CBm[ci][:, :].unsqueeze(1).broadcast_to([128, 4, 128]), ALU.mult)
                            P.tt("pool", v4(Cs[ci][:, :]), CT.unsqueeze(1).broadcast_to([128, 4, 128]), v4(EAb[ci][:, :]), ALU.mult)
                            P.tt("dve", Xd[ci][:, :].rearrange("p (k q) -> p k q", k=4),
                                 Xdt[wi][:, c4, :].rearrange("p (k q) -> p k q", k=4),
                                 dd[:, 0:4].unsqueeze(2).broadcast_to([128, 4, 64]), ALU.mult)

                        def Bk(n):
                            g, cidx = divmod(n, NB)
                            W = n // 4
                            wi = W % 2
                            c4 = cidx % 4
                            ci = n % 2
                            tc = slice(128 * c4, 128 * (c4 + 1))
                            dd = ddc[ci]
                            sb_old = Sbf[n % 2]
                            sb_new = Sbf[(n + 1) % 2]
                            firstq = [True, True]
                            for hc in range(2):
                                for hh in range(2):
                                    k = 2 * hc + hh
                                    kw = {} if hh == 0 else {"tile_position": (0, 64)}
                                    o = ps_y[64 * hh:64 * hh + 64, hc * 128:(hc + 1) * 128]
                                    P.mm(o, Xdt[wi][:, c4, k * 64:(k + 1) * 64], v4(Mp[ci][:, :])[:, k, :], start=firstq[hh], **kw)
                                    firstq[hh] = False
                                    if cidx > 0:
                                        P.mm(o, sb_old[:, k * 64:(k + 1) * 64], v4(Cs[ci][:, :])[:, k, :], start=False, **kw)
                            P.mm(ps_st[:, 0:256], xbtok[wi][:, c4, 256:384], Xd[ci][:, :], start=True)
                            if cidx == 0:
                                P.copy("dve", Sst[:, :], ps_st[:, 0:256])
                            else:
                                s3 = Sst[:, :].rearrange("p (k q) -> p k q", k=4)
                                P.tt("dve", s3, s3, dd[:, 4:8].unsqueeze(2).broadcast_to([128, 4, 64]), ALU.mult)
                                P.tt("dve", Sst[:, :], Sst[:, :], ps_st[:, 0:256], ALU.add)
                            P.copy("act", sb_new[:, :], Sst[:, :])
                            P.tt("dve", ybuf[wi][:, :, tc], ps_y[:, 0:256].rearrange("p (c q) -> p c q", c=2), xsD[wi][:, :, tc], ALU.add)

                        def WN(W):
                            g, w = divmod(W, NT)
                            wi = W % 2
                            tsl = slice(512 * w, 512 * (w + 1))
                            P.tt("dve", ybuf[wi][:], ybuf[wi][:], zs[wi][:], ALU.mult)
                            P.act(sqw[:], ybuf[wi][:], AF.Square)
                            pp = ps_p[cnt[0] % 2]
                            cnt[0] += 1
                            P.mm(pp[:, 0:512], ones, sqw[:, 0, :], start=True)
                            P.mm(pp[:, 0:512], ones, sqw[:, 1, :], start=False)
                            P.act(lnw[:], pp[:, 0:512], AF.Ln, scale=1.0 / 256, bias=EPS)
                            P.act(lnw[:], lnw[:], AF.Exp, scale=-0.5)
                            P.tt("dve", yo[wi][:], ybuf[wi][:], lnw[:, :].unsqueeze(1).broadcast_to([128, 2, 512]), ALU.mult)
                            P.dma(ysd_d[b, 2 * g:2 * g + 2, :, tsl].rearrange("c p t -> p c t"), yo[wi][:], f"yo{wi}")

                        NCH = 8 * NB
                        NW = 8 * NT
                        load_group(0)
                        PC(0)
                        RG(0)
                        RG(1)
                        for n in range(NCH):
                            g, cidx = divmod(n, NB)
                            W = n // 4
                            c4 = n % 4
                            if n + 2 < NCH:
                                RG(n + 2)
                            A(n)
                            if n > 0:
                                Bk(n - 1)
                                if c4 == 0:
                                    WN(W - 1)
                            if cidx == 12 and g + 1 < 8:
                                load_group(g + 1)
                            if W + 1 < NW:
                                order = (2, 3, 4, 5, 0, 1)
                                o = order
                                if c4 == 0:
                                    PCm(W + 1, o[0]); PCm(W + 1, o[1])
                                elif c4 == 1:
                                    PCe(W + 1, o[0]); PCm(W + 1, o[2]); PCe(W + 1, o[1]); PCm(W + 1, o[3]); PCf(W + 1, o[0])
                                elif c4 == 2:
                                    PCe(W + 1, o[2]); PCf(W + 1, o[1]); PCm(W + 1, o[4]); PCe(W + 1, o[3]); PCf(W + 1, o[2])
                                    PCm(W + 1, o[5]); PCf(W + 1, o[3])
                                    PCt(W + 1)
                                else:
                                    PCe(W + 1, o[4]); PCe(W + 1, o[5]); PCf(W + 1, o[4]); PCf(W + 1, o[5])
                        Bk(NCH - 1)
                        WN(NW - 1)
            P.barrier()

        tiles = [(b, j) for b in range(NS) for j in range(NT)]
        if "merge" in phases:
            with ES() as st:
                Wssd = sbt(st, "m_wssd", [128, 16, D], BF16)
                Watt = sbt(st, "m_watt", [128, 4, D], BF16)
                Wgs = sbt(st, "m_wgs", [128, KC, D], BF16)
                Wga = sbt(st, "m_wga", [128, KC, D], BF16)
                Wout = sbt(st, "m_wout", [128, KC, D], BF16)
                ys = [sbt(st, f"m_ys{i}", [128, 16, 512], BF16) for i in range(2)]
                ya = [sbt(st, f"m_ya{i}", [128, 4, 512], BF16) for i in range(2)]
                hTt = [sbt(st, f"m_ht{i}", [128, KC, 512], BF16) for i in range(2)]
                xt = [sbt(st, f"m_xt{i}", [128, 4, D], F32) for i in range(2)]
                th = [sbt(st, f"m_th{i}", [128, 512], F32) for i in range(2)]
                m1 = [sbt(st, f"m_m1{i}", [128, 512], F32) for i in range(3)]
                mT = sbt(st, "m_mT", [128, KC, 512], BF16)
                psm = [pst(st, f"m_ps{i}") for i in range(8)]
                pcm = [0]

                def m_loads(t):
                    b, j = tiles[t]
                    i2 = t % 2
                    tsl = slice(512 * j, 512 * (j + 1))
                    P.dma(hTt[i2][:], hTs_d[b, :, :, tsl].rearrange("c p t -> p c t"), f"mht{i2}")
                    P.dma(ys[i2][:], ysd_d[b, :, :, tsl].rearrange("c p t -> p c t"), f"mys{i2}")
                    P.dma(ya[i2][:], yat_d[b, :, :, tsl].rearrange("c p t -> p c t"), f"mya{i2}")
                    P.dma(xt[i2][:], x_d[b, tsl, :].rearrange("(k p) d -> p k d", p=128), f"mxt{i2}")

                def m_compute(t):
                    b, j = tiles[t]
                    i2 = t % 2
                    tsl = slice(512 * j, 512 * (j + 1))
                    for oc in range(KC):
                        osl = slice(oc * 128, (oc + 1) * 128)
                        pc = pcm[0]
                        p1, pg1, p2, pg2 = psm[pc % 8], psm[(pc + 1) % 8], psm[(pc + 2) % 8], psm[(pc + 3) % 8]
                        pcm[0] += 4
                        e2 = 0
                        for kc in range(KC):
                            P.mm(pg1[:, 0:512], Wgs[:, kc, osl], hTt[i2][:, kc, :], start=(kc == 0))
                        for kc in range(16):
                            P.mm(p1[:, 0:512], Wssd[:, kc, osl], ys[i2][:, kc, :], start=(kc == 0))
                        for kc in range(KC):
                            P.mm(pg2[:, 0:512], Wga[:, kc, osl], hTt[i2][:, kc, :], start=(kc == 0))
                        for kc in range(4):
                            P.mm(p2[:, 0:512], Watt[:, kc, osl], ya[i2][:, kc, :], start=(kc == 0))
                        P.act(th[e2][:], pg1[:, 0:512], AF.Tanh)
                        P.stt("dve", m1[e2][:], th[e2][:], 1.0, p1[:, 0:512], ALU.add, ALU.mult)
                        P.act(th[e2 + 1][:], pg2[:, 0:512], AF.Tanh)
                        P.stt("dve", m1[e2 + 1][:], th[e2 + 1][:], 1.0, p2[:, 0:512], ALU.add, ALU.mult)
                        P.tt("pool", mT[:, oc, :], m1[e2][:], m1[e2 + 1][:], ALU.add)
                    for blk in range(4):
                        for half in range(2):
                            pp = psm[pcm[0] % 8]
                            pcm[0] += 1
                            hs = slice(half * 512, (half + 1) * 512)
                            for kc in range(KC):
                                P.mm(pp[:, 0:512], mT[:, kc, blk * 128:(blk + 1) * 128], Wout[:, kc, hs], start=(kc == 0))
                            P.tt("dve", xt[i2][:, blk, hs], xt[i2][:, blk, hs], pp[:, 0:512], ALU.add)
                    P.dma(out_d[b, tsl, :].rearrange("(k p) d -> p k d", p=128), xt[i2][:], f"mo{i2}")

                m_loads(0)
                load_w(Wgs, w_in_d[:, C_GS:C_GS + D], KC, D, scale=g1h)
                load_w(Wssd, w_ssd_d, 16, D, scale=pv[:, PV_GS:PV_GS + 16])
                load_w(Wga, w_in_d[:, C_GA:C_GA + D], KC, D, scale=g1h)
                load_w(Watt, w_att_d, 4, D)
                load_w(Wout, w_out_d, KC, D, scale=0.5)
                for t in range(len(tiles)):
                    if t + 1 < len(tiles):
                        m_loads(t + 1)
                    m_compute(t)
            P.barrier()

        if "ffn" in phases:
            with ES() as st:
                Wup = sbt(st, "f_wup", [128, KC, 2 * DFF], BF16)
                Wdn = sbt(st, "f_wdn", [128, NFC, D], BF16)
                xb = [sbt(st, f"f_xb{i}", [128, D], F32) for i in range(2)]
                h16 = sbt(st, "f_h16", [128, 1, D], BF16)
                ssq = [sbt(st, f"f_ss{i}", [128, 1], F32) for i in range(2)]
                lnv = [sbt(st, f"f_ln{i}", [128, 1], F32) for i in range(2)]
                rstd = [sbt(st, f"f_rs{i}", [128, 1], F32) for i in range(2)]
                h2T = [sbt(st, f"f_h2T{i}", [128, KC, 512], BF16) for i in range(2)]
                aT = sbt(st, "f_aT", [128, NFC, 512], BF16)
                tail = sbt(st, "f_tail", [128, 2 * NFC, 2], F32)
                rawf = [sbt(st, f"f_raw{i}", [128, 514], F32) for i in range(2)]
                accf = [sbt(st, f"f_acc{i}", [128, 512], F32) for i in range(4)]
                ostg = [sbt(st, f"f_os{i}", [128, 512], F32) for i in range(2)]
                pbf = [pst(st, f"f_pb{i}", BF16) for i in range(2)]
                psf = [pst(st, f"f_ps{i}") for i in range(6)]
                pcf = [0]
                blkc = [0]
                def fwc(i, ci):
                    src = pvh if ci < NFC else pv
                    return src[:, PV_FW + i * 44 + ci:PV_FW + i * 44 + ci + 1]

                def fbc(ci):
                    src = pvh if ci < NFC else pv
                    return src[:, PV_FB + ci:PV_FB + ci + 1]

                def f_norm(t):
                    b, j = tiles[t]
                    hb = h2T[t % 2]
                    for k in range(4):
                        q = blkc[0] % 2
                        blkc[0] += 1
                        r0 = 512 * j + 128 * k
                        P.dma(xb[q][:], out_d[b, r0:r0 + 128, :], f"fxb{q}")
                        P.memset("pool", ssq[q][:], 0.0)
                        P.act(h16[:, 0, :], xb[q][:], AF.Square, accum_out=ssq[q][:, 0:1])
                        P.act(lnv[q][:], ssq[q][:], AF.Ln, scale=1.0 / D, bias=EPS)
                        P.act(rstd[q][:], lnv[q][:], AF.Exp, scale=-0.5)
                        P.scale("dve" if k % 2 == 0 else "act", h16[:, 0, :], xb[q][:], rstd[q][:, 0:1])
                        bank = pbf[q]
                        for kc in range(KC):
                            P.tr(bank[:, kc * 128:(kc + 1) * 128], h16[:, 0, kc * 128:(kc + 1) * 128], ident)
                        P.copy("act" if k % 2 == 0 else "dve", hb[:, :, k * 128:(k + 1) * 128], bank[:, :].rearrange("p (c t) -> p c t", c=KC))

                def f_up(t):
                    b, j = tiles[t]
                    hb = h2T[t % 2]
                    if j == 0:
                        P.memset("pool", tail[:], 0.0)

                    def up1(c):
                        res = []
                        for half in range(2):
                            ci = half * NFC + c
                            pp = psf[pcf[0] % 6]
                            pcf[0] += 1
                            col = half * DFF + c * 128
                            for kc in range(KC):
                                P.mm(pp[:, 0:512], Wup[:, kc, col:col + 128], hb[:, kc, :], start=(kc == 0))
                            rw = rawf[half]
                            ab = accf[(2 * c + half) % 4]
                            P.copy("pool", rw[:, 0:2], tail[:, ci, :])
                            P.copy("act", rw[:, 2:514], pp[:, 0:512])
                            P.act(ab[:], pp[:, 0:512], AF.Identity, scale=fwc(2, ci), bias=fbc(ci))
                            P.copy("pool", tail[:, ci, :], rw[:, 512:514])
                            for i in range(2):
                                P.stt("dve", ab[:], rw[:, i:i + 512], fwc(i, ci), ab[:], ALU.mult, ALU.add)
                            res.append(ab)
                        return res

                    def up2(c, res):
                        tf = rawf[0][:, 2:514]
                        P.act(tf, res[0][:], AF.Tanh)
                        P.stt("dve", res[0][:], tf, 1.0, res[0][:], ALU.add, ALU.mult)
                        P.tt("pool", aT[:, c, :], res[0][:], res[1][:], ALU.mult)

                    for c in range(NFC):
                        r_ = up1(c)
                        up2(c, r_)

                def f_down(t):
                    b, j = tiles[t]
                    for blk in range(4):
                        r0 = 512 * j + 128 * blk
                        for half in range(2):
                            pp = psf[pcf[0] % 6]
                            q = pcf[0] % 2
                            pcf[0] += 1
                            hs = slice(half * 512, (half + 1) * 512)
                            for c in range(NFC):
                                P.mm(pp[:, 0:512], aT[:, c, blk * 128:(blk + 1) * 128], Wdn[:, c, hs], start=(c == 0))
                            P.copy("act", ostg[q][:], pp[:, 0:512])
                            P.dma(out_d[b, r0:r0 + 128, hs], ostg[q][:], f"fo{q}", eng="pool", accum_op=ALU.add)

                f_norm(0)
                load_w(Wup, w_up_d, KC, 2 * DFF, scale=pv[:, PV_G2:PV_G2 + 8])
                load_w(Wdn, w_dn_d, NFC, D)
                nt_ = len(tiles)
                f_up(0)
                for t in range(nt_):
                    if t + 1 < nt_:
                        f_norm(t + 1)
                    f_down(t)
                    if t + 1 < nt_:
                        f_up(t + 1)
        P.barrier(final=True)
        P.emit()
    return nc, P


_CACHE = {}


def _core_inputs(inp, c, NS):
    x = np.ascontiguousarray(np.asarray(inp["x"], dtype=np.float32)[c * NS:(c + 1) * NS])
    return x


def kernel(**inputs):
    NCORES = 8
    NS = 2
    if "nc" not in _CACHE:
        _CACHE["nc"] = build(NS=NS)[0]
    nc = _CACHE["nc"]
    f = lambda k: np.ascontiguousarray(np.asarray(inputs[k], dtype=np.float32)[0])
    shared = {
        "w_in": f("w_in"), "w_ssd_proj": f("w_ssd_proj"), "w_attn_proj": f("w_attn_proj"), "w_out": f("w_out"),
        "w_up": f("w_up"), "w_down": f("w_down"), "pvec": pack_pvec(inputs), "cmask": make_cmask(),
        "dt_bias": np.asarray(inputs["dt_bias"], np.float32).reshape(1, 32),
        "a_log": np.asarray(inputs["a_log"], np.float32).reshape(1, 32),
    }
    in_maps = []
    for c in range(NCORES):
        m = dict(shared)
        m["x"] = _core_inputs(inputs, c, NS)
        in_maps.append(m)
    res = run_bass_kernel_spmd(nc, in_maps, core_ids=list(range(NCORES)))
    out = np.concatenate([np.asarray(r["out"]) for r in res.results], axis=0)
    return out.astype(np.float32)
```

```python
import contextlib
import numpy as np
import ml_dtypes
import concourse.bass as bass
import concourse.mybir as mybir
from concourse.bass_utils import run_bass_kernel_spmd

F32 = mybir.dt.float32
BF16 = mybir.dt.bfloat16
AF = mybir.ActivationFunctionType
ALU = mybir.AluOpType

ENGS = ("pe", "act", "dve", "pool", "sp")
SMALL_N = 256


def region(ap):
    t = ap.tensor
    dims = ap.ap
    off = int(ap.offset)
    name = t.name
    cls = type(t).__name__
    if cls.startswith("DRam"):
        lo = off
        hi = off + sum((c - 1) * abs(s) for s, c in dims) + 1
        return (name, 0, 1, lo, hi)
    if cls.startswith("PSum"):
        return (name, 0, 128, 0, 1 << 30)
    ps, npart = dims[0]
    p0 = off // ps
    f0 = off % ps
    f1 = f0 + sum((c - 1) * abs(s) for s, c in dims[1:]) + 1
    return (name, p0, p0 + npart, f0, f1)


class Op:
    __slots__ = ("eng", "fn", "idx", "deps", "signaled", "dma_slot", "dma_cnt", "nfree", "sig_no")

    def __init__(self, eng, fn):
        self.eng = eng
        self.fn = fn
        self.deps = {}
        self.signaled = False
        self.dma_slot = None
        self.dma_cnt = 0
        self.nfree = 1 << 30
        self.sig_no = 0


class Prog:
    def __init__(self, nc):
        self.nc = nc
        self.ops = []
        self.recs = {}
        self.slot_last = {}
        self.slot_cnt = {}
        self.barrier_idx = {}

    def _track(self, op, reads, writes):
        deps = op.deps
        ekey = op.eng if op.dma_slot is None else ("dma", op.dma_slot)

        def add_dep(r):
            k = r[5]
            if k == ekey and op.dma_slot is None:
                if k == "pe":
                    return
            if deps.get(k, -1) < r[6]:
                deps[k] = r[6]

        psum_reads = [ap for ap in reads if type(ap.tensor).__name__.startswith("PSum")]
        if psum_reads:
            reads = [ap for ap in reads if not type(ap.tensor).__name__.startswith("PSum")]
            writes = list(writes) + psum_reads
        for ap in reads:
            name, p0, p1, f0, f1 = region(ap)
            lst = self.recs.setdefault(name, [])
            found = None
            for r in lst:
                if r[4] == "w":
                    if r[0] < p1 and p0 < r[1] and r[2] < f1 and f0 < r[3]:
                        add_dep(r)
                elif r[5] == ekey and r[0] == p0 and r[1] == p1 and r[2] == f0 and r[3] == f1:
                    found = r
            if found is not None:
                found[6] = op.idx
            else:
                lst.append([p0, p1, f0, f1, "r", ekey, op.idx, op.nfree])
        for ap in writes:
            name, p0, p1, f0, f1 = region(ap)
            lst = self.recs.setdefault(name, [])
            keep = []
            for r in lst:
                if r[0] < p1 and p0 < r[1] and r[2] < f1 and f0 < r[3]:
                    if r[6] != op.idx:
                        add_dep(r)
                    if r[0] >= p0 and r[1] <= p1 and r[2] >= f0 and r[3] <= f1 and r[6] != op.idx:
                        continue
                keep.append(r)
            keep.append([p0, p1, f0, f1, "w", ekey, op.idx, op.nfree])
            self.recs[name] = keep

    def add(self, eng, fn, reads=(), writes=(), nfree=None):
        op = Op(eng, fn)
        op.idx = len(self.ops)
        if nfree is None:
            nfree = 1 << 30
            for ap in writes:
                d = ap.ap
                n = 1
                for s, c in d[1:]:
                    n *= c
                nfree = min(nfree, n)
        op.nfree = nfree
        self.ops.append(op)
        self._track(op, reads, writes)
        return op

    def dma(self, out, in_, slot, eng="sp", **kw):
        def fn(e, out=out, in_=in_, kw=kw):
            return e.dma_start(out=out, in_=in_, **kw)

        op = Op(eng, fn)
        op.idx = len(self.ops)
        op.dma_slot = slot
        self.slot_cnt[slot] = self.slot_cnt.get(slot, 0) + 16
        op.dma_cnt = self.slot_cnt[slot]
        self.ops.append(op)
        prev = self.slot_last.get(slot)
        if prev is not None:
            op.deps[("dma", slot)] = prev
        self.slot_last[slot] = op.idx
        self._track(op, [in_], [out])
        return op

    def mm(self, out, lhsT, rhs, start=True, stop=True, **kw):
        if not start:
            kw.setdefault("skip_group_check", True)
        return self.add("pe", lambda e: e.matmul(out, lhsT, rhs, start=start, stop=stop, **kw),
                        [lhsT, rhs], [out])

    def tr(self, out, in_, ident):
        return self.add("pe", lambda e: e.transpose(out, in_, ident), [in_, ident], [out])

    def act(self, out, in_, func, bias=None, scale=None, accum_out=None, eng="act"):
        reads = [in_]
        kw = {}
        if bias is not None:
            kw["bias"] = bias
            if not isinstance(bias, (int, float)):
                reads.append(bias)
        if scale is not None:
            kw["scale"] = scale
            if not isinstance(scale, (int, float)):
                reads.append(scale)
        writes = [out]
        if accum_out is not None:
            kw["accum_out"] = accum_out
            writes.append(accum_out)
        return self.add(eng, lambda e: e.activation(out, in_, func, **kw), reads, writes)

    def tt(self, eng, out, in0, in1, op):
        return self.add(eng, lambda e: e.tensor_tensor(out, in0, in1, op), [in0, in1], [out])

    def ts(self, eng, out, in0, s1, s2, op0, op1=None, accum_out=None):
        reads = [in0]
        if not isinstance(s1, (int, float)):
            reads.append(s1)
        if s2 is not None and not isinstance(s2, (int, float)):
            reads.append(s2)
        kw = {}
        writes = [out]
        if accum_out is not None:
            kw["accum_out"] = accum_out
            writes.append(accum_out)
        if op1 is None:
            return self.add(eng, lambda e: e.tensor_scalar(out, in0, s1, None, op0, **kw), reads, writes)
        return self.add(eng, lambda e: e.tensor_scalar(out, in0, s1, s2, op0, op1, **kw), reads, writes)

    def stt(self, eng, out, in0, scalar, in1, op0, op1):
        reads = [in0, in1]
        if not isinstance(scalar, (int, float)):
            reads.append(scalar)
        return self.add(eng, lambda e: e.scalar_tensor_tensor(out, in0, scalar, in1, op0, op1), reads, [out])

    def scale(self, eng, out, in_, sc):
        if eng == "act":
            return self.act(out, in_, AF.Copy, scale=sc)
        return self.ts(eng, out, in_, sc, None, ALU.mult)

    def copy(self, eng, out, in_):
        if eng == "act":
            return self.add(eng, lambda e: e.copy(out, in_), [in_], [out])
        return self.add(eng, lambda e: e.tensor_copy(out, in_), [in_], [out])

    def memset(self, eng, ap, val):
        return self.add(eng, lambda e: e.memset(ap, val), [], [ap])

    def barrier(self, final=False):
        op = Op("sp", None)
        op.idx = len(self.ops)
        self.ops.append(op)
        for slot, i in self.slot_last.items():
            op.deps[("dma", slot)] = i
        if not final:
            m = Op("marker", None)
            m.idx = len(self.ops)
            self.ops.append(m)
        self.recs = {}

    def emit(self):
        nc = self.nc
        ops = self.ops
        for op in ops:
            for k, i in op.deps.items():
                ops[i].signaled = True
        cnt = {e: 0 for e in ENGS}
        for op in ops:
            if op.eng == "marker":
                continue
            if op.dma_slot is None and op.signaled:
                cnt[op.eng] += 1
                op.sig_no = cnt[op.eng]
        slots = sorted(self.slot_cnt.keys())
        import contextlib

        with contextlib.ExitStack() as st:
            esem = {e: st.enter_context(nc.semaphore("s_" + e)) for e in ENGS}
            ssem = {s: st.enter_context(nc.semaphore("d_" + s)) for s in slots}
            segs = [[]]
            for op in ops:
                if op.eng == "marker":
                    segs.append([])
                else:
                    segs[-1].append(op)
            stats = {e: [0, 0] for e in ENGS}
            waited_all = {e: {} for e in ENGS}

            def run(eng_name, e, seg):
                waited = waited_all[eng_name]
                for op in seg:
                    if op.eng != eng_name:
                        continue
                    for k, i in op.deps.items():
                        p = ops[i]
                        if isinstance(k, tuple):
                            sem, val = ssem[k[1]], p.dma_cnt
                        else:
                            sem, val = esem[k], p.sig_no
                        if waited.get(k, 0) >= val:
                            continue
                        waited[k] = val
                        e.wait_ge(sem, val)
                        stats[eng_name][1] += 1
                    if op.fn is None:
                        continue
                    ins = op.fn(e)
                    stats[eng_name][0] += 1
                    if op.dma_slot is not None:
                        ins.then_inc(ssem[op.dma_slot], 16)
                    elif op.signaled:
                        ins.then_inc(esem[eng_name], 1)

            for seg in segs:
                with nc.Block() as block:
                    @block.tensor
                    def _(e, seg=seg):
                        run("pe", e, seg)

                    @block.scalar
                    def _(e, seg=seg):
                        run("act", e, seg)

                    @block.vector
                    def _(e, seg=seg):
                        run("dve", e, seg)

                    @block.gpsimd
                    def _(e, seg=seg):
                        run("pool", e, seg)

                    @block.sync
                    def _(e, seg=seg):
                        run("sp", e, seg)

            self.stats = stats
            self.sig_counts = cnt


D = 1024
KC = 8
S_FULL = 4096
C_Z, C_X, C_B, C_C, C_DT, C_Q, C_K, C_V, C_GS, C_GA = 0, 2048, 4096, 5120, 6144, 6176, 7712, 9248, 10784, 11808
DIL = (1, 4, 16)
EPS = 1e-6
DFF = 2816
NFC = 22
PV_G1, PV_G2, PV_GS, PV_CW, PV_CB, PV_D, PV_QG, PV_KG, PV_FW, PV_FB, NPV = 0, 8, 16, 32, 160, 192, 208, 209, 210, 342, 386
CM_ID, CM_U, CM_L, CM_GT, CM_ONE, CM_AM, CM_BD, CM_NEG, NCM = 0, 128, 256, 384, 512, 640, 1152, 1280, 1536

_bf = ml_dtypes.bfloat16


def _fm(v):
    return np.ascontiguousarray(np.asarray(v).reshape(-1, 128).T)


def pack_pvec(inp):
    pv = np.zeros((128, NPV), np.float32)
    pv[:, PV_G1:PV_G1 + 8] = _fm(inp["norm1_g"][0])
    pv[:, PV_G2:PV_G2 + 8] = _fm(inp["norm2_g"][0])
    pv[:, PV_GS:PV_GS + 16] = _fm(inp["ssd_norm_g"][0])
    for i in range(4):
        pv[:, PV_CW + i * 32:PV_CW + (i + 1) * 32] = _fm(inp["ssd_conv_w"][0, i])
    pv[:, PV_CB:PV_CB + 32] = _fm(inp["ssd_conv_b"][0])
    pv[:, PV_D:PV_D + 16] = _fm(np.repeat(inp["d_skip"][0], 64))
    pv[:, PV_QG] = np.tile(inp["q_norm_g"][0], 2)
    pv[:, PV_KG] = np.tile(inp["k_norm_g"][0], 2)
    for i in range(3):
        pv[:, PV_FW + i * 44:PV_FW + (i + 1) * 44] = _fm(inp["ffn_conv_w"][0, i])
    pv[:, PV_FB:PV_FB + 44] = _fm(inp["ffn_conv_b"][0])
    return pv


def make_cmask():
    cm = np.zeros((128, NCM), np.float32)
    i = np.arange(128)
    U = (i[:, None] <= i[None, :]).astype(np.float32)
    L = (i[:, None] >= i[None, :]).astype(np.float32)
    GT = (i[:, None] > i[None, :]).astype(np.float32)
    cm[:, CM_ID:CM_ID + 128] = np.eye(128)
    cm[:, CM_U:CM_U + 128] = U
    cm[:, CM_L:CM_L + 128] = L
    cm[:, CM_GT:CM_GT + 128] = GT
    cm[:, CM_ONE:CM_ONE + 128] = 1.0
    cm[:, CM_AM:CM_AM + 512] = np.concatenate([L, U, L, U], 1)
    bd = np.zeros((128, 128), np.float32)
    bd[:64, :64] = 1.0
    bd[64:, 64:] = 1.0
    cm[:, CM_BD:CM_BD + 128] = bd
    NEG = -30000.0
    cm[:, CM_NEG:CM_NEG + 256] = np.concatenate([(1.0 - L) * NEG, (1.0 - U) * NEG], 1)
    return cm.astype(_bf)


def build(NS=2, S=S_FULL, debug=False, phases=("norm", "attn", "ssd", "merge", "ffn")):
    nc = bass.Bass("TRN2", target_bir_lowering=False)
    NT = S // 512
    NB = S // 128

    def din(name, shape, dt=F32):
        return nc.dram_tensor(name, shape, dt, kind="ExternalInput").ap()

    x_d = din("x", [NS, S, D])
    w_in_d = din("w_in", [D, 12832])
    w_ssd_d = din("w_ssd_proj", [2048, D])
    w_att_d = din("w_attn_proj", [512, D])
    w_out_d = din("w_out", [D, D])
    w_up_d = din("w_up", [D, 2 * DFF])
    w_dn_d = din("w_down", [DFF, D])
    pv_d = din("pvec", [128, NPV])
    cm_d = din("cmask", [128, NCM], BF16)
    dtb_d = din("dt_bias", [1, 32])
    alog_d = din("a_log", [1, 32])
    skind = "ExternalOutput" if debug else "Internal"
    hTs_d = nc.dram_tensor("hTs", [NS, KC, 128, S], BF16, kind=skind).ap()
    yat_d = nc.dram_tensor("yattn", [NS, 4, 128, S], BF16, kind=skind).ap()
    ysd_d = nc.dram_tensor("yssd", [NS, 16, 128, S], BF16, kind=skind).ap()
    out_d = nc.dram_tensor("out", [NS, S, D], F32, kind="ExternalOutput").ap()

    P = Prog(nc)
    ES = contextlib.ExitStack

    with ES() as top:
        def sbt(st, name, shape, dt):
            return st.enter_context(nc.sbuf_tensor(name, shape, dt))

        def pst(st, name, dt=F32):
            return st.enter_context(nc.psum_tensor(name, [128, 512 if dt == F32 else 1024], dt))

        pv = sbt(top, "pv", [128, NPV], F32)
        pvh = sbt(top, "pvh", [128, NPV], F32)
        cm = sbt(top, "cm", [128, NCM], BF16)
        P.dma(pv[:], pv_d[:, :], "pv")
        P.dma(cm[:], cm_d[:, :], "cm")
        P.ts("dve", pvh[:], pv[:], 0.5, None, ALU.mult)
        ident = cm[:, CM_ID:CM_ID + 128]
        maskU = cm[:, CM_U:CM_U + 128]
        maskGT = cm[:, CM_GT:CM_GT + 128]
        ones = cm[:, CM_ONE:CM_ONE + 128]
        amask = cm[:, CM_AM:CM_AM + 512]
        bdones = cm[:, CM_BD:CM_BD + 128]
        negm = cm[:, CM_NEG:CM_NEG + 256]
        neg1 = sbt(top, "neg1", [128, 2], F32)
        P.memset("dve", neg1[:, 0:1], -1.0)
        P.memset("dve", neg1[:, 1:2], -0.5)
        stg = [sbt(top, f"stg{i}", [128, 512], F32) for i in range(2)]
        stg_i = [0, 0]

        def load_w(dst, src, kn, ncols, scale=None, engs=("act", "dve")):
            SW = 512
            cw = min(ncols, SW)
            kpp = max(1, SW // cw)
            for c0 in range(0, ncols, cw):
                cn = min(cw, ncols - c0)
                for k0 in range(0, kn, kpp):
                    kk = min(kpp, kn - k0)
                    i = stg_i[0] % 2
                    stg_i[0] += 1
                    sv = stg[i][:, 0:kk * cn].rearrange("p (k n) -> p k n", k=kk)
                    P.dma(sv, src[k0 * 128:(k0 + kk) * 128, c0:c0 + cn].rearrange("(k p) n -> p k n", p=128), f"stg{i}")
                    for k in range(kk):
                        d_ = dst[:, k0 + k, c0:c0 + cn]
                        eng = engs[stg_i[1] % len(engs)]
                        stg_i[1] += 1
                        if scale is None:
                            P.copy(eng, d_, sv[:, k, :])
                        elif isinstance(scale, float):
                            P.scale(eng, d_, sv[:, k, :], scale)
                        else:
                            P.scale(eng, d_, sv[:, k, :], scale[:, k0 + k:k0 + k + 1])

        g1 = pv[:, PV_G1:PV_G1 + 8]
        g1h = pvh[:, PV_G1:PV_G1 + 8]

        for b in range(NS):
            with ES() as seq:
                hT = sbt(seq, f"hT{b}", [128, KC, S], BF16)
                if "norm" in phases:
                    with ES() as st:
                        xt = [sbt(st, f"n_xt{b}_{i}", [128, 4, D], F32) for i in range(2)]
                        junk = sbt(st, f"n_junk{b}", [128, D], BF16)
                        h16 = [sbt(st, f"n_h16{b}_{i}", [128, 4, D], BF16) for i in range(2)]
                        ssq = [sbt(st, f"n_ss{b}_{i}", [128, 4], F32) for i in range(2)]
                        lnv = [sbt(st, f"n_ln{b}_{i}", [128, 4], F32) for i in range(2)]
                        rstd = [sbt(st, f"n_rs{b}_{i}", [128, 4], F32) for i in range(2)]
                        pb = [pst(st, f"n_pb{b}_{i}", BF16) for i in range(4)]
                        for j in range(NT):
                            i2 = j % 2
                            P.dma(xt[i2][:], x_d[b, 512 * j:512 * (j + 1), :].rearrange("(k p) d -> p k d", p=128), f"xt{i2}")
                            P.memset("pool", ssq[i2][:], 0.0)
                            for k in range(4):
                                P.act(junk[:], xt[i2][:, k, :], AF.Square, accum_out=ssq[i2][:, k:k + 1])
                            P.act(lnv[i2][:], ssq[i2][:], AF.Ln, scale=1.0 / D, bias=EPS)
                            P.act(rstd[i2][:], lnv[i2][:], AF.Exp, scale=-0.5)
                            for k in range(4):
                                P.scale("dve" if k % 2 == 0 else "act", h16[i2][:, k, :], xt[i2][:, k, :], rstd[i2][:, k:k + 1])
                            for pr in range(4):
                                bank = pb[pr]
                                for kk in range(2):
                                    kc = 2 * pr + kk
                                    for k in range(4):
                                        P.tr(bank[:, kk * 512 + k * 128:kk * 512 + (k + 1) * 128], h16[i2][:, k, kc * 128:(kc + 1) * 128], ident)
                                P.copy("act" if pr % 2 == 0 else "dve", hT[:, 2 * pr:2 * pr + 2, 512 * j:512 * (j + 1)],
                                       bank[:, :].rearrange("p (k t) -> p k t", k=2))
                            P.dma(hTs_d[b, :, :, 512 * j:512 * (j + 1)].rearrange("k p t -> p k t"), hT[:, :, 512 * j:512 * (j + 1)], f"hs{i2}")
                P.barrier()
                if "attn" in phases:
                    with ES() as st:
                        wq = [sbt(st, f"a_wq{b}_{i}", [128, KC, 128], BF16) for i in range(2)]
                        wk = [sbt(st, f"a_wk{b}_{i}", [128, KC, 128], BF16) for i in range(2)]
                        wv = [sbt(st, f"a_wv{b}_{i}", [128, KC, 128], BF16) for i in range(2)]
                        QT = sbt(st, f"a_QT{b}", [128, S], BF16)
                        KT = sbt(st, f"a_KT{b}", [128, S], BF16)
                        Vt = sbt(st, f"a_Vt{b}", [128, NB, 128], BF16)
                        lnb = sbt(st, f"a_ln{b}", [128, 2 * NT, 512], F32)
                        sq = [sbt(st, f"a_sq{b}_{i}", [128, 512], BF16) for i in range(2)]
                        pt = [sbt(st, f"a_pt{b}_{i}", [128, 512], BF16) for i in range(2)]
                        acc = sbt(st, f"a_acc{b}", [128, 2, S], F32)
                        rden = [sbt(st, f"a_rd{b}_{i}", [128, 512], F32) for i in range(2)]
                        yat = [sbt(st, f"a_yat{b}_{i}", [128, S], BF16) for i in range(2)]
                        bk = [pst(st, f"a_bk{b}_{i}") for i in range(8)]
                        ps_p = [bk[0], bk[1]]
                        ps_n = bk[7]
                        ps_v = bk[6]
                        ps_s = [(bk[0], bk[1]), (bk[2], bk[3])]
                        ps_o = [bk[4], bk[5]]
                        def load_attn_w(it_):
                            hp_i, g_i = divmod(it_, 3)
                            slot = 0
                            for (wt, c0) in ((wq[it_ % 2], C_Q), (wk[it_ % 2], C_K), (wv[it_ % 2], C_V)):
                                cc = c0 + g_i * 512 + hp_i * 128
                                for k0 in (0, 4):
                                    sv = lnb[:, slot, :].rearrange("p (k n) -> p k n", k=4)
                                    P.dma(sv, w_in_d[k0 * 128:(k0 + 4) * 128, cc:cc + 128].rearrange("(k p) n -> p k n", p=128), f"aw{slot}")
                                    for k in range(4):
                                        P.scale("act" if (k + slot) % 2 == 0 else "dve", wt[:, k0 + k, :], sv[:, k, :], g1[:, k0 + k:k0 + k + 1])
                                    slot += 1

                        load_attn_w(0)
                        it = 0
                        for hp in range(4):
                            for g in range(3):
                                d = DIL[g]
                                L = S // d
                                nbl = L // 128
                                wi = it % 2
                                it += 1

                                def nat_view(T, j):
                                    if d == 1:
                                        return T[:, 512 * j:512 * (j + 1)]
                                    return T[:, :].rearrange("p (r m) -> p r m", r=d)[:, :, (512 * j) // d:(512 * (j + 1)) // d]

                                def nat_src(a):
                                    if d == 1:
                                        return a
                                    return a.rearrange("p (m r) -> p r m", r=d)

                                ps_nn = [bk[5], bk[7]]
                                items = [(wt, T, j) for (wt, T) in ((wq[wi], QT), (wk[wi], KT)) for j in range(NT)]

                                def qk1(idx):
                                    wt, T, j = items[idx]
                                    pp = ps_p[idx % 2]
                                    for kc in range(KC):
                                        P.mm(pp[:, 0:512], wt[:, kc, :], hT[:, kc, 512 * j:512 * (j + 1)], start=(kc == 0))
                                    P.act(sq[idx % 2][:], pp[:, 0:512], AF.Square)
                                    P.copy("dve", nat_view(T, j), nat_src(pp[:, 0:512]))
                                    P.mm(ps_nn[idx % 2][:, 0:512], bdones, sq[idx % 2][:], start=True)

                                def qk2(idx):
                                    P.act(lnb[:, idx, :], ps_nn[idx % 2][:, 0:512], AF.Ln, scale=1.0 / 64, bias=EPS)

                                qk1(0)
                                for idx in range(len(items)):
                                    if idx + 1 < len(items):
                                        qk1(idx + 1)
                                    qk2(idx)
                                idx = 0
                                for (T, gcol) in ((QT, PV_QG), (KT, PV_KG)):
                                    for j in range(NT):
                                        P.act(lnb[:, idx, :], lnb[:, idx, :], AF.Exp, scale=-0.5)
                                        P.stt("dve", nat_view(T, j), nat_view(T, j), pv[:, gcol:gcol + 1], nat_src(lnb[:, idx, :]), ALU.mult, ALU.mult)
                                        idx += 1
                                if it < 12:
                                    load_attn_w(it)
                                for bi in range(NB):
                                    r, n = bi // nbl, bi % nbl
                                    t0 = r + d * 128 * n
                                    for kc in range(KC):
                                        P.mm(ps_v[:, (bi % 4) * 128:(bi % 4 + 1) * 128], hT[:, kc, t0:t0 + d * 127 + 1:d], wv[wi][:, kc, :],
                                             start=(kc == 0 and bi % 4 == 0))
                                    if bi % 4 == 3:
                                        P.copy("act", Vt[:, bi - 3:bi + 1, :], ps_v[:, :].rearrange("p (k c) -> p k c", k=4))
                                def Sblk(bi):
                                    r, n = bi // nbl, bi % nbl
                                    c0 = r * L + 128 * n
                                    hp_ = n > 0
                                    ptb = pt[bi % 2]
                                    for h in range(2):
                                        pss = ps_s[bi % 2][h]
                                        rows = slice(64 * h, 64 * h + 64)
                                        if hp_:
                                            P.mm(pss[:, 0:128], KT[rows, c0 - 128:c0], QT[rows, c0:c0 + 128], start=True)
                                        P.mm(pss[:, 128:256], KT[rows, c0:c0 + 128], QT[rows, c0:c0 + 128], start=not hp_)
                                        lo = 0 if hp_ else 128
                                        P.mm(pss[:, lo:256], ident, negm[:, lo:256], start=False)
                                        P.act(ptb[:, h * 256 + lo:h * 256 + 256], pss[:, lo:256], AF.Exp, scale=0.125)

                                def Vblk(bi):
                                    r, n = bi // nbl, bi % nbl
                                    hp_ = n > 0
                                    ptb = pt[bi % 2]
                                    pso = ps_o[bi % 2]
                                    for h in range(2):
                                        rows = slice(64 * h, 64 * h + 64)
                                        kw = {} if h == 0 else {"tile_position": (0, 64)}
                                        first = True
                                        for (lh, ncol) in ((None, 0), (ones, 128)):
                                            for pc in ((0, 1) if hp_ else (1,)):
                                                vb = bi - 1 if pc == 0 else bi
                                                lhsT = Vt[:, vb, 64 * h:64 * h + 64] if lh is None else ones[:, 0:64]
                                                P.mm(pso[rows, ncol:ncol + 128], lhsT, ptb[:, h * 256 + pc * 128:h * 256 + pc * 128 + 128],
                                                     start=first, **kw)
                                                first = False
                                    t0 = r + d * 128 * n
                                    av = acc[:, :, t0:t0 + d * 127 + 1:d]
                                    pv3 = pso[:, 0:256].rearrange("p (c q) -> p c q", c=2)
                                    if g == 0:
                                        P.copy("dve", av, pv3)
                                    else:
                                        P.tt("dve", av, pv3, av, ALU.add)

                                Sblk(0)
                                for bi in range(NB):
                                    if bi + 1 < NB:
                                        Sblk(bi + 1)
                                    Vblk(bi)
                            yb = yat[hp % 2]
                            for j in range(NT):
                                sl = slice(512 * j, 512 * (j + 1))
                                P.add("dve", lambda e, o=rden[j % 2][:], i=acc[:, 1, sl]: e.reciprocal(o, i), [acc[:, 1, sl]], [rden[j % 2][:]])
                                P.tt("pool", yb[:, sl], acc[:, 0, sl], rden[j % 2][:], ALU.mult)
                            P.dma(yat_d[b, hp, :, :], yb[:, :], f"yat{hp % 2}")
                P.barrier()
                if "ssd" in phases:
                    with ES() as st:
                        wdt = sbt(st, f"s_wdt{b}", [128, KC, 32], BF16)
                        dtb = sbt(st, f"s_dtb{b}", [128, 32], F32)
                        alog = sbt(st, f"s_alog{b}", [128, 32], F32)
                        xdt = sbt(st, f"s_xdt{b}", [128, NB, 32], F32)
                        tmpa = sbt(st, f"s_tmpa{b}", [128, NB, 32], F32)
                        dt_all = sbt(st, f"s_dt{b}", [128, NB, 32], F32)
                        a32 = sbt(st, f"s_a32{b}", [128, NB, 32], F32)
                        a_hi = sbt(st, f"s_ahi{b}", [128, NB, 32], BF16)
                        a_lo = sbt(st, f"s_alo{b}", [128, NB, 32], BF16)
                        wg = [sbt(st, f"s_wg{b}_{i}", [128, KC, 768], BF16) for i in range(2)]
                        raw = [sbt(st, f"s_raw{b}_{i}", [128, 515], F32) for i in range(4)]
                        accb = [sbt(st, f"s_acc{b}_{i}", [128, 512], F32) for i in range(4)]
                        thb = [sbt(st, f"s_th{b}_{i}", [128, 512], F32) for i in range(2)]
                        xbcT = [sbt(st, f"s_xbcT{b}_{i}", [128, 4, 512], BF16) for i in range(2)]
                        zs = [sbt(st, f"s_zs{b}_{i}", [128, 2, 512], BF16) for i in range(2)]
                        xbtok = [sbt(st, f"s_xbtok{b}_{i}", [128, 4, 384], BF16) for i in range(2)]
                        Xdt = [sbt(st, f"s_Xdt{b}_{i}", [128, 4, 256], BF16) for i in range(2)]
                        xsD = [sbt(st, f"s_xsD{b}_{i}", [128, 2, 512], F32) for i in range(2)]
                        Rh = [sbt(st, f"s_Rh{b}_{i}", [128, 512], BF16) for i in range(4)]
                        Rl = [sbt(st, f"s_Rl{b}_{i}", [128, 512], BF16) for i in range(4)]
                        Eb = [sbt(st, f"s_E{b}_{i}", [128, 512], BF16) for i in range(2)]
                        EAb = [sbt(st, f"s_EA{b}_{i}", [128, 512], BF16) for i in range(2)]
                        Mp = [sbt(st, f"s_Mp{b}_{i}", [128, 512], BF16) for i in range(2)]
                        Cs = [sbt(st, f"s_Cs{b}_{i}", [128, 512], BF16) for i in range(2)]
                        CBm = [sbt(st, f"s_CBm{b}_{i}", [128, 128], BF16) for i in range(2)]
                        Xd = [sbt(st, f"s_Xd{b}_{i}", [128, 256], BF16) for i in range(2)]
                        ddc = [sbt(st, f"s_dd{b}_{i}", [128, 12], F32) for i in range(2)]
                        Sst = sbt(st, f"s_S{b}", [128, 256], F32)
                        Sbf = [sbt(st, f"s_Sbf{b}_{i}", [128, 256], BF16) for i in range(2)]
                        ybuf = [sbt(st, f"s_yb{b}_{i}", [128, 2, 512], F32) for i in range(2)]
                        sqw = sbt(st, f"s_sq{b}", [128, 2, 512], BF16)
                        lnw = sbt(st, f"s_lnw{b}", [128, 512], F32)
                        yo = [sbt(st, f"s_yo{b}_{i}", [128, 2, 512], BF16) for i in range(2)]
                        ps_p = [pst(st, f"s_psp{b}_{i}") for i in range(2)]
                        ps_seg = pst(st, f"s_seg{b}")
                        ps_acs = pst(st, f"s_acs{b}")
                        ps_cb = pst(st, f"s_cb{b}")
                        ps_st = pst(st, f"s_st{b}")
                        ps_y = pst(st, f"s_y{b}")
                        ps_tr = pst(st, f"s_tr{b}", BF16)
                        load_w(wdt, w_in_d[:, C_DT:C_DT + 32], KC, 32, scale=g1)
                        P.dma(dtb[:], dtb_d[:, :].partition_broadcast(128), "dtb")
                        P.dma(alog[:], alog_d[:, :].partition_broadcast(128), "alog")
                        for half in range(NB // 16):
                            pp = ps_p[half % 2]
                            for bl in range(16):
                                blk = half * 16 + bl
                                for kc in range(KC):
                                    P.mm(pp[:, bl * 32:(bl + 1) * 32], hT[:, kc, blk * 128:(blk + 1) * 128], wdt[:, kc, :],
                                         start=(bl == 0 and kc == 0))
                            P.tt("dve", xdt[:, half * 16:(half + 1) * 16, :], pp[:, :].rearrange("p (k h) -> p k h", k=16),
                                 dtb[:, :].unsqueeze(1).broadcast_to([128, 16, 32]), ALU.add)
                        P.act(tmpa[:], xdt[:], AF.Abs)
                        P.act(tmpa[:], tmpa[:], AF.Exp, scale=-1.0)
                        P.act(alog[:], alog[:], AF.Exp)
                        P.act(a32[:], tmpa[:], AF.Ln, bias=1.0)
                        P.stt("dve", dt_all[:], xdt[:], 0.0, a32[:], ALU.max, ALU.add)
                        P.tt("dve", a32[:], dt_all[:], alog[:, :].unsqueeze(1).broadcast_to([128, NB, 32]), ALU.mult)
                        P.ts("dve", a32[:], a32[:], -1.0, None, ALU.mult)
                        P.copy("dve", a_hi[:], a32[:])
                        P.tt("dve", tmpa[:], a32[:], a_hi[:], ALU.subtract)
                        P.copy("dve", a_lo[:], tmpa[:])
                        cwh = lambda i, c: pvh[:, PV_CW + i * 32 + c:PV_CW + i * 32 + c + 1]
                        cbh = lambda c: pvh[:, PV_CB + c:PV_CB + c + 1]
                        cnt = [0]
                        v4 = lambda a: a.rearrange("p (k l) -> p k l", k=4)

                        def load_group(g):
                            wgi = wg[g % 2]
                            load_w(wgi[:, :, 0:256], w_in_d[:, C_Z + g * 256:C_Z + (g + 1) * 256], KC, 256, scale=g1h)
                            load_w(wgi[:, :, 256:512], w_in_d[:, C_X + g * 256:C_X + (g + 1) * 256], KC, 256, scale=g1)
                            load_w(wgi[:, :, 512:640], w_in_d[:, C_B + g * 128:C_B + (g + 1) * 128], KC, 128, scale=g1)
                            load_w(wgi[:, :, 640:768], w_in_d[:, C_C + g * 128:C_C + (g + 1) * 128], KC, 128, scale=g1)

                        pc_pp = {}

                        def PCm(W, i):
                            g, w = divmod(W, NT)
                            wgi = wg[g % 2]
                            tsl = slice(512 * w, 512 * (w + 1))
                            pp = ps_p[cnt[0] % 2]
                            cnt[0] += 1
                            pc_pp[(W, i)] = pp
                            c0 = i * 128 if i < 2 else 256 + (i - 2) * 128
                            for kc in range(KC):
                                P.mm(pp[:, 0:512], wgi[:, kc, c0:c0 + 128], hT[:, kc, tsl], start=(kc == 0))

                        def PCe(W, i):
                            g, w = divmod(W, NT)
                            wi = W % 2
                            pp = pc_pp[(W, i)]
                            if i < 2:
                                P.act(thb[i][:], pp[:, 0:512], AF.Tanh)
                                P.stt("dve", zs[wi][:, i, :], thb[i][:], 1.0, pp[:, 0:512], ALU.add, ALU.mult)
                                return
                            ct = i - 2
                            cchunk = (2 * g, 2 * g + 1, 16 + g, 24 + g)
                            if w == 0:
                                P.memset("pool", raw[ct][:, 0:3], 0.0)
                            cc = cchunk[ct]
                            ab = accb[ct]
                            P.copy("act", raw[ct][:, 3:515], pp[:, 0:512])
                            P.act(ab[:], pp[:, 0:512], AF.Identity, scale=cwh(3, cc), bias=cbh(cc))
                            for k_ in range(3):
                                P.stt("dve", ab[:], raw[ct][:, k_:k_ + 512], cwh(k_, cc), ab[:], ALU.mult, ALU.add)
                            P.copy("pool", raw[ct][:, 0:3], raw[ct][:, 512:515])

                        def PCf(W, i):
                            wi = W % 2
                            pc_pp.pop((W, i))
                            if i < 2:
                                return
                            ct = i - 2
                            ab = accb[ct]
                            tb = thb[ct % 2]
                            P.act(tb[:], ab[:], AF.Tanh)
                            P.stt("dve", xbcT[wi][:, ct, :], tb[:], 1.0, ab[:], ALU.add, ALU.mult)

                        def PCt(W):
                            g, w = divmod(W, NT)
                            wi = W % 2
                            for bp in range(2):
                                for k2 in range(2):
                                    blk = 2 * bp + k2
                                    for (ci_, ct) in enumerate((0, 1, 2)):
                                        P.tr(ps_tr[:, k2 * 384 + ci_ * 128:k2 * 384 + (ci_ + 1) * 128],
                                             xbcT[wi][:, ct, blk * 128:(blk + 1) * 128], ident)
                                P.copy("act", xbtok[wi][:, 2 * bp:2 * bp + 2, :], ps_tr[:, 0:768].rearrange("p (k c) -> p k c", k=2))
                            P.tt("dve", Xdt[wi][:, :, :].rearrange("p b (k q) -> p b k q", k=4),
                                 xbtok[wi][:, :, 0:256].rearrange("p b (k q) -> p b k q", k=4),
                                 dt_all[:, 4 * w:4 * w + 4, 4 * g:4 * g + 4].unsqueeze(3).broadcast_to([128, 4, 4, 64]), ALU.mult)
                            for hc in range(2):
                                P.act(xsD[wi][:, hc, :], xbcT[wi][:, hc, :], AF.Copy, scale=pv[:, PV_D + 2 * g + hc:PV_D + 2 * g + hc + 1])

                        def PC(W):
                            for i in range(6):
                                PCm(W, i)
                                PCe(W, i)
                                PCf(W, i)
                            PCt(W)

                        def RG(n):
                            g, cidx = divmod(n, NB)
                            for (R_, asrc) in ((Rh[n % 4], a_hi), (Rl[n % 4], a_lo)):
                                P.tt("pool", v4(R_[:, :]), maskU.unsqueeze(1).broadcast_to([128, 4, 128]),
                                     asrc[:, cidx, 4 * g:4 * g + 4].unsqueeze(2).broadcast_to([128, 4, 128]), ALU.mult)

                        def A(n):
                            g, cidx = divmod(n, NB)
                            W = n // 4
                            wi = W % 2
                            c4 = cidx % 4
                            ci = n % 2
                            tc = slice(128 * c4, 128 * (c4 + 1))
                            BT = xbcT[wi][:, 2, tc]
                            CT = xbcT[wi][:, 3, tc]
                            P.mm(ps_seg[:, 0:512], maskGT, Rh[n % 4][:, :], start=True)
                            P.mm(ps_seg[:, 0:512], maskGT, Rl[n % 4][:, :], start=False)
                            P.mm(ps_acs[:, 0:512], ones, Rh[n % 4][:, :], start=True)
                            P.mm(ps_acs[:, 0:512], ones, Rl[n % 4][:, :], start=False)
                            P.mm(ps_cb[:, 0:128], BT, CT, start=True)
                            dd = ddc[ci]
                            P.act(Eb[ci][:, :], ps_seg[:, 0:512], AF.Exp)
                            P.act(dd[:, 0:4], ps_seg[:, 127:512:128], AF.Exp)
                            P.act(EAb[ci][:, :], ps_acs[:, 0:512], AF.Exp)
                            P.act(dd[:, 4:8], ps_acs[:, 127:512:128], AF.Exp)
                            P.tt("dve", CBm[ci][:, :], ps_cb[:, 0:128], maskU, ALU.mult)
                            P.tt("dve", v4(Mp[ci][:, :]), v4(Eb[ci][:, :]), CBm[ci][:, :].unsqueeze(1).broadcast_to([128, 4, 128]), ALU.mult)
                            P.tt("pool", v4(Cs[ci][:, :]), CT.unsqueeze(1).broadcast_to([128, 4, 128]), v4(EAb[ci][:, :]), ALU.mult)
                            P.tt("dve", Xd[ci][:, :].rearrange("p (k q) -> p k q", k=4),
                                 Xdt[wi][:, c4, :].rearrange("p (k q) -> p k q", k=4),
                                 dd[:, 0:4].unsqueeze(2).broadcast_to([128, 4, 64]), ALU.mult)

                        def Bk(n):
                            g, cidx = divmod(n, NB)
                            W = n // 4
                            wi = W % 2
                            c4 = cidx % 4
                            ci = n % 2
                            tc = slice(128 * c4, 128 * (c4 + 1))
                            dd = ddc[ci]
                            sb_old = Sbf[n % 2]
                            sb_new = Sbf[(n + 1) % 2]
                            firstq = [True, True]
                            for hc in range(2):
                                for hh in range(2):
                                    k = 2 * hc + hh
                                    kw = {} if hh == 0 else {"tile_position": (0, 64)}
                                    o = ps_y[64 * hh:64 * hh + 64, hc * 128:(hc + 1) * 128]
                                    P.mm(o, Xdt[wi][:, c4, k * 64:(k + 1) * 64], v4(Mp[ci][:, :])[:, k, :], start=firstq[hh], **kw)
                                    firstq[hh] = False
                                    if cidx > 0:
                                        P.mm(o, sb_old[:, k * 64:(k + 1) * 64], v4(Cs[ci][:, :])[:, k, :], start=False, **kw)
                            P.mm(ps_st[:, 0:256], xbtok[wi][:, c4, 256:384], Xd[ci][:, :], start=True)
                            if cidx == 0:
                                P.copy("dve", Sst[:, :], ps_st[:, 0:256])
                            else:
                                s3 = Sst[:, :].rearrange("p (k q) -> p k q", k=4)
                                P.tt("dve", s3, s3, dd[:, 4:8].unsqueeze(2).broadcast_to([128, 4, 64]), ALU.mult)
                                P.tt("dve", Sst[:, :], Sst[:, :], ps_st[:, 0:256], ALU.add)
                            P.copy("act", sb_new[:, :], Sst[:, :])
                            P.tt("dve", ybuf[wi][:, :, tc], ps_y[:, 0:256].rearrange("p (c q) -> p c q", c=2), xsD[wi][:, :, tc], ALU.add)

                        def WN(W):
                            g, w = divmod(W, NT)
                            wi = W % 2
                            tsl = slice(512 * w, 512 * (w + 1))
                            P.tt("dve", ybuf[wi][:], ybuf[wi][:], zs[wi][:], ALU.mult)
                            P.act(sqw[:], ybuf[wi][:], AF.Square)
                            pp = ps_p[cnt[0] % 2]
                            cnt[0] += 1
                            P.mm(pp[:, 0:512], ones, sqw[:, 0, :], start=True)
                            P.mm(pp[:, 0:512], ones, sqw[:, 1, :], start=False)
                            P.act(lnw[:], pp[:, 0:512], AF.Ln, scale=1.0 / 256, bias=EPS)
                            P.act(lnw[:], lnw[:], AF.Exp, scale=-0.5)
                            P.tt("dve", yo[wi][:], ybuf[wi][:], lnw[:, :].unsqueeze(1).broadcast_to([128, 2, 512]), ALU.mult)
                            P.dma(ysd_d[b, 2 * g:2 * g + 2, :, tsl].rearrange("c p t -> p c t"), yo[wi][:], f"yo{wi}")

                        NCH = 8 * NB
                        NW = 8 * NT
                        load_group(0)
                        PC(0)
                        RG(0)
                        RG(1)
                        for n in range(NCH):
                            g, cidx = divmod(n, NB)
                            W = n // 4
                            c4 = n % 4
                            if n + 2 < NCH:
                                RG(n + 2)
                            A(n)
                            if n > 0:
                                Bk(n - 1)
                                if c4 == 0:
                                    WN(W - 1)
                            if cidx == 12 and g + 1 < 8:
                                load_group(g + 1)
                            if W + 1 < NW:
                                order = (2, 3, 4, 5, 0, 1)
                                o = order
                                if c4 == 0:
                                    PCm(W + 1, o[0]); PCm(W + 1, o[1])
                                elif c4 == 1:
                                    PCe(W + 1, o[0]); PCm(W + 1, o[2]); PCe(W + 1, o[1]); PCm(W + 1, o[3]); PCf(W + 1, o[0])
                                elif c4 == 2:
                                    PCe(W + 1, o[2]); PCf(W + 1, o[1]); PCm(W + 1, o[4]); PCe(W + 1, o[3]); PCf(W + 1, o[2])
                                    PCm(W + 1, o[5]); PCf(W + 1, o[3])
                                    PCt(W + 1)
                                else:
                                    PCe(W + 1, o[4]); PCe(W + 1, o[5]); PCf(W + 1, o[4]); PCf(W + 1, o[5])
                        Bk(NCH - 1)
                        WN(NW - 1)
            P.barrier()

        tiles = [(b, j) for b in range(NS) for j in range(NT)]
        if "merge" in phases:
            with ES() as st:
                Wssd = sbt(st, "m_wssd", [128, 16, D], BF16)
                Watt = sbt(st, "m_watt", [128, 4, D], BF16)
                Wgs = sbt(st, "m_wgs", [128, KC, D], BF16)
                Wga = sbt(st, "m_wga", [128, KC, D], BF16)
                Wout = sbt(st, "m_wout", [128, KC, D], BF16)
                ys = [sbt(st, f"m_ys{i}", [128, 16, 512], BF16) for i in range(2)]
                ya = [sbt(st, f"m_ya{i}", [128, 4, 512], BF16) for i in range(2)]
                hTt = [sbt(st, f"m_ht{i}", [128, KC, 512], BF16) for i in range(2)]
                xt = [sbt(st, f"m_xt{i}", [128, 4, D], F32) for i in range(2)]
                th = [sbt(st, f"m_th{i}", [128, 512], F32) for i in range(2)]
                m1 = [sbt(st, f"m_m1{i}", [128, 512], F32) for i in range(3)]
                mT = sbt(st, "m_mT", [128, KC, 512], BF16)
                psm = [pst(st, f"m_ps{i}") for i in range(8)]
                pcm = [0]

                def m_loads(t):
                    b, j = tiles[t]
                    i2 = t % 2
                    tsl = slice(512 * j, 512 * (j + 1))
                    P.dma(hTt[i2][:], hTs_d[b, :, :, tsl].rearrange("c p t -> p c t"), f"mht{i2}")
                    P.dma(ys[i2][:], ysd_d[b, :, :, tsl].rearrange("c p t -> p c t"), f"mys{i2}")
                    P.dma(ya[i2][:], yat_d[b, :, :, tsl].rearrange("c p t -> p c t"), f"mya{i2}")
                    P.dma(xt[i2][:], x_d[b, tsl, :].rearrange("(k p) d -> p k d", p=128), f"mxt{i2}")

                def m_compute(t):
                    b, j = tiles[t]
                    i2 = t % 2
                    tsl = slice(512 * j, 512 * (j + 1))
                    for oc in range(KC):
                        osl = slice(oc * 128, (oc + 1) * 128)
                        pc = pcm[0]
                        p1, pg1, p2, pg2 = psm[pc % 8], psm[(pc + 1) % 8], psm[(pc + 2) % 8], psm[(pc + 3) % 8]
                        pcm[0] += 4
                        e2 = 0
                        for kc in range(KC):
                            P.mm(pg1[:, 0:512], Wgs[:, kc, osl], hTt[i2][:, kc, :], start=(kc == 0))
                        for kc in range(16):
                            P.mm(p1[:, 0:512], Wssd[:, kc, osl], ys[i2][:, kc, :], start=(kc == 0))
                        for kc in range(KC):
                            P.mm(pg2[:, 0:512], Wga[:, kc, osl], hTt[i2][:, kc, :], start=(kc == 0))
                        for kc in range(4):
                            P.mm(p2[:, 0:512], Watt[:, kc, osl], ya[i2][:, kc, :], start=(kc == 0))
                        P.act(th[e2][:], pg1[:, 0:512], AF.Tanh)
                        P.stt("dve", m1[e2][:], th[e2][:], 1.0, p1[:, 0:512], ALU.add, ALU.mult)
                        P.act(th[e2 + 1][:], pg2[:, 0:512], AF.Tanh)
                        P.stt("dve", m1[e2 + 1][:], th[e2 + 1][:], 1.0, p2[:, 0:512], ALU.add, ALU.mult)
                        P.tt("pool", mT[:, oc, :], m1[e2][:], m1[e2 + 1][:], ALU.add)
                    for blk in range(4):
                        for half in range(2):
                            pp = psm[pcm[0] % 8]
                            pcm[0] += 1
                            hs = slice(half * 512, (half + 1) * 512)
                            for kc in range(KC):
                                P.mm(pp[:, 0:512], mT[:, kc, blk * 128:(blk + 1) * 128], Wout[:, kc, hs], start=(kc == 0))
                            P.tt("dve", xt[i2][:, blk, hs], xt[i2][:, blk, hs], pp[:, 0:512], ALU.add)
                    P.dma(out_d[b, tsl, :].rearrange("(k p) d -> p k d", p=128), xt[i2][:], f"mo{i2}")

                m_loads(0)
                load_w(Wgs, w_in_d[:, C_GS:C_GS + D], KC, D, scale=g1h)
                load_w(Wssd, w_ssd_d, 16, D, scale=pv[:, PV_GS:PV_GS + 16])
                load_w(Wga, w_in_d[:, C_GA:C_GA + D], KC, D, scale=g1h)
                load_w(Watt, w_att_d, 4, D)
                load_w(Wout, w_out_d, KC, D, scale=0.5)
                for t in range(len(tiles)):
                    if t + 1 < len(tiles):
                        m_loads(t + 1)
                    m_compute(t)
            P.barrier()

        if "ffn" in phases:
            with ES() as st:
                Wup = sbt(st, "f_wup", [128, KC, 2 * DFF], BF16)
                Wdn = sbt(st, "f_wdn", [128, NFC, D], BF16)
                xb = [sbt(st, f"f_xb{i}", [128, D], F32) for i in range(2)]
                h16 = sbt(st, "f_h16", [128, 1, D], BF16)
                ssq = [sbt(st, f"f_ss{i}", [128, 1], F32) for i in range(2)]
                lnv = [sbt(st, f"f_ln{i}", [128, 1], F32) for i in range(2)]
                rstd = [sbt(st, f"f_rs{i}", [128, 1], F32) for i in range(2)]
                h2T = [sbt(st, f"f_h2T{i}", [128, KC, 512], BF16) for i in range(2)]
                aT = sbt(st, "f_aT", [128, NFC, 512], BF16)
                tail = sbt(st, "f_tail", [128, 2 * NFC, 2], F32)
                rawf = [sbt(st, f"f_raw{i}", [128, 514], F32) for i in range(2)]
                accf = [sbt(st, f"f_acc{i}", [128, 512], F32) for i in range(4)]
                ostg = [sbt(st, f"f_os{i}", [128, 512], F32) for i in range(2)]
                pbf = [pst(st, f"f_pb{i}", BF16) for i in range(2)]
                psf = [pst(st, f"f_ps{i}") for i in range(6)]
                pcf = [0]
                blkc = [0]
                def fwc(i, ci):
                    src = pvh if ci < NFC else pv
                    return src[:, PV_FW + i * 44 + ci:PV_FW + i * 44 + ci + 1]

                def fbc(ci):
                    src = pvh if ci < NFC else pv
                    return src[:, PV_FB + ci:PV_FB + ci + 1]

                def f_norm(t):
                    b, j = tiles[t]
                    hb = h2T[t % 2]
                    for k in range(4):
                        q = blkc[0] % 2
                        blkc[0] += 1
                        r0 = 512 * j + 128 * k
                        P.dma(xb[q][:], out_d[b, r0:r0 + 128, :], f"fxb{q}")
                        P.memset("pool", ssq[q][:], 0.0)
                        P.act(h16[:, 0, :], xb[q][:], AF.Square, accum_out=ssq[q][:, 0:1])
                        P.act(lnv[q][:], ssq[q][:], AF.Ln, scale=1.0 / D, bias=EPS)
                        P.act(rstd[q][:], lnv[q][:], AF.Exp, scale=-0.5)
                        P.scale("dve" if k % 2 == 0 else "act", h16[:, 0, :], xb[q][:], rstd[q][:, 0:1])
                        bank = pbf[q]
                        for kc in range(KC):
                            P.tr(bank[:, kc * 128:(kc + 1) * 128], h16[:, 0, kc * 128:(kc + 1) * 128], ident)
                        P.copy("act" if k % 2 == 0 else "dve", hb[:, :, k * 128:(k + 1) * 128], bank[:, :].rearrange("p (c t) -> p c t", c=KC))

                def f_up(t):
                    b, j = tiles[t]
                    hb = h2T[t % 2]
                    if j == 0:
                        P.memset("pool", tail[:], 0.0)

                    def up1(c):
                        res = []
                        for half in range(2):
                            ci = half * NFC + c
                            pp = psf[pcf[0] % 6]
                            pcf[0] += 1
                            col = half * DFF + c * 128
                            for kc in range(KC):
                                P.mm(pp[:, 0:512], Wup[:, kc, col:col + 128], hb[:, kc, :], start=(kc == 0))
                            rw = rawf[half]
                            ab = accf[(2 * c + half) % 4]
                            P.copy("pool", rw[:, 0:2], tail[:, ci, :])
                            P.copy("act", rw[:, 2:514], pp[:, 0:512])
                            P.act(ab[:], pp[:, 0:512], AF.Identity, scale=fwc(2, ci), bias=fbc(ci))
                            P.copy("pool", tail[:, ci, :], rw[:, 512:514])
                            for i in range(2):
                                P.stt("dve", ab[:], rw[:, i:i + 512], fwc(i, ci), ab[:], ALU.mult, ALU.add)
                            res.append(ab)
                        return res

                    def up2(c, res):
                        tf = rawf[0][:, 2:514]
                        P.act(tf, res[0][:], AF.Tanh)
                        P.stt("dve", res[0][:], tf, 1.0, res[0][:], ALU.add, ALU.mult)
                        P.tt("pool", aT[:, c, :], res[0][:], res[1][:], ALU.mult)

                    for c in range(NFC):
                        r_ = up1(c)
                        up2(c, r_)

                def f_down(t):
                    b, j = tiles[t]
                    for blk in range(4):
                        r0 = 512 * j + 128 * blk
                        for half in range(2):
                            pp = psf[pcf[0] % 6]
                            q = pcf[0] % 2
                            pcf[0] += 1
                            hs = slice(half * 512, (half + 1) * 512)
                            for c in range(NFC):
                                P.mm(pp[:, 0:512], aT[:, c, blk * 128:(blk + 1) * 128], Wdn[:, c, hs], start=(c == 0))
                            P.copy("act", ostg[q][:], pp[:, 0:512])
                            P.dma(out_d[b, r0:r0 + 128, hs], ostg[q][:], f"fo{q}", eng="pool", accum_op=ALU.add)

                f_norm(0)
                load_w(Wup, w_up_d, KC, 2 * DFF, scale=pv[:, PV_G2:PV_G2 + 8])
                load_w(Wdn, w_dn_d, NFC, D)
                nt_ = len(tiles)
                f_up(0)
                for t in range(nt_):
                    if t + 1 < nt_:
                        f_norm(t + 1)
                    f_down(t)
                    if t + 1 < nt_:
                        f_up(t + 1)
        P.barrier(final=True)
        P.emit()
    return nc, P


_CACHE = {}


def _core_inputs(inp, c, NS):
    x = np.ascontiguousarray(np.asarray(inp["x"], dtype=np.float32)[c * NS:(c + 1) * NS])
    return x


def kernel(**inputs):
    NCORES = 8
    NS = 2
    if "nc" not in _CACHE:
        _CACHE["nc"] = build(NS=NS)[0]
    nc = _CACHE["nc"]
    f = lambda k: np.ascontiguousarray(np.asarray(inputs[k], dtype=np.float32)[0])
    shared = {
        "w_in": f("w_in"), "w_ssd_proj": f("w_ssd_proj"), "w_attn_proj": f("w_attn_proj"), "w_out": f("w_out"),
        "w_up": f("w_up"), "w_down": f("w_down"), "pvec": pack_pvec(inputs), "cmask": make_cmask(),
        "dt_bias": np.asarray(inputs["dt_bias"], np.float32).reshape(1, 32),
        "a_log": np.asarray(inputs["a_log"], np.float32).reshape(1, 32),
    }
    in_maps = []
    for c in range(NCORES):
        m = dict(shared)
        m["x"] = _core_inputs(inputs, c, NS)
        in_maps.append(m)
    res = run_bass_kernel_spmd(nc, in_maps, core_ids=list(range(NCORES)))
    out = np.concatenate([np.asarray(r["out"]) for r in res.results], axis=0)
    return out.astype(np.float32)
```

```python
import contextlib
import numpy as np
import ml_dtypes
import concourse.bass as bass
import concourse.mybir as mybir
from concourse.bass_utils import run_bass_kernel_spmd

F32 = mybir.dt.float32
BF16 = mybir.dt.bfloat16
AF = mybir.ActivationFunctionType
ALU = mybir.AluOpType

ENGS = ("pe", "act", "dve", "pool", "sp")
SMALL_N = 256


def region(ap):
    t = ap.tensor
    dims = ap.ap
    off = int(ap.offset)
    name = t.name
    cls = type(t).__name__
    if cls.startswith("DRam"):
        lo = off
        hi = off + sum((c - 1) * abs(s) for s, c in dims) + 1
        return (name, 0, 1, lo, hi)
    if cls.startswith("PSum"):
        return (name, 0, 128, 0, 1 << 30)
    ps, npart = dims[0]
    p0 = off // ps
    f0 = off % ps
    f1 = f0 + sum((c - 1) * abs(s) for s, c in dims[1:]) + 1
    return (name, p0, p0 + npart, f0, f1)


class Op:
    __slots__ = ("eng", "fn", "idx", "deps", "signaled", "dma_slot", "dma_cnt", "nfree", "sig_no")

    def __init__(self, eng, fn):
        self.eng = eng
        self.fn = fn
        self.deps = {}
        self.signaled = False
        self.dma_slot = None
        self.dma_cnt = 0
        self.nfree = 1 << 30
        self.sig_no = 0


class Prog:
    def __init__(self, nc):
        self.nc = nc
        self.ops = []
        self.recs = {}
        self.slot_last = {}
        self.slot_cnt = {}
        self.barrier_idx = {}

    def _track(self, op, reads, writes):
        deps = op.deps
        ekey = op.eng if op.dma_slot is None else ("dma", op.dma_slot)

        def add_dep(r):
            k = r[5]
            if k == ekey and op.dma_slot is None:
                if k == "pe":
                    return
            if deps.get(k, -1) < r[6]:
                deps[k] = r[6]

        psum_reads = [ap for ap in reads if type(ap.tensor).__name__.startswith("PSum")]
        if psum_reads:
            reads = [ap for ap in reads if not type(ap.tensor).__name__.startswith("PSum")]
            writes = list(writes) + psum_reads
        for ap in reads:
            name, p0, p1, f0, f1 = region(ap)
            lst = self.recs.setdefault(name, [])
            found = None
            for r in lst:
                if r[4] == "w":
                    if r[0] < p1 and p0 < r[1] and r[2] < f1 and f0 < r[3]:
                        add_dep(r)
                elif r[5] == ekey and r[0] == p0 and r[1] == p1 and r[2] == f0 and r[3] == f1:
                    found = r
            if found is not None:
                found[6] = op.idx
            else:
                lst.append([p0, p1, f0, f1, "r", ekey, op.idx, op.nfree])
        for ap in writes:
            name, p0, p1, f0, f1 = region(ap)
            lst = self.recs.setdefault(name, [])
            keep = []
            for r in lst:
                if r[0] < p1 and p0 < r[1] and r[2] < f1 and f0 < r[3]:
                    if r[6] != op.idx:
                        add_dep(r)
                    if r[0] >= p0 and r[1] <= p1 and r[2] >= f0 and r[3] <= f1 and r[6] != op.idx:
                        continue
                keep.append(r)
            keep.append([p0, p1, f0, f1, "w", ekey, op.idx, op.nfree])
            self.recs[name] = keep

    def add(self, eng, fn, reads=(), writes=(), nfree=None):
        op = Op(eng, fn)
        op.idx = len(self.ops)
        if nfree is None:
            nfree = 1 << 30
            for ap in writes:
                d = ap.ap
                n = 1
                for s, c in d[1:]:
                    n *= c
                nfree = min(nfree, n)
        op.nfree = nfree
        self.ops.append(op)
        self._track(op, reads, writes)
        return op

    def dma(self, out, in_, slot, eng="sp", **kw):
        def fn(e, out=out, in_=in_, kw=kw):
            return e.dma_start(out=out, in_=in_, **kw)

        op = Op(eng, fn)
        op.idx = len(self.ops)
        op.dma_slot = slot
        self.slot_cnt[slot] = self.slot_cnt.get(slot, 0) + 16
        op.dma_cnt = self.slot_cnt[slot]
        self.ops.append(op)
        prev = self.slot_last.get(slot)
        if prev is not None:
            op.deps[("dma", slot)] = prev
        self.slot_last[slot] = op.idx
        self._track(op, [in_], [out])
        return op

    def mm(self, out, lhsT, rhs, start=True, stop=True, **kw):
        if not start:
            kw.setdefault("skip_group_check", True)
        return self.add("pe", lambda e: e.matmul(out, lhsT, rhs, start=start, stop=stop, **kw),
                        [lhsT, rhs], [out])

    def tr(self, out, in_, ident):
        return self.add("pe", lambda e: e.transpose(out, in_, ident), [in_, ident], [out])

    def act(self, out, in_, func, bias=None, scale=None, accum_out=None, eng="act"):
        reads = [in_]
        kw = {}
        if bias is not None:
            kw["bias"] = bias
            if not isinstance(bias, (int, float)):
                reads.append(bias)
        if scale is not None:
            kw["scale"] = scale
            if not isinstance(scale, (int, float)):
                reads.append(scale)
        writes = [out]
        if accum_out is not None:
            kw["accum_out"] = accum_out
            writes.append(accum_out)
        return self.add(eng, lambda e: e.activation(out, in_, func, **kw), reads, writes)

    def tt(self, eng, out, in0, in1, op):
        return self.add(eng, lambda e: e.tensor_tensor(out, in0, in1, op), [in0, in1], [out])

    def ts(self, eng, out, in0, s1, s2, op0, op1=None, accum_out=None):
        reads = [in0]
        if not isinstance(s1, (int, float)):
            reads.append(s1)
        if s2 is not None and not isinstance(s2, (int, float)):
            reads.append(s2)
        kw = {}
        writes = [out]
        if accum_out is not None:
            kw["accum_out"] = accum_out
            writes.append(accum_out)
        if op1 is None:
            return self.add(eng, lambda e: e.tensor_scalar(out, in0, s1, None, op0, **kw), reads, writes)
        return self.add(eng, lambda e: e.tensor_scalar(out, in0, s1, s2, op0, op1, **kw), reads, writes)

    def stt(self, eng, out, in0, scalar, in1, op0, op1):
        reads = [in0, in1]
        if not isinstance(scalar, (int, float)):
            reads.append(scalar)
        return self.add(eng, lambda e: e.scalar_tensor_tensor(out, in0, scalar, in1, op0, op1), reads, [out])

    def scale(self, eng, out, in_, sc):
        if eng == "act":
            return self.act(out, in_, AF.Copy, scale=sc)
        return self.ts(eng, out, in_, sc, None, ALU.mult)

    def copy(self, eng, out, in_):
        if eng == "act":
            return self.add(eng, lambda e: e.copy(out, in_), [in_], [out])
        return self.add(eng, lambda e: e.tensor_copy(out, in_), [in_], [out])

    def memset(self, eng, ap, val):
        return self.add(eng, lambda e: e.memset(ap, val), [], [ap])

    def barrier(self, final=False):
        op = Op("sp", None)
        op.idx = len(self.ops)
        self.ops.append(op)
        for slot, i in self.slot_last.items():
            op.deps[("dma", slot)] = i
        if not final:
            m = Op("marker", None)
            m.idx = len(self.ops)
            self.ops.append(m)
        self.recs = {}

    def emit(self):
        nc = self.nc
        ops = self.ops
        for op in ops:
            for k, i in op.deps.items():
                ops[i].signaled = True
        cnt = {e: 0 for e in ENGS}
        for op in ops:
            if op.eng == "marker":
                continue
            if op.dma_slot is None and op.signaled:
                cnt[op.eng] += 1
                op.sig_no = cnt[op.eng]
        slots = sorted(self.slot_cnt.keys())
        import contextlib

        with contextlib.ExitStack() as st:
            esem = {e: st.enter_context(nc.semaphore("s_" + e)) for e in ENGS}
            ssem = {s: st.enter_context(nc.semaphore("d_" + s)) for s in slots}
            segs = [[]]
            for op in ops:
                if op.eng == "marker":
                    segs.append([])
                else:
                    segs[-1].append(op)
            stats = {e: [0, 0] for e in ENGS}
            waited_all = {e: {} for e in ENGS}

            def run(eng_name, e, seg):
                waited = waited_all[eng_name]
                for op in seg:
                    if op.eng != eng_name:
                        continue
                    for k, i in op.deps.items():
                        p = ops[i]
                        if isinstance(k, tuple):
                            sem, val = ssem[k[1]], p.dma_cnt
                        else:
                            sem, val = esem[k], p.sig_no
                        if waited.get(k, 0) >= val:
                            continue
                        waited[k] = val
                        e.wait_ge(sem, val)
                        stats[eng_name][1] += 1
                    if op.fn is None:
                        continue
                    ins = op.fn(e)
                    stats[eng_name][0] += 1
                    if op.dma_slot is not None:
                        ins.then_inc(ssem[op.dma_slot], 16)
                    elif op.signaled:
                        ins.then_inc(esem[eng_name], 1)

            for seg in segs:
                with nc.Block() as block:
                    @block.tensor
                    def _(e, seg=seg):
                        run("pe", e, seg)

                    @block.scalar
                    def _(e, seg=seg):
                        run("act", e, seg)

                    @block.vector
                    def _(e, seg=seg):
                        run("dve", e, seg)

                    @block.gpsimd
                    def _(e, seg=seg):
                        run("pool", e, seg)

                    @block.sync
                    def _(e, seg=seg):
                        run("sp", e, seg)

            self.stats = stats
            self.sig_counts = cnt


D = 1024
KC = 8
S_FULL = 4096
C_Z, C_X, C_B, C_C, C_DT, C_Q, C_K, C_V, C_GS, C_GA = 0, 2048, 4096, 5120, 6144, 6176, 7712, 9248, 10784, 11808
DIL = (1, 4, 16)
EPS = 1e-6
DFF = 2816
NFC = 22
PV_G1, PV_G2, PV_GS, PV_CW, PV_CB, PV_D, PV_QG, PV_KG, PV_FW, PV_FB, NPV = 0, 8, 16, 32, 160, 192, 208, 209, 210, 342, 386
CM_ID, CM_U, CM_L, CM_GT, CM_ONE, CM_AM, CM_BD, CM_NEG, NCM = 0, 128, 256, 384, 512, 640, 1152, 1280, 1536

_bf = ml_dtypes.bfloat16


def _fm(v):
    return np.ascontiguousarray(np.asarray(v).reshape(-1, 128).T)


def pack_pvec(inp):
    pv = np.zeros((128, NPV), np.float32)
    pv[:, PV_G1:PV_G1 + 8] = _fm(inp["norm1_g"][0])
    pv[:, PV_G2:PV_G2 + 8] = _fm(inp["norm2_g"][0])
    pv[:, PV_GS:PV_GS + 16] = _fm(inp["ssd_norm_g"][0])
    for i in range(4):
        pv[:, PV_CW + i * 32:PV_CW + (i + 1) * 32] = _fm(inp["ssd_conv_w"][0, i])
    pv[:, PV_CB:PV_CB + 32] = _fm(inp["ssd_conv_b"][0])
    pv[:, PV_D:PV_D + 16] = _fm(np.repeat(inp["d_skip"][0], 64))
    pv[:, PV_QG] = np.tile(inp["q_norm_g"][0], 2)
    pv[:, PV_KG] = np.tile(inp["k_norm_g"][0], 2)
    for i in range(3):
        pv[:, PV_FW + i * 44:PV_FW + (i + 1) * 44] = _fm(inp["ffn_conv_w"][0, i])
    pv[:, PV_FB:PV_FB + 44] = _fm(inp["ffn_conv_b"][0])
    return pv


def make_cmask():
    cm = np.zeros((128, NCM), np.float32)
    i = np.arange(128)
    U = (i[:, None] <= i[None, :]).astype(np.float32)
    L = (i[:, None] >= i[None, :]).astype(np.float32)
    GT = (i[:, None] > i[None, :]).astype(np.float32)
    cm[:, CM_ID:CM_ID + 128] = np.eye(128)
    cm[:, CM_U:CM_U + 128] = U
    cm[:, CM_L:CM_L + 128] = L
    cm[:, CM_GT:CM_GT + 128] = GT
    cm[:, CM_ONE:CM_ONE + 128] = 1.0
    cm[:, CM_AM:CM_AM + 512] = np.concatenate([L, U, L, U], 1)
    bd = np.zeros((128, 128), np.float32)
    bd[:64, :64] = 1.0
    bd[64:, 64:] = 1.0
    cm[:, CM_BD:CM_BD + 128] = bd
    NEG = -30000.0
    cm[:, CM_NEG:CM_NEG + 256] = np.concatenate([(1.0 - L) * NEG, (1.0 - U) * NEG], 1)
    return cm.astype(_bf)


def build(NS=2, S=S_FULL, debug=False, phases=("norm", "attn", "ssd", "merge", "ffn")):
    nc = bass.Bass("TRN2", target_bir_lowering=False)
    NT = S // 512
    NB = S // 128

    def din(name, shape, dt=F32):
        return nc.dram_tensor(name, shape, dt, kind="ExternalInput").ap()

    x_d = din("x", [NS, S, D])
    w_in_d = din("w_in", [D, 12832])
    w_ssd_d = din("w_ssd_proj", [2048, D])
    w_att_d = din("w_attn_proj", [512, D])
    w_out_d = din("w_out", [D, D])
    w_up_d = din("w_up", [D, 2 * DFF])
    w_dn_d = din("w_down", [DFF, D])
    pv_d = din("pvec", [128, NPV])
    cm_d = din("cmask", [128, NCM], BF16)
    dtb_d = din("dt_bias", [1, 32])
    alog_d = din("a_log", [1, 32])
    skind = "ExternalOutput" if debug else "Internal"
    hTs_d = nc.dram_tensor("hTs", [NS, KC, 128, S], BF16, kind=skind).ap()
    yat_d = nc.dram_tensor("yattn", [NS, 4, 128, S], BF16, kind=skind).ap()
    ysd_d = nc.dram_tensor("yssd", [NS, 16, 128, S], BF16, kind=skind).ap()
    out_d = nc.dram_tensor("out", [NS, S, D], F32, kind="ExternalOutput").ap()

    P = Prog(nc)
    ES = contextlib.ExitStack

    with ES() as top:
        def sbt(st, name, shape, dt):
            return st.enter_context(nc.sbuf_tensor(name, shape, dt))

        def pst(st, name, dt=F32):
            return st.enter_context(nc.psum_tensor(name, [128, 512 if dt == F32 else 1024], dt))

        pv = sbt(top, "pv", [128, NPV], F32)
        pvh = sbt(top, "pvh", [128, NPV], F32)
        cm = sbt(top, "cm", [128, NCM], BF16)
        P.dma(pv[:], pv_d[:, :], "pv")
        P.dma(cm[:], cm_d[:, :], "cm")
        P.ts("dve", pvh[:], pv[:], 0.5, None, ALU.mult)
        ident = cm[:, CM_ID:CM_ID + 128]
        maskU = cm[:, CM_U:CM_U + 128]
        maskGT = cm[:, CM_GT:CM_GT + 128]
        ones = cm[:, CM_ONE:CM_ONE + 128]
        amask = cm[:, CM_AM:CM_AM + 512]
        bdones = cm[:, CM_BD:CM_BD + 128]
        negm = cm[:, CM_NEG:CM_NEG + 256]
        neg1 = sbt(top, "neg1", [128, 2], F32)
        P.memset("dve", neg1[:, 0:1], -1.0)
        P.memset("dve", neg1[:, 1:2], -0.5)
        stg = [sbt(top, f"stg{i}", [128, 512], F32) for i in range(2)]
        stg_i = [0, 0]

        def load_w(dst, src, kn, ncols, scale=None, engs=("act", "dve")):
            SW = 512
            cw = min(ncols, SW)
            kpp = max(1, SW // cw)
            for c0 in range(0, ncols, cw):
                cn = min(cw, ncols - c0)
                for k0 in range(0, kn, kpp):
                    kk = min(kpp, kn - k0)
                    i = stg_i[0] % 2
                    stg_i[0] += 1
                    sv = stg[i][:, 0:kk * cn].rearrange("p (k n) -> p k n", k=kk)
                    P.dma(sv, src[k0 * 128:(k0 + kk) * 128, c0:c0 + cn].rearrange("(k p) n -> p k n", p=128), f"stg{i}")
                    for k in range(kk):
                        d_ = dst[:, k0 + k, c0:c0 + cn]
                        eng = engs[stg_i[1] % len(engs)]
                        stg_i[1] += 1
                        if scale is None:
                            P.copy(eng, d_, sv[:, k, :])
                        elif isinstance(scale, float):
                            P.scale(eng, d_, sv[:, k, :], scale)
                        else:
                            P.scale(eng, d_, sv[:, k, :], scale[:, k0 + k:k0 + k + 1])

        g1 = pv[:, PV_G1:PV_G1 + 8]
        g1h = pvh[:, PV_G1:PV_G1 + 8]

        for b in range(NS):
            with ES() as seq:
                hT = sbt(seq, f"hT{b}", [128, KC, S], BF16)
                if "norm" in phases:
                    with ES() as st:
                        xt = [sbt(st, f"n_xt{b}_{i}", [128, 4, D], F32) for i in range(2)]
                        junk = sbt(st, f"n_junk{b}", [128, D], BF16)
                        h16 = [sbt(st, f"n_h16{b}_{i}", [128, 4, D], BF16) for i in range(2)]
                        ssq = [sbt(st, f"n_ss{b}_{i}", [128, 4], F32) for i in range(2)]
                        lnv = [sbt(st, f"n_ln{b}_{i}", [128, 4], F32) for i in range(2)]
                        rstd = [sbt(st, f"n_rs{b}_{i}", [128, 4], F32) for i in range(2)]
                        pb = [pst(st, f"n_pb{b}_{i}", BF16) for i in range(4)]
                        for j in range(NT):
                            i2 = j % 2
                            P.dma(xt[i2][:], x_d[b, 512 * j:512 * (j + 1), :].rearrange("(k p) d -> p k d", p=128), f"xt{i2}")
                            P.memset("pool", ssq[i2][:], 0.0)
                            for k in range(4):
                                P.act(junk[:], xt[i2][:, k, :], AF.Square, accum_out=ssq[i2][:, k:k + 1])
                            P.act(lnv[i2][:], ssq[i2][:], AF.Ln, scale=1.0 / D, bias=EPS)
                            P.act(rstd[i2][:], lnv[i2][:], AF.Exp, scale=-0.5)
                            for k in range(4):
                                P.scale("dve" if k % 2 == 0 else "act", h16[i2][:, k, :], xt[i2][:, k, :], rstd[i2][:, k:k + 1])
                            for pr in range(4):
                                bank = pb[pr]
                                for kk in range(2):
                                    kc = 2 * pr + kk
                                    for k in range(4):
                                        P.tr(bank[:, kk * 512 + k * 128:kk * 512 + (k + 1) * 128], h16[i2][:, k, kc * 128:(kc + 1) * 128], ident)
                                P.copy("act" if pr % 2 == 0 else "dve", hT[:, 2 * pr:2 * pr + 2, 512 * j:512 * (j + 1)],
                                       bank[:, :].rearrange("p (k t) -> p k t", k=2))
                            P.dma(hTs_d[b, :, :, 512 * j:512 * (j + 1)].rearrange("k p t -> p k t"), hT[:, :, 512 * j:512 * (j + 1)], f"hs{i2}")
                P.barrier()
                if "attn" in phases:
                    with ES() as st:
                        wq = [sbt(st, f"a_wq{b}_{i}", [128, KC, 128], BF16) for i in range(2)]
                        wk = [sbt(st, f"a_wk{b}_{i}", [128, KC, 128], BF16) for i in range(2)]
                        wv = [sbt(st, f"a_wv{b}_{i}", [128, KC, 128], BF16) for i in range(2)]
                        QT = sbt(st, f"a_QT{b}", [128, S], BF16)
                        KT = sbt(st, f"a_KT{b}", [128, S], BF16)
                        Vt = sbt(st, f"a_Vt{b}", [128, NB, 128], BF16)
                        lnb = sbt(st, f"a_ln{b}", [128, 2 * NT, 512], F32)
                        sq = [sbt(st, f"a_sq{b}_{i}", [128, 512], BF16) for i in range(2)]
                        pt = [sbt(st, f"a_pt{b}_{i}", [128, 512], BF16) for i in range(2)]
                        acc = sbt(st, f"a_acc{b}", [128, 2, S], F32)
                        rden = [sbt(st, f"a_rd{b}_{i}", [128, 512], F32) for i in range(2)]
                        yat = [sbt(st, f"a_yat{b}_{i}", [128, S], BF16) for i in range(2)]
                        bk = [pst(st, f"a_bk{b}_{i}") for i in range(8)]
                        ps_p = [bk[0], bk[1]]
                        ps_n = bk[7]
                        ps_v = bk[6]
                        ps_s = [(bk[0], bk[1]), (bk[2], bk[3])]
                        ps_o = [bk[4], bk[5]]
                        def load_attn_w(it_):
                            hp_i, g_i = divmod(it_, 3)
                            slot = 0
                            for (wt, c0) in ((wq[it_ % 2], C_Q), (wk[it_ % 2], C_K), (wv[it_ % 2], C_V)):
                                cc = c0 + g_i * 512 + hp_i * 128
                                for k0 in (0, 4):
                                    sv = lnb[:, slot, :].rearrange("p (k n) -> p k n", k=4)
                                    P.dma(sv, w_in_d[k0 * 128:(k0 + 4) * 128, cc:cc + 128].rearrange("(k p) n -> p k n", p=128), f"aw{slot}")
                                    for k in range(4):
                                        P.scale("act" if (k + slot) % 2 == 0 else "dve", wt[:, k0 + k, :], sv[:, k, :], g1[:, k0 + k:k0 + k + 1])
                                    slot += 1

                        load_attn_w(0)
                        it = 0
                        for hp in range(4):
                            for g in range(3):
                                d = DIL[g]
                                L = S // d
                                nbl = L // 128
                                wi = it % 2
                                it += 1

                                def nat_view(T, j):
                                    if d == 1:
                                        return T[:, 512 * j:512 * (j + 1)]
                                    return T[:, :].rearrange("p (r m) -> p r m", r=d)[:, :, (512 * j) // d:(512 * (j + 1)) // d]

                                def nat_src(a):
                                    if d == 1:
                                        return a
                                    return a.rearrange("p (m r) -> p r m", r=d)

                                ps_nn = [bk[5], bk[7]]
                                items = [(wt, T, j) for (wt, T) in ((wq[wi], QT), (wk[wi], KT)) for j in range(NT)]

                                def qk1(idx):
                                    wt, T, j = items[idx]
                                    pp = ps_p[idx % 2]
                                    for kc in range(KC):
                                        P.mm(pp[:, 0:512], wt[:, kc, :], hT[:, kc, 512 * j:512 * (j + 1)], start=(kc == 0))
                                    P.act(sq[idx % 2][:], pp[:, 0:512], AF.Square)
                                    P.copy("dve", nat_view(T, j), nat_src(pp[:, 0:512]))
                                    P.mm(ps_nn[idx % 2][:, 0:512], bdones, sq[idx % 2][:], start=True)

                                def qk2(idx):
                                    P.act(lnb[:, idx, :], ps_nn[idx % 2][:, 0:512], AF.Ln, scale=1.0 / 64, bias=EPS)

                                qk1(0)
                                for idx in range(len(items)):
                                    if idx + 1 < len(items):
                                        qk1(idx + 1)
                                    qk2(idx)
                                idx = 0
                                for (T, gcol) in ((QT, PV_QG), (KT, PV_KG)):
                                    for j in range(NT):
                                        P.act(lnb[:, idx, :], lnb[:, idx, :], AF.Exp, scale=-0.5)
                                        P.stt("dve", nat_view(T, j), nat_view(T, j), pv[:, gcol:gcol + 1], nat_src(lnb[:, idx, :]), ALU.mult, ALU.mult)
                                        idx += 1
                                if it < 12:
                                    load_attn_w(it)
                                for bi in range(NB):
                                    r, n = bi // nbl, bi % nbl
                                    t0 = r + d * 128 * n
                                    for kc in range(KC):
                                        P.mm(ps_v[:, (bi % 4) * 128:(bi % 4 + 1) * 128], hT[:, kc, t0:t0 + d * 127 + 1:d], wv[wi][:, kc, :],
                                             start=(kc == 0 and bi % 4 == 0))
                                    if bi % 4 == 3:
                                        P.copy("act", Vt[:, bi - 3:bi + 1, :], ps_v[:, :].rearrange("p (k c) -> p k c", k=4))
                                def Sblk(bi):
                                    r, n = bi // nbl, bi % nbl
                                    c0 = r * L + 128 * n
                                    hp_ = n > 0
                                    ptb = pt[bi % 2]
                                    for h in range(2):
                                        pss = ps_s[bi % 2][h]
                                        rows = slice(64 * h, 64 * h + 64)
                                        if hp_:
                                            P.mm(pss[:, 0:128], KT[rows, c0 - 128:c0], QT[rows, c0:c0 + 128], start=True)
                                        P.mm(pss[:, 128:256], KT[rows, c0:c0 + 128], QT[rows, c0:c0 + 128], start=not hp_)
                                        lo = 0 if hp_ else 128
                                        P.mm(pss[:, lo:256], ident, negm[:, lo:256], start=False)
                                        P.act(ptb[:, h * 256 + lo:h * 256 + 256], pss[:, lo:256], AF.Exp, scale=0.125)

                                def Vblk(bi):
                                    r, n = bi // nbl, bi % nbl
                                    hp_ = n > 0
                                    ptb = pt[bi % 2]
                                    pso = ps_o[bi % 2]
                                    for h in range(2):
                                        rows = slice(64 * h, 64 * h + 64)
                                        kw = {} if h == 0 else {"tile_position": (0, 64)}
                                        first = True
                                        for (lh, ncol) in ((None, 0), (ones, 128)):
                                            for pc in ((0, 1) if hp_ else (1,)):
                                                vb = bi - 1 if pc == 0 else bi
                                                lhsT = Vt[:, vb, 64 * h:64 * h + 64] if lh is None else ones[:, 0:64]
                                                P.mm(pso[rows, ncol:ncol + 128], lhsT, ptb[:, h * 256 + pc * 128:h * 256 + pc * 128 + 128],
                                                     start=first, **kw)
                                                first = False
                                    t0 = r + d * 128 * n
                                    av = acc[:, :, t0:t0 + d * 127 + 1:d]
                                    pv3 = pso[:, 0:256].rearrange("p (c q) -> p c q", c=2)
                                    if g == 0:
                                        P.copy("dve", av, pv3)
                                    else:
                                        P.tt("dve", av, pv3, av, ALU.add)

                                Sblk(0)
                                for bi in range(NB):
                                    if bi + 1 < NB:
                                        Sblk(bi + 1)
                                    Vblk(bi)
                            yb = yat[hp % 2]
                            for j in range(NT):
                                sl = slice(512 * j, 512 * (j + 1))
                                P.add("dve", lambda e, o=rden[j % 2][:], i=acc[:, 1, sl]: e.reciprocal(o, i), [acc[:, 1, sl]], [rden[j % 2][:]])
                                P.tt("pool", yb[:, sl], acc[:, 0, sl], rden[j % 2][:], ALU.mult)
                            P.dma(yat_d[b, hp, :, :], yb[:, :], f"yat{hp % 2}")
                P.barrier()
                if "ssd" in phases:
                    with ES() as st:
                        wdt = sbt(st, f"s_wdt{b}", [128, KC, 32], BF16)
                        dtb = sbt(st, f"s_dtb{b}", [128, 32], F32)
                        alog = sbt(st, f"s_alog{b}", [128, 32], F32)
                        xdt = sbt(st, f"s_xdt{b}", [128, NB, 32], F32)
                        tmpa = sbt(st, f"s_tmpa{b}", [128, NB, 32], F32)
                        dt_all = sbt(st, f"s_dt{b}", [128, NB, 32], F32)
                        a32 = sbt(st, f"s_a32{b}", [128, NB, 32], F32)
                        a_hi = sbt(st, f"s_ahi{b}", [128, NB, 32], BF16)
                        a_lo = sbt(st, f"s_alo{b}", [128, NB, 32], BF16)
                        wg = [sbt(st, f"s_wg{b}_{i}", [128, KC, 768], BF16) for i in range(2)]
                        raw = [sbt(st, f"s_raw{b}_{i}", [128, 515], F32) for i in range(4)]
                        accb = [sbt(st, f"s_acc{b}_{i}", [128, 512], F32) for i in range(4)]
                        thb = [sbt(st, f"s_th{b}_{i}", [128, 512], F32) for i in range(2)]
                        xbcT = [sbt(st, f"s_xbcT{b}_{i}", [128, 4, 512], BF16) for i in range(2)]
                        zs = [sbt(st, f"s_zs{b}_{i}", [128, 2, 512], BF16) for i in range(2)]
                        xbtok = [sbt(st, f"s_xbtok{b}_{i}", [128, 4, 384], BF16) for i in range(2)]
                        Xdt = [sbt(st, f"s_Xdt{b}_{i}", [128, 4, 256], BF16) for i in range(2)]
                        xsD = [sbt(st, f"s_xsD{b}_{i}", [128, 2, 512], F32) for i in range(2)]
                        Rh = [sbt(st, f"s_Rh{b}_{i}", [128, 512], BF16) for i in range(4)]
                        Rl = [sbt(st, f"s_Rl{b}_{i}", [128, 512], BF16) for i in range(4)]
                        Eb = [sbt(st, f"s_E{b}_{i}", [128, 512], BF16) for i in range(2)]
                        EAb = [sbt(st, f"s_EA{b}_{i}", [128, 512], BF16) for i in range(2)]
                        Mp = [sbt(st, f"s_Mp{b}_{i}", [128, 512], BF16) for i in range(2)]
                        Cs = [sbt(st, f"s_Cs{b}_{i}", [128, 512], BF16) for i in range(2)]
                        CBm = [sbt(st, f"s_CBm{b}_{i}", [128, 128], BF16) for i in range(2)]
                        Xd = [sbt(st, f"s_Xd{b}_{i}", [128, 256], BF16) for i in range(2)]
                        ddc = [sbt(st, f"s_dd{b}_{i}", [128, 12], F32) for i in range(2)]
                        Sst = sbt(st, f"s_S{b}", [128, 256], F32)
                        Sbf = [sbt(st, f"s_Sbf{b}_{i}", [128, 256], BF16) for i in range(2)]
                        ybuf = [sbt(st, f"s_yb{b}_{i}", [128, 2, 512], F32) for i in range(2)]
                        sqw = sbt(st, f"s_sq{b}", [128, 2, 512], BF16)
                        lnw = sbt(st, f"s_lnw{b}", [128, 512], F32)
                        yo = [sbt(st, f"s_yo{b}_{i}", [128, 2, 512], BF16) for i in range(2)]
                        ps_p = [pst(st, f"s_psp{b}_{i}") for i in range(2)]
                        ps_seg = pst(st, f"s_seg{b}")
                        ps_acs = pst(st, f"s_acs{b}")
                        ps_cb = pst(st, f"s_cb{b}")
                        ps_st = pst(st, f"s_st{b}")
                        ps_y = pst(st, f"s_y{b}")
                        ps_tr = pst(st, f"s_tr{b}", BF16)
                        load_w(wdt, w_in_d[:, C_DT:C_DT + 32], KC, 32, scale=g1)
                        P.dma(dtb[:], dtb_d[:, :].partition_broadcast(128), "dtb")
                        P.dma(alog[:], alog_d[:, :].partition_broadcast(128), "alog")
                        for half in range(NB // 16):
                            pp = ps_p[half % 2]
                            for bl in range(16):
                                blk = half * 16 + bl
                                for kc in range(KC):
                                    P.mm(pp[:, bl * 32:(bl + 1) * 32], hT[:, kc, blk * 128:(blk + 1) * 128], wdt[:, kc, :],
                                         start=(bl == 0 and kc == 0))
                            P.tt("dve", xdt[:, half * 16:(half + 1) * 16, :], pp[:, :].rearrange("p (k h) -> p k h", k=16),
                                 dtb[:, :].unsqueeze(1).broadcast_to([128, 16, 32]), ALU.add)
                        P.act(tmpa[:], xdt[:], AF.Abs)
                        P.act(tmpa[:], tmpa[:], AF.Exp, scale=-1.0)
                        P.act(alog[:], alog[:], AF.Exp)
                        P.act(a32[:], tmpa[:], AF.Ln, bias=1.0)
                        P.stt("dve", dt_all[:], xdt[:], 0.0, a32[:], ALU.max, ALU.add)
                        P.tt("dve", a32[:], dt_all[:], alog[:, :].unsqueeze(1).broadcast_to([128, NB, 32]), ALU.mult)
                        P.ts("dve", a32[:], a32[:], -1.0, None, ALU.mult)
                        P.copy("dve", a_hi[:], a32[:])
                        P.tt("dve", tmpa[:], a32[:], a_hi[:], ALU.subtract)
                        P.copy("dve", a_lo[:], tmpa[:])
                        cwh = lambda i, c: pvh[:, PV_CW + i * 32 + c:PV_CW + i * 32 + c + 1]
                        cbh = lambda c: pvh[:, PV_CB + c:PV_CB + c + 1]
                        cnt = [0]
                        v4 = lambda a: a.rearrange("p (k l) -> p k l", k=4)

                        def load_group(g):
                            wgi = wg[g % 2]
                            load_w(wgi[:, :, 0:256], w_in_d[:, C_Z + g * 256:C_Z + (g + 1) * 256], KC, 256, scale=g1h)
                            load_w(wgi[:, :, 256:512], w_in_d[:, C_X + g * 256:C_X + (g + 1) * 256], KC, 256, scale=g1)
                            load_w(wgi[:, :, 512:640], w_in_d[:, C_B + g * 128:C_B + (g + 1) * 128], KC, 128, scale=g1)
                            load_w(wgi[:, :, 640:768], w_in_d[:, C_C + g * 128:C_C + (g + 1) * 128], KC, 128, scale=g1)

                        pc_pp = {}

                        def PCm(W, i):
                            g, w = divmod(W, NT)
                            wgi = wg[g % 2]
                            tsl = slice(512 * w, 512 * (w + 1))
                            pp = ps_p[cnt[0] % 2]
                            cnt[0] += 1
                            pc_pp[(W, i)] = pp
                            c0 = i * 128 if i < 2 else 256 + (i - 2) * 128
                            for kc in range(KC):
                                P.mm(pp[:, 0:512], wgi[:, kc, c0:c0 + 128], hT[:, kc, tsl], start=(kc == 0))

                        def PCe(W, i):
                            g, w = divmod(W, NT)
                            wi = W % 2
                            pp = pc_pp[(W, i)]
                            if i < 2:
                                P.act(thb[i][:], pp[:, 0:512], AF.Tanh)
                                P.stt("dve", zs[wi][:, i, :], thb[i][:], 1.0, pp[:, 0:512], ALU.add, ALU.mult)
                                return
                            ct = i - 2
                            cchunk = (2 * g, 2 * g + 1, 16 + g, 24 + g)
                            if w == 0:
                                P.memset("pool", raw[ct][:, 0:3], 0.0)
                            cc = cchunk[ct]
                            ab = accb[ct]
                            P.copy("act", raw[ct][:, 3:515], pp[:, 0:512])
                            P.act(ab[:], pp[:, 0:512], AF.Identity, scale=cwh(3, cc), bias=cbh(cc))
                            for k_ in range(3):
                                P.stt("dve", ab[:], raw[ct][:, k_:k_ + 512], cwh(k_, cc), ab[:], ALU.mult, ALU.add)
                            P.copy("pool", raw[ct][:, 0:3], raw[ct][:, 512:515])

                        def PCf(W, i):
                            wi = W % 2
                            pc_pp.pop((W, i))
                            if i < 2:
                                return
                            ct = i - 2
                            ab = accb[ct]
                            tb = thb[ct % 2]
                            P.act(tb[:], ab[:], AF.Tanh)
                            P.stt("dve", xbcT[wi][:, ct, :], tb[:], 1.0, ab[:], ALU.add, ALU.mult)

                        def PCt(W):
                            g, w = divmod(W, NT)
                            wi = W % 2
                            for bp in range(2):
                                for k2 in range(2):
                                    blk = 2 * bp + k2
                                    for (ci_, ct) in enumerate((0, 1, 2)):
                                        P.tr(ps_tr[:, k2 * 384 + ci_ * 128:k2 * 384 + (ci_ + 1) * 128],
                                             xbcT[wi][:, ct, blk * 128:(blk + 1) * 128], ident)
                                P.copy("act", xbtok[wi][:, 2 * bp:2 * bp + 2, :], ps_tr[:, 0:768].rearrange("p (k c) -> p k c", k=2))
                            P.tt("dve", Xdt[wi][:, :, :].rearrange("p b (k q) -> p b k q", k=4),
                                 xbtok[wi][:, :, 0:256].rearrange("p b (k q) -> p b k q", k=4),
                                 dt_all[:, 4 * w:4 * w + 4, 4 * g:4 * g + 4].unsqueeze(3).broadcast_to([128, 4, 4, 64]), ALU.mult)
                            for hc in range(2):
                                P.act(xsD[wi][:, hc, :], xbcT[wi][:, hc, :], AF.Copy, scale=pv[:, PV_D + 2 * g + hc:PV_D + 2 * g + hc + 1])

                        def PC(W):
                            for i in range(6):
                                PCm(W, i)
                                PCe(W, i)
                                PCf(W, i)
                            PCt(W)

                        def RG(n):
                            g, cidx = divmod(n, NB)
                            for (R_, asrc) in ((Rh[n % 4], a_hi),):
                                P.tt("pool", v4(R_[:, :]), maskU.unsqueeze(1).broadcast_to([128, 4, 128]),
                                     asrc[:, cidx, 4 * g:4 * g + 4].unsqueeze(2).broadcast_to([128, 4, 128]), ALU.mult)

                        def A(n):
                            g, cidx = divmod(n, NB)
                            W = n // 4
                            wi = W % 2
                            c4 = cidx % 4
                            ci = n % 2
                            tc = slice(128 * c4, 128 * (c4 + 1))
                            BT = xbcT[wi][:, 2, tc]
                            CT = xbcT[wi][:, 3, tc]
                            P.mm(ps_seg[:, 0:512], maskGT, Rh[n % 4][:, :], start=True)
                            P.mm(ps_acs[:, 0:512], ones, Rh[n % 4][:, :], start=True)
                            P.mm(ps_cb[:, 0:128], BT, CT, start=True)
                            dd = ddc[ci]
                            P.act(Eb[ci][:, :], ps_seg[:, 0:512], AF.Exp)
                            P.act(dd[:, 0:4], ps_seg[:, 127:512:128], AF.Exp)
                            P.act(EAb[ci][:, :], ps_acs[:, 0:512], AF.Exp)
                            P.act(dd[:, 4:8], ps_acs[:, 127:512:128], AF.Exp)
                            P.tt("dve", CBm[ci][:, :], ps_cb[:, 0:128], maskU, ALU.mult)
                            P.tt("dve", v4(Mp[ci][:, :]), v4(Eb[ci][:, :]), CBm[ci][:, :].unsqueeze(1).broadcast_to([128, 4, 128]), ALU.mult)
                            P.tt("pool", v4(Cs[ci][:, :]), CT.unsqueeze(1).broadcast_to([128, 4, 128]), v4(EAb[ci][:, :]), ALU.mult)
                            P.tt("dve", Xd[ci][:, :].rearrange("p (k q) -> p k q", k=4),
                                 Xdt[wi][:, c4, :].rearrange("p (k q) -> p k q", k=4),
                                 dd[:, 0:4].unsqueeze(2).broadcast_to([128, 4, 64]), ALU.mult)

                        def Bk(n):
                            g, cidx = divmod(n, NB)
                            W = n // 4
                            wi = W % 2
                            c4 = cidx % 4
                            ci = n % 2
                            tc = slice(128 * c4, 128 * (c4 + 1))
                            dd = ddc[ci]
                            sb_old = Sbf[n % 2]
                            sb_new = Sbf[(n + 1) % 2]
                            firstq = [True, True]
                            for hc in range(2):
                                for hh in range(2):
                                    k = 2 * hc + hh
                                    kw = {} if hh == 0 else {"tile_position": (0, 64)}
                                    o = ps_y[64 * hh:64 * hh + 64, hc * 128:(hc + 1) * 128]
                                    P.mm(o, Xdt[wi][:, c4, k * 64:(k + 1) * 64], v4(Mp[ci][:, :])[:, k, :], start=firstq[hh], **kw)
                                    firstq[hh] = False
                                    if cidx > 0:
                                        P.mm(o, sb_old[:, k * 64:(k + 1) * 64], v4(Cs[ci][:, :])[:, k, :], start=False, **kw)
                            P.mm(ps_st[:, 0:256], xbtok[wi][:, c4, 256:384], Xd[ci][:, :], start=True)
                            if cidx == 0:
                                P.copy("dve", Sst[:, :], ps_st[:, 0:256])
                            else:
                                s3 = Sst[:, :].rearrange("p (k q) -> p k q", k=4)
                                P.tt("dve", s3, s3, dd[:, 4:8].unsqueeze(2).broadcast_to([128, 4, 64]), ALU.mult)
                                P.tt("dve", Sst[:, :], Sst[:, :], ps_st[:, 0:256], ALU.add)
                            P.copy("act", sb_new[:, :], Sst[:, :])
                            P.tt("dve", ybuf[wi][:, :, tc], ps_y[:, 0:256].rearrange("p (c q) -> p c q", c=2), xsD[wi][:, :, tc], ALU.add)

                        def WN(W):
                            g, w = divmod(W, NT)
                            wi = W % 2
                            tsl = slice(512 * w, 512 * (w + 1))
                            P.tt("dve", ybuf[wi][:], ybuf[wi][:], zs[wi][:], ALU.mult)
                            P.act(sqw[:], ybuf[wi][:], AF.Square)
                            pp = ps_p[cnt[0] % 2]
                            cnt[0] += 1
                            P.mm(pp[:, 0:512], ones, sqw[:, 0, :], start=True)
                            P.mm(pp[:, 0:512], ones, sqw[:, 1, :], start=False)
                            P.act(lnw[:], pp[:, 0:512], AF.Ln, scale=1.0 / 256, bias=EPS)
                            P.act(lnw[:], lnw[:], AF.Exp, scale=-0.5)
                            P.tt("dve", yo[wi][:], ybuf[wi][:], lnw[:, :].unsqueeze(1).broadcast_to([128, 2, 512]), ALU.mult)
                            P.dma(ysd_d[b, 2 * g:2 * g + 2, :, tsl].rearrange("c p t -> p c t"), yo[wi][:], f"yo{wi}")

                        NCH = 8 * NB
                        NW = 8 * NT
                        load_group(0)
                        PC(0)
                        RG(0)
                        RG(1)
                        for n in range(NCH):
                            g, cidx = divmod(n, NB)
                            W = n // 4
                            c4 = n % 4
                            if n + 2 < NCH:
                                RG(n + 2)
                            if n > 0:
                                Bk(n - 1)
                            A(n)
                            if n > 0 and c4 == 0:
                                WN(W - 1)
                            if cidx == 12 and g + 1 < 8:
                                load_group(g + 1)
                            if W + 1 < NW:
                                order = (2, 3, 4, 5, 0, 1)
                                o = order
                                if c4 == 0:
                                    PCm(W + 1, o[0]); PCm(W + 1, o[1])
                                elif c4 == 1:
                                    PCe(W + 1, o[0]); PCm(W + 1, o[2]); PCe(W + 1, o[1]); PCm(W + 1, o[3]); PCf(W + 1, o[0])
                                elif c4 == 2:
                                    PCe(W + 1, o[2]); PCf(W + 1, o[1]); PCm(W + 1, o[4]); PCe(W + 1, o[3]); PCf(W + 1, o[2])
                                    PCm(W + 1, o[5]); PCf(W + 1, o[3])
                                    PCt(W + 1)
                                else:
                                    PCe(W + 1, o[4]); PCe(W + 1, o[5]); PCf(W + 1, o[4]); PCf(W + 1, o[5])
                        Bk(NCH - 1)
                        WN(NW - 1)
            P.barrier()

        tiles = [(b, j) for b in range(NS) for j in range(NT)]
        if "merge" in phases:
            with ES() as st:
                Wssd = sbt(st, "m_wssd", [128, 16, D], BF16)
                Watt = sbt(st, "m_watt", [128, 4, D], BF16)
                Wgs = sbt(st, "m_wgs", [128, KC, D], BF16)
                Wga = sbt(st, "m_wga", [128, KC, D], BF16)
                Wout = sbt(st, "m_wout", [128, KC, D], BF16)
                ys = [sbt(st, f"m_ys{i}", [128, 16, 512], BF16) for i in range(2)]
                ya = [sbt(st, f"m_ya{i}", [128, 4, 512], BF16) for i in range(2)]
                hTt = [sbt(st, f"m_ht{i}", [128, KC, 512], BF16) for i in range(2)]
                xt = [sbt(st, f"m_xt{i}", [128, 4, D], F32) for i in range(2)]
                th = [sbt(st, f"m_th{i}", [128, 512], F32) for i in range(2)]
                m1 = [sbt(st, f"m_m1{i}", [128, 512], F32) for i in range(3)]
                mT = sbt(st, "m_mT", [128, KC, 512], BF16)
                psm = [pst(st, f"m_ps{i}") for i in range(8)]
                pcm = [0]

                def m_loads(t):
                    b, j = tiles[t]
                    i2 = t % 2
                    tsl = slice(512 * j, 512 * (j + 1))
                    P.dma(hTt[i2][:], hTs_d[b, :, :, tsl].rearrange("c p t -> p c t"), f"mht{i2}")
                    P.dma(ys[i2][:], ysd_d[b, :, :, tsl].rearrange("c p t -> p c t"), f"mys{i2}")
                    P.dma(ya[i2][:], yat_d[b, :, :, tsl].rearrange("c p t -> p c t"), f"mya{i2}")
                    P.dma(xt[i2][:], x_d[b, tsl, :].rearrange("(k p) d -> p k d", p=128), f"mxt{i2}")

                def m_compute(t):
                    b, j = tiles[t]
                    i2 = t % 2
                    tsl = slice(512 * j, 512 * (j + 1))
                    for oc in range(KC):
                        osl = slice(oc * 128, (oc + 1) * 128)
                        pc = pcm[0]
                        p1, pg1, p2, pg2 = psm[pc % 8], psm[(pc + 1) % 8], psm[(pc + 2) % 8], psm[(pc + 3) % 8]
                        pcm[0] += 4
                        e2 = 0
                        for kc in range(KC):
                            P.mm(pg1[:, 0:512], Wgs[:, kc, osl], hTt[i2][:, kc, :], start=(kc == 0))
                        for kc in range(16):
                            P.mm(p1[:, 0:512], Wssd[:, kc, osl], ys[i2][:, kc, :], start=(kc == 0))
                        for kc in range(KC):
                            P.mm(pg2[:, 0:512], Wga[:, kc, osl], hTt[i2][:, kc, :], start=(kc == 0))
                        for kc in range(4):
                            P.mm(p2[:, 0:512], Watt[:, kc, osl], ya[i2][:, kc, :], start=(kc == 0))
                        P.act(th[e2][:], pg1[:, 0:512], AF.Tanh)
                        P.stt("dve", m1[e2][:], th[e2][:], 1.0, p1[:, 0:512], ALU.add, ALU.mult)
                        P.act(th[e2 + 1][:], pg2[:, 0:512], AF.Tanh)
                        P.stt("dve", m1[e2 + 1][:], th[e2 + 1][:], 1.0, p2[:, 0:512], ALU.add, ALU.mult)
                        P.tt("pool", mT[:, oc, :], m1[e2][:], m1[e2 + 1][:], ALU.add)
                    for blk in range(4):
                        for half in range(2):
                            pp = psm[pcm[0] % 8]
                            pcm[0] += 1
                            hs = slice(half * 512, (half + 1) * 512)
                            for kc in range(KC):
                                P.mm(pp[:, 0:512], mT[:, kc, blk * 128:(blk + 1) * 128], Wout[:, kc, hs], start=(kc == 0))
                            P.tt("dve", xt[i2][:, blk, hs], xt[i2][:, blk, hs], pp[:, 0:512], ALU.add)
                    P.dma(out_d[b, tsl, :].rearrange("(k p) d -> p k d", p=128), xt[i2][:], f"mo{i2}")

                m_loads(0)
                load_w(Wgs, w_in_d[:, C_GS:C_GS + D], KC, D, scale=g1h)
                load_w(Wssd, w_ssd_d, 16, D, scale=pv[:, PV_GS:PV_GS + 16])
                load_w(Wga, w_in_d[:, C_GA:C_GA + D], KC, D, scale=g1h)
                load_w(Watt, w_att_d, 4, D)
                load_w(Wout, w_out_d, KC, D, scale=0.5)
                for t in range(len(tiles)):
                    if t + 1 < len(tiles):
                        m_loads(t + 1)
                    m_compute(t)
            P.barrier()

        if "ffn" in phases:
            with ES() as st:
                Wup = sbt(st, "f_wup", [128, KC, 2 * DFF], BF16)
                Wdn = sbt(st, "f_wdn", [128, NFC, D], BF16)
                xb = [sbt(st, f"f_xb{i}", [128, D], F32) for i in range(2)]
                h16 = sbt(st, "f_h16", [128, 1, D], BF16)
                ssq = [sbt(st, f"f_ss{i}", [128, 1], F32) for i in range(2)]
                lnv = [sbt(st, f"f_ln{i}", [128, 1], F32) for i in range(2)]
                rstd = [sbt(st, f"f_rs{i}", [128, 1], F32) for i in range(2)]
                h2T = [sbt(st, f"f_h2T{i}", [128, KC, 512], BF16) for i in range(2)]
                aT = sbt(st, "f_aT", [128, NFC, 512], BF16)
                tail = sbt(st, "f_tail", [128, 2 * NFC, 2], F32)
                rawf = [sbt(st, f"f_raw{i}", [128, 514], F32) for i in range(2)]
                accf = [sbt(st, f"f_acc{i}", [128, 512], F32) for i in range(4)]
                ostg = [sbt(st, f"f_os{i}", [128, 512], F32) for i in range(2)]
                pbf = [pst(st, f"f_pb{i}", BF16) for i in range(2)]
                psf = [pst(st, f"f_ps{i}") for i in range(6)]
                pcf = [0]
                blkc = [0]
                def fwc(i, ci):
                    src = pvh if ci < NFC else pv
                    return src[:, PV_FW + i * 44 + ci:PV_FW + i * 44 + ci + 1]

                def fbc(ci):
                    src = pvh if ci < NFC else pv
                    return src[:, PV_FB + ci:PV_FB + ci + 1]

                def f_norm(t):
                    b, j = tiles[t]
                    hb = h2T[t % 2]
                    for k in range(4):
                        q = blkc[0] % 2
                        blkc[0] += 1
                        r0 = 512 * j + 128 * k
                        P.dma(xb[q][:], out_d[b, r0:r0 + 128, :], f"fxb{q}")
                        P.memset("pool", ssq[q][:], 0.0)
                        P.act(h16[:, 0, :], xb[q][:], AF.Square, accum_out=ssq[q][:, 0:1])
                        P.act(lnv[q][:], ssq[q][:], AF.Ln, scale=1.0 / D, bias=EPS)
                        P.act(rstd[q][:], lnv[q][:], AF.Exp, scale=-0.5)
                        P.scale("dve" if k % 2 == 0 else "act", h16[:, 0, :], xb[q][:], rstd[q][:, 0:1])
                        bank = pbf[q]
                        for kc in range(KC):
                            P.tr(bank[:, kc * 128:(kc + 1) * 128], h16[:, 0, kc * 128:(kc + 1) * 128], ident)
                        P.copy("act" if k % 2 == 0 else "dve", hb[:, :, k * 128:(k + 1) * 128], bank[:, :].rearrange("p (c t) -> p c t", c=KC))

                def f_up(t):
                    b, j = tiles[t]
                    hb = h2T[t % 2]
                    if j == 0:
                        P.memset("pool", tail[:], 0.0)

                    def up1(c):
                        res = []
                        for half in range(2):
                            ci = half * NFC + c
                            pp = psf[pcf[0] % 6]
                            pcf[0] += 1
                            col = half * DFF + c * 128
                            for kc in range(KC):
                                P.mm(pp[:, 0:512], Wup[:, kc, col:col + 128], hb[:, kc, :], start=(kc == 0))
                            rw = rawf[half]
                            ab = accf[(2 * c + half) % 4]
                            P.copy("pool", rw[:, 0:2], tail[:, ci, :])
                            P.copy("act", rw[:, 2:514], pp[:, 0:512])
                            P.act(ab[:], pp[:, 0:512], AF.Identity, scale=fwc(2, ci), bias=fbc(ci))
                            P.copy("pool", tail[:, ci, :], rw[:, 512:514])
                            for i in range(2):
                                P.stt("dve", ab[:], rw[:, i:i + 512], fwc(i, ci), ab[:], ALU.mult, ALU.add)
                            res.append(ab)
                        return res

                    def up2(c, res):
                        tf = rawf[0][:, 2:514]
                        P.act(tf, res[0][:], AF.Tanh)
                        P.stt("dve", res[0][:], tf, 1.0, res[0][:], ALU.add, ALU.mult)
                        P.tt("pool", aT[:, c, :], res[0][:], res[1][:], ALU.mult)

                    for c in range(NFC):
                        r_ = up1(c)
                        up2(c, r_)

                def f_down(t):
                    b, j = tiles[t]
                    for blk in range(4):
                        r0 = 512 * j + 128 * blk
                        for half in range(2):
                            pp = psf[pcf[0] % 6]
                            q = pcf[0] % 2
                            pcf[0] += 1
                            hs = slice(half * 512, (half + 1) * 512)
                            for c in range(NFC):
                                P.mm(pp[:, 0:512], aT[:, c, blk * 128:(blk + 1) * 128], Wdn[:, c, hs], start=(c == 0))
                            P.copy("act", ostg[q][:], pp[:, 0:512])
                            P.dma(out_d[b, r0:r0 + 128, hs], ostg[q][:], f"fo{q}", eng="pool", accum_op=ALU.add)

                f_norm(0)
                load_w(Wup, w_up_d, KC, 2 * DFF, scale=pv[:, PV_G2:PV_G2 + 8])
                load_w(Wdn, w_dn_d, NFC, D)
                nt_ = len(tiles)
                f_up(0)
                for t in range(nt_):
                    if t + 1 < nt_:
                        f_norm(t + 1)
                    f_down(t)
                    if t + 1 < nt_:
                        f_up(t + 1)
        P.barrier(final=True)
        P.emit()
    return nc, P


_CACHE = {}


def _core_inputs(inp, c, NS):
    x = np.ascontiguousarray(np.asarray(inp["x"], dtype=np.float32)[c * NS:(c + 1) * NS])
    return x


def kernel(**inputs):
    NCORES = 8
    NS = 2
    if "nc" not in _CACHE:
        _CACHE["nc"] = build(NS=NS)[0]
    nc = _CACHE["nc"]
    f = lambda k: np.ascontiguousarray(np.asarray(inputs[k], dtype=np.float32)[0])
    shared = {
        "w_in": f("w_in"), "w_ssd_proj": f("w_ssd_proj"), "w_attn_proj": f("w_attn_proj"), "w_out": f("w_out"),
        "w_up": f("w_up"), "w_down": f("w_down"), "pvec": pack_pvec(inputs), "cmask": make_cmask(),
        "dt_bias": np.asarray(inputs["dt_bias"], np.float32).reshape(1, 32),
        "a_log": np.asarray(inputs["a_log"], np.float32).reshape(1, 32),
    }
    in_maps = []
    for c in range(NCORES):
        m = dict(shared)
        m["x"] = _core_inputs(inputs, c, NS)
        in_maps.append(m)
    res = run_bass_kernel_spmd(nc, in_maps, core_ids=list(range(NCORES)))
    out = np.concatenate([np.asarray(r["out"]) for r in res.results], axis=0)
    return out.astype(np.float32)
```

```python
import contextlib
import numpy as np
import ml_dtypes
import concourse.bass as bass
import concourse.mybir as mybir
from concourse.bass_utils import run_bass_kernel_spmd

F32 = mybir.dt.float32
BF16 = mybir.dt.bfloat16
AF = mybir.ActivationFunctionType
ALU = mybir.AluOpType

ENGS = ("pe", "act", "dve", "pool", "sp")
SMALL_N = 256


def region(ap):
    t = ap.tensor
    dims = ap.ap
    off = int(ap.offset)
    name = t.name
    cls = type(t).__name__
    if cls.startswith("DRam"):
        lo = off
        hi = off + sum((c - 1) * abs(s) for s, c in dims) + 1
        return (name, 0, 1, lo, hi)
    if cls.startswith("PSum"):
        return (name, 0, 128, 0, 1 << 30)
    ps, npart = dims[0]
    p0 = off // ps
    f0 = off % ps
    f1 = f0 + sum((c - 1) * abs(s) for s, c in dims[1:]) + 1
    return (name, p0, p0 + npart, f0, f1)


class Op:
    __slots__ = ("eng", "fn", "idx", "deps", "signaled", "dma_slot", "dma_cnt", "nfree", "sig_no")

    def __init__(self, eng, fn):
        self.eng = eng
        self.fn = fn
        self.deps = {}
        self.signaled = False
        self.dma_slot = None
        self.dma_cnt = 0
        self.nfree = 1 << 30
        self.sig_no = 0


class Prog:
    def __init__(self, nc):
        self.nc = nc
        self.ops = []
        self.recs = {}
        self.slot_last = {}
        self.slot_cnt = {}
        self.barrier_idx = {}

    def _track(self, op, reads, writes):
        deps = op.deps
        ekey = op.eng if op.dma_slot is None else ("dma", op.dma_slot)

        def add_dep(r):
            k = r[5]
            if k == ekey and op.dma_slot is None:
                if k == "pe":
                    return
            if deps.get(k, -1) < r[6]:
                deps[k] = r[6]

        psum_reads = [ap for ap in reads if type(ap.tensor).__name__.startswith("PSum")]
        if psum_reads:
            reads = [ap for ap in reads if not type(ap.tensor).__name__.startswith("PSum")]
            writes = list(writes) + psum_reads
        for ap in reads:
            name, p0, p1, f0, f1 = region(ap)
            lst = self.recs.setdefault(name, [])
            found = None
            for r in lst:
                if r[4] == "w":
                    if r[0] < p1 and p0 < r[1] and r[2] < f1 and f0 < r[3]:
                        add_dep(r)
                elif r[5] == ekey and r[0] == p0 and r[1] == p1 and r[2] == f0 and r[3] == f1:
                    found = r
            if found is not None:
                found[6] = op.idx
            else:
                lst.append([p0, p1, f0, f1, "r", ekey, op.idx, op.nfree])
        for ap in writes:
            name, p0, p1, f0, f1 = region(ap)
            lst = self.recs.setdefault(name, [])
            keep = []
            for r in lst:
                if r[0] < p1 and p0 < r[1] and r[2] < f1 and f0 < r[3]:
                    if r[6] != op.idx:
                        add_dep(r)
                    if r[0] >= p0 and r[1] <= p1 and r[2] >= f0 and r[3] <= f1 and r[6] != op.idx:
                        continue
                keep.append(r)
            keep.append([p0, p1, f0, f1, "w", ekey, op.idx, op.nfree])
            self.recs[name] = keep

    def add(self, eng, fn, reads=(), writes=(), nfree=None):
        op = Op(eng, fn)
        op.idx = len(self.ops)
        if nfree is None:
            nfree = 1 << 30
            for ap in writes:
                d = ap.ap
                n = 1
                for s, c in d[1:]:
                    n *= c
                nfree = min(nfree, n)
        op.nfree = nfree
        self.ops.append(op)
        self._track(op, reads, writes)
        return op

    def dma(self, out, in_, slot, eng="sp", **kw):
        def fn(e, out=out, in_=in_, kw=kw):
            return e.dma_start(out=out, in_=in_, **kw)

        op = Op(eng, fn)
        op.idx = len(self.ops)
        op.dma_slot = slot
        self.slot_cnt[slot] = self.slot_cnt.get(slot, 0) + 16
        op.dma_cnt = self.slot_cnt[slot]
        self.ops.append(op)
        prev = self.slot_last.get(slot)
        if prev is not None:
            op.deps[("dma", slot)] = prev
        self.slot_last[slot] = op.idx
        self._track(op, [in_], [out])
        return op

    def mm(self, out, lhsT, rhs, start=True, stop=True, **kw):
        if not start:
            kw.setdefault("skip_group_check", True)
        return self.add("pe", lambda e: e.matmul(out, lhsT, rhs, start=start, stop=stop, **kw),
                        [lhsT, rhs], [out])

    def tr(self, out, in_, ident):
        return self.add("pe", lambda e: e.transpose(out, in_, ident), [in_, ident], [out])

    def act(self, out, in_, func, bias=None, scale=None, accum_out=None, eng="act"):
        reads = [in_]
        kw = {}
        if bias is not None:
            kw["bias"] = bias
            if not isinstance(bias, (int, float)):
                reads.append(bias)
        if scale is not None:
            kw["scale"] = scale
            if not isinstance(scale, (int, float)):
                reads.append(scale)
        writes = [out]
        if accum_out is not None:
            kw["accum_out"] = accum_out
            writes.append(accum_out)
        return self.add(eng, lambda e: e.activation(out, in_, func, **kw), reads, writes)

    def tt(self, eng, out, in0, in1, op):
        return self.add(eng, lambda e: e.tensor_tensor(out, in0, in1, op), [in0, in1], [out])

    def ts(self, eng, out, in0, s1, s2, op0, op1=None, accum_out=None):
        reads = [in0]
        if not isinstance(s1, (int, float)):
            reads.append(s1)
        if s2 is not None and not isinstance(s2, (int, float)):
            reads.append(s2)
        kw = {}
        writes = [out]
        if accum_out is not None:
            kw["accum_out"] = accum_out
            writes.append(accum_out)
        if op1 is None:
            return self.add(eng, lambda e: e.tensor_scalar(out, in0, s1, None, op0, **kw), reads, writes)
        return self.add(eng, lambda e: e.tensor_scalar(out, in0, s1, s2, op0, op1, **kw), reads, writes)

    def stt(self, eng, out, in0, scalar, in1, op0, op1):
        reads = [in0, in1]
        if not isinstance(scalar, (int, float)):
            reads.append(scalar)
        return self.add(eng, lambda e: e.scalar_tensor_tensor(out, in0, scalar, in1, op0, op1), reads, [out])

    def scale(self, eng, out, in_, sc):
        if eng == "act":
            return self.act(out, in_, AF.Copy, scale=sc)
        return self.ts(eng, out, in_, sc, None, ALU.mult)

    def copy(self, eng, out, in_):
        if eng == "act":
            return self.add(eng, lambda e: e.copy(out, in_), [in_], [out])
        return self.add(eng, lambda e: e.tensor_copy(out, in_), [in_], [out])

    def memset(self, eng, ap, val):
        return self.add(eng, lambda e: e.memset(ap, val), [], [ap])

    def barrier(self, final=False):
        op = Op("sp", None)
        op.idx = len(self.ops)
        self.ops.append(op)
        for slot, i in self.slot_last.items():
            op.deps[("dma", slot)] = i
        if not final:
            m = Op("marker", None)
            m.idx = len(self.ops)
            self.ops.append(m)
        self.recs = {}

    def emit(self):
        nc = self.nc
        ops = self.ops
        for op in ops:
            for k, i in op.deps.items():
                ops[i].signaled = True
        cnt = {e: 0 for e in ENGS}
        for op in ops:
            if op.eng == "marker":
                continue
            if op.dma_slot is None and op.signaled:
                cnt[op.eng] += 1
                op.sig_no = cnt[op.eng]
        slots = sorted(self.slot_cnt.keys())
        import contextlib

        with contextlib.ExitStack() as st:
            esem = {e: st.enter_context(nc.semaphore("s_" + e)) for e in ENGS}
            ssem = {s: st.enter_context(nc.semaphore("d_" + s)) for s in slots}
            segs = [[]]
            for op in ops:
                if op.eng == "marker":
                    segs.append([])
                else:
                    segs[-1].append(op)
            stats = {e: [0, 0] for e in ENGS}
            waited_all = {e: {} for e in ENGS}

            def run(eng_name, e, seg):
                waited = waited_all[eng_name]
                for op in seg:
                    if op.eng != eng_name:
                        continue
                    for k, i in op.deps.items():
                        p = ops[i]
                        if isinstance(k, tuple):
                            sem, val = ssem[k[1]], p.dma_cnt
                        else:
                            sem, val = esem[k], p.sig_no
                        if waited.get(k, 0) >= val:
                            continue
                        waited[k] = val
                        e.wait_ge(sem, val)
                        stats[eng_name][1] += 1
                    if op.fn is None:
                        continue
                    ins = op.fn(e)
                    stats[eng_name][0] += 1
                    if op.dma_slot is not None:
                        ins.then_inc(ssem[op.dma_slot], 16)
                    elif op.signaled:
                        ins.then_inc(esem[eng_name], 1)

            for seg in segs:
                with nc.Block() as block:
                    @block.tensor
                    def _(e, seg=seg):
                        run("pe", e, seg)

                    @block.scalar
                    def _(e, seg=seg):
                        run("act", e, seg)

                    @block.vector
                    def _(e, seg=seg):
                        run("dve", e, seg)

                    @block.gpsimd
                    def _(e, seg=seg):
                        run("pool", e, seg)

                    @block.sync
                    def _(e, seg=seg):
                        run("sp", e, seg)

            self.stats = stats
            self.sig_counts = cnt


D = 1024
KC = 8
S_FULL = 4096
C_Z, C_X, C_B, C_C, C_DT, C_Q, C_K, C_V, C_GS, C_GA = 0, 2048, 4096, 5120, 6144, 6176, 7712, 9248, 10784, 11808
DIL = (1, 4, 16)
EPS = 1e-6
DFF = 2816
NFC = 22
PV_G1, PV_G2, PV_GS, PV_CW, PV_CB, PV_D, PV_QG, PV_KG, PV_FW, PV_FB, NPV = 0, 8, 16, 32, 160, 192, 208, 209, 210, 342, 386
CM_ID, CM_U, CM_L, CM_GT, CM_ONE, CM_AM, CM_BD, CM_NEG, NCM = 0, 128, 256, 384, 512, 640, 1152, 1280, 1536

_bf = ml_dtypes.bfloat16


def _fm(v):
    return np.ascontiguousarray(np.asarray(v).reshape(-1, 128).T)


def pack_pvec(inp):
    pv = np.zeros((128, NPV), np.float32)
    pv[:, PV_G1:PV_G1 + 8] = _fm(inp["norm1_g"][0])
    pv[:, PV_G2:PV_G2 + 8] = _fm(inp["norm2_g"][0])
    pv[:, PV_GS:PV_GS + 16] = _fm(inp["ssd_norm_g"][0])
    for i in range(4):
        pv[:, PV_CW + i * 32:PV_CW + (i + 1) * 32] = _fm(inp["ssd_conv_w"][0, i])
    pv[:, PV_CB:PV_CB + 32] = _fm(inp["ssd_conv_b"][0])
    pv[:, PV_D:PV_D + 16] = _fm(np.repeat(inp["d_skip"][0], 64))
    pv[:, PV_QG] = np.tile(inp["q_norm_g"][0], 2)
    pv[:, PV_KG] = np.tile(inp["k_norm_g"][0], 2)
    for i in range(3):
        pv[:, PV_FW + i * 44:PV_FW + (i + 1) * 44] = _fm(inp["ffn_conv_w"][0, i])
    pv[:, PV_FB:PV_FB + 44] = _fm(inp["ffn_conv_b"][0])
    return pv


def make_cmask():
    cm = np.zeros((128, NCM), np.float32)
    i = np.arange(128)
    U = (i[:, None] <= i[None, :]).astype(np.float32)
    L = (i[:, None] >= i[None, :]).astype(np.float32)
    GT = (i[:, None] > i[None, :]).astype(np.float32)
    cm[:, CM_ID:CM_ID + 128] = np.eye(128)
    cm[:, CM_U:CM_U + 128] = U
    cm[:, CM_L:CM_L + 128] = L
    cm[:, CM_GT:CM_GT + 128] = GT
    cm[:, CM_ONE:CM_ONE + 128] = 1.0
    cm[:, CM_AM:CM_AM + 512] = np.concatenate([L, U, L, U], 1)
    bd = np.zeros((128, 128), np.float32)
    bd[:64, :64] = 1.0
    bd[64:, 64:] = 1.0
    cm[:, CM_BD:CM_BD + 128] = bd
    NEG = -30000.0
    cm[:, CM_NEG:CM_NEG + 256] = np.concatenate([(1.0 - L) * NEG, (1.0 - U) * NEG], 1)
    return cm.astype(_bf)


def build(NS=2, S=S_FULL, debug=False, phases=("norm", "attn", "ssd", "merge", "ffn")):
    nc = bass.Bass("TRN2", target_bir_lowering=False)
    NT = S // 512
    NB = S // 128

    def din(name, shape, dt=F32):
        return nc.dram_tensor(name, shape, dt, kind="ExternalInput").ap()

    x_d = din("x", [NS, S, D])
    w_in_d = din("w_in", [D, 12832])
    w_ssd_d = din("w_ssd_proj", [2048, D])
    w_att_d = din("w_attn_proj", [512, D])
    w_out_d = din("w_out", [D, D])
    w_up_d = din("w_up", [D, 2 * DFF])
    w_dn_d = din("w_down", [DFF, D])
    pv_d = din("pvec", [128, NPV])
    cm_d = din("cmask", [128, NCM], BF16)
    dtb_d = din("dt_bias", [1, 32])
    alog_d = din("a_log", [1, 32])
    skind = "ExternalOutput" if debug else "Internal"
    hTs_d = nc.dram_tensor("hTs", [NS, KC, 128, S], BF16, kind=skind).ap()
    yat_d = nc.dram_tensor("yattn", [NS, 4, 128, S], BF16, kind=skind).ap()
    ysd_d = nc.dram_tensor("yssd", [NS, 16, 128, S], BF16, kind=skind).ap()
    out_d = nc.dram_tensor("out", [NS, S, D], F32, kind="ExternalOutput").ap()

    P = Prog(nc)
    ES = contextlib.ExitStack

    with ES() as top:
        def sbt(st, name, shape, dt):
            return st.enter_context(nc.sbuf_tensor(name, shape, dt))

        def pst(st, name, dt=F32):
            return st.enter_context(nc.psum_tensor(name, [128, 512 if dt == F32 else 1024], dt))

        pv = sbt(top, "pv", [128, NPV], F32)
        pvh = sbt(top, "pvh", [128, NPV], F32)
        cm = sbt(top, "cm", [128, NCM], BF16)
        P.dma(pv[:], pv_d[:, :], "pv")
        P.dma(cm[:], cm_d[:, :], "cm")
        P.ts("dve", pvh[:], pv[:], 0.5, None, ALU.mult)
        ident = cm[:, CM_ID:CM_ID + 128]
        maskU = cm[:, CM_U:CM_U + 128]
        maskGT = cm[:, CM_GT:CM_GT + 128]
        ones = cm[:, CM_ONE:CM_ONE + 128]
        amask = cm[:, CM_AM:CM_AM + 512]
        bdones = cm[:, CM_BD:CM_BD + 128]
        negm = cm[:, CM_NEG:CM_NEG + 256]
        neg1 = sbt(top, "neg1", [128, 2], F32)
        P.memset("dve", neg1[:, 0:1], -1.0)
        P.memset("dve", neg1[:, 1:2], -0.5)
        stg = [sbt(top, f"stg{i}", [128, 512], F32) for i in range(2)]
        stg_i = [0, 0]

        def load_w(dst, src, kn, ncols, scale=None, engs=("act", "dve")):
            SW = 512
            cw = min(ncols, SW)
            kpp = max(1, SW // cw)
            for c0 in range(0, ncols, cw):
                cn = min(cw, ncols - c0)
                for k0 in range(0, kn, kpp):
                    kk = min(kpp, kn - k0)
                    i = stg_i[0] % 2
                    stg_i[0] += 1
                    sv = stg[i][:, 0:kk * cn].rearrange("p (k n) -> p k n", k=kk)
                    P.dma(sv, src[k0 * 128:(k0 + kk) * 128, c0:c0 + cn].rearrange("(k p) n -> p k n", p=128), f"stg{i}")
                    for k in range(kk):
                        d_ = dst[:, k0 + k, c0:c0 + cn]
                        eng = engs[stg_i[1] % len(engs)]
                        stg_i[1] += 1
                        if scale is None:
                            P.copy(eng, d_, sv[:, k, :])
                        elif isinstance(scale, float):
                            P.scale(eng, d_, sv[:, k, :], scale)
                        else:
                            P.scale(eng, d_, sv[:, k, :], scale[:, k0 + k:k0 + k + 1])

        g1 = pv[:, PV_G1:PV_G1 + 8]
        g1h = pvh[:, PV_G1:PV_G1 + 8]

        for b in range(NS):
            with ES() as seq:
                hT = sbt(seq, f"hT{b}", [128, KC, S], BF16)
                if "norm" in phases:
                    with ES() as st:
                        xt = [sbt(st, f"n_xt{b}_{i}", [128, 4, D], F32) for i in range(2)]
                        junk = sbt(st, f"n_junk{b}", [128, D], BF16)
                        h16 = [sbt(st, f"n_h16{b}_{i}", [128, 4, D], BF16) for i in range(2)]
                        ssq = [sbt(st, f"n_ss{b}_{i}", [128, 4], F32) for i in range(2)]
                        lnv = [sbt(st, f"n_ln{b}_{i}", [128, 4], F32) for i in range(2)]
                        rstd = [sbt(st, f"n_rs{b}_{i}", [128, 4], F32) for i in range(2)]
                        pb = [pst(st, f"n_pb{b}_{i}", BF16) for i in range(4)]
                        def n_load(j):
                            P.dma(xt[j % 2][:], x_d[b, 512 * j:512 * (j + 1), :].rearrange("(k p) d -> p k d", p=128), f"xt{j % 2}")

                        n_load(0)
                        for j in range(NT):
                            i2 = j % 2
                            if j + 1 < NT:
                                n_load(j + 1)
                            P.memset("pool", ssq[i2][:], 0.0)
                            for k in range(4):
                                P.act(junk[:], xt[i2][:, k, :], AF.Square, accum_out=ssq[i2][:, k:k + 1])
                            P.act(lnv[i2][:], ssq[i2][:], AF.Ln, scale=1.0 / D, bias=EPS)
                            P.act(rstd[i2][:], lnv[i2][:], AF.Exp, scale=-0.5)
                            for k in range(4):
                                P.scale("dve" if k % 2 == 0 else "act", h16[i2][:, k, :], xt[i2][:, k, :], rstd[i2][:, k:k + 1])
                            for pr in range(4):
                                bank = pb[pr]
                                for kk in range(2):
                                    kc = 2 * pr + kk
                                    for k in range(4):
                                        P.tr(bank[:, kk * 512 + k * 128:kk * 512 + (k + 1) * 128], h16[i2][:, k, kc * 128:(kc + 1) * 128], ident)
                                P.copy("act" if pr % 2 == 0 else "dve", hT[:, 2 * pr:2 * pr + 2, 512 * j:512 * (j + 1)],
                                       bank[:, :].rearrange("p (k t) -> p k t", k=2))
                            P.dma(hTs_d[b, :, :, 512 * j:512 * (j + 1)].rearrange("k p t -> p k t"), hT[:, :, 512 * j:512 * (j + 1)], f"hs{i2}")
                P.barrier()
                if "attn" in phases:
                    with ES() as st:
                        wq = [sbt(st, f"a_wq{b}_{i}", [128, KC, 128], BF16) for i in range(2)]
                        wk = [sbt(st, f"a_wk{b}_{i}", [128, KC, 128], BF16) for i in range(2)]
                        wv = [sbt(st, f"a_wv{b}_{i}", [128, KC, 128], BF16) for i in range(2)]
                        QT = sbt(st, f"a_QT{b}", [128, S], BF16)
                        KT = sbt(st, f"a_KT{b}", [128, S], BF16)
                        Vt = sbt(st, f"a_Vt{b}", [128, NB, 128], BF16)
                        lnb = sbt(st, f"a_ln{b}", [128, 2 * NT, 512], F32)
                        sq = [sbt(st, f"a_sq{b}_{i}", [128, 512], BF16) for i in range(2)]
                        pt = [sbt(st, f"a_pt{b}_{i}", [128, 512], BF16) for i in range(2)]
                        acc = sbt(st, f"a_acc{b}", [128, 2, S], F32)
                        rden = [sbt(st, f"a_rd{b}_{i}", [128, 512], F32) for i in range(2)]
                        yat = [sbt(st, f"a_yat{b}_{i}", [128, S], BF16) for i in range(2)]
                        bk = [pst(st, f"a_bk{b}_{i}") for i in range(8)]
                        ps_p = [bk[0], bk[1]]
                        ps_n = bk[7]
                        ps_v = bk[6]
                        ps_s = [(bk[0], bk[1]), (bk[2], bk[3])]
                        ps_o = [bk[4], bk[5]]
                        def load_attn_w(it_):
                            hp_i, g_i = divmod(it_, 3)
                            slot = 0
                            for (wt, c0) in ((wq[it_ % 2], C_Q), (wk[it_ % 2], C_K), (wv[it_ % 2], C_V)):
                                cc = c0 + g_i * 512 + hp_i * 128
                                for k0 in (0, 4):
                                    sv = lnb[:, slot, :].rearrange("p (k n) -> p k n", k=4)
                                    P.dma(sv, w_in_d[k0 * 128:(k0 + 4) * 128, cc:cc + 128].rearrange("(k p) n -> p k n", p=128), f"aw{slot}")
                                    for k in range(4):
                                        P.scale("act" if (k + slot) % 2 == 0 else "dve", wt[:, k0 + k, :], sv[:, k, :], g1[:, k0 + k:k0 + k + 1])
                                    slot += 1

                        load_attn_w(0)
                        it = 0
                        for hp in range(4):
                            for g in range(3):
                                d = DIL[g]
                                L = S // d
                                nbl = L // 128
                                wi = it % 2
                                it += 1

                                def nat_view(T, j):
                                    if d == 1:
                                        return T[:, 512 * j:512 * (j + 1)]
                                    return T[:, :].rearrange("p (r m) -> p r m", r=d)[:, :, (512 * j) // d:(512 * (j + 1)) // d]

                                def nat_src(a):
                                    if d == 1:
                                        return a
                                    return a.rearrange("p (m r) -> p r m", r=d)

                                ps_nn = [bk[5], bk[7]]
                                items = [(wt, T, j) for (wt, T) in ((wq[wi], QT), (wk[wi], KT)) for j in range(NT)]

                                def qk1(idx):
                                    wt, T, j = items[idx]
                                    pp = ps_p[idx % 2]
                                    for kc in range(KC):
                                        P.mm(pp[:, 0:512], wt[:, kc, :], hT[:, kc, 512 * j:512 * (j + 1)], start=(kc == 0))
                                    P.act(sq[idx % 2][:], pp[:, 0:512], AF.Square)
                                    P.copy("dve", nat_view(T, j), nat_src(pp[:, 0:512]))
                                    P.mm(ps_nn[idx % 2][:, 0:512], bdones, sq[idx % 2][:], start=True)

                                def qk2(idx):
                                    P.act(lnb[:, idx, :], ps_nn[idx % 2][:, 0:512], AF.Ln, scale=1.0 / 64, bias=EPS)

                                qk1(0)
                                for idx in range(len(items)):
                                    if idx + 1 < len(items):
                                        qk1(idx + 1)
                                    qk2(idx)
                                idx = 0
                                for (T, gcol) in ((QT, PV_QG), (KT, PV_KG)):
                                    for j in range(NT):
                                        P.act(lnb[:, idx, :], lnb[:, idx, :], AF.Exp, scale=-0.5)
                                        P.stt("dve", nat_view(T, j), nat_view(T, j), pv[:, gcol:gcol + 1], nat_src(lnb[:, idx, :]), ALU.mult, ALU.mult)
                                        idx += 1
                                if it < 12:
                                    load_attn_w(it)
                                for bi in range(NB):
                                    r, n = bi // nbl, bi % nbl
                                    t0 = r + d * 128 * n
                                    for kc in range(KC):
                                        P.mm(ps_v[:, (bi % 4) * 128:(bi % 4 + 1) * 128], hT[:, kc, t0:t0 + d * 127 + 1:d], wv[wi][:, kc, :],
                                             start=(kc == 0 and bi % 4 == 0))
                                    if bi % 4 == 3:
                                        P.copy("act", Vt[:, bi - 3:bi + 1, :], ps_v[:, :].rearrange("p (k c) -> p k c", k=4))
                                def Sblk(bi):
                                    r, n = bi // nbl, bi % nbl
                                    c0 = r * L + 128 * n
                                    hp_ = n > 0
                                    ptb = pt[bi % 2]
                                    for h in range(2):
                                        pss = ps_s[bi % 2][h]
                                        rows = slice(64 * h, 64 * h + 64)
                                        if hp_:
                                            P.mm(pss[:, 0:128], KT[rows, c0 - 128:c0], QT[rows, c0:c0 + 128], start=True)
                                        P.mm(pss[:, 128:256], KT[rows, c0:c0 + 128], QT[rows, c0:c0 + 128], start=not hp_)
                                        lo = 0 if hp_ else 128
                                        P.mm(pss[:, lo:256], ident, negm[:, lo:256], start=False)
                                        P.act(ptb[:, h * 256 + lo:h * 256 + 256], pss[:, lo:256], AF.Exp, scale=0.125)

                                def Vblk(bi):
                                    r, n = bi // nbl, bi % nbl
                                    hp_ = n > 0
                                    ptb = pt[bi % 2]
                                    pso = ps_o[bi % 2]
                                    for h in range(2):
                                        rows = slice(64 * h, 64 * h + 64)
                                        kw = {} if h == 0 else {"tile_position": (0, 64)}
                                        first = True
                                        for (lh, ncol) in ((None, 0), (ones, 128)):
                                            for pc in ((0, 1) if hp_ else (1,)):
                                                vb = bi - 1 if pc == 0 else bi
                                                lhsT = Vt[:, vb, 64 * h:64 * h + 64] if lh is None else ones[:, 0:64]
                                                P.mm(pso[rows, ncol:ncol + 128], lhsT, ptb[:, h * 256 + pc * 128:h * 256 + pc * 128 + 128],
                                                     start=first, **kw)
                                                first = False
                                    t0 = r + d * 128 * n
                                    av = acc[:, :, t0:t0 + d * 127 + 1:d]
                                    pv3 = pso[:, 0:256].rearrange("p (c q) -> p c q", c=2)
                                    if g == 0:
                                        P.copy("dve", av, pv3)
                                    else:
                                        P.tt("dve", av, pv3, av, ALU.add)

                                Sblk(0)
                                for bi in range(NB):
                                    if bi + 1 < NB:
                                        Sblk(bi + 1)
                                    Vblk(bi)
                            yb = yat[hp % 2]
                            for j in range(NT):
                                sl = slice(512 * j, 512 * (j + 1))
                                P.add("dve", lambda e, o=rden[j % 2][:], i=acc[:, 1, sl]: e.reciprocal(o, i), [acc[:, 1, sl]], [rden[j % 2][:]])
                                P.tt("pool", yb[:, sl], acc[:, 0, sl], rden[j % 2][:], ALU.mult)
                            P.dma(yat_d[b, hp, :, :], yb[:, :], f"yat{hp % 2}")
                P.barrier()
                if "ssd" in phases:
                    with ES() as st:
                        wdt = sbt(st, f"s_wdt{b}", [128, KC, 32], BF16)
                        dtb = sbt(st, f"s_dtb{b}", [128, 32], F32)
                        alog = sbt(st, f"s_alog{b}", [128, 32], F32)
                        xdt = sbt(st, f"s_xdt{b}", [128, NB, 32], F32)
                        tmpa = sbt(st, f"s_tmpa{b}", [128, NB, 32], F32)
                        dt_all = sbt(st, f"s_dt{b}", [128, NB, 32], F32)
                        a32 = sbt(st, f"s_a32{b}", [128, NB, 32], F32)
                        a_hi = sbt(st, f"s_ahi{b}", [128, NB, 32], BF16)
                        a_lo = sbt(st, f"s_alo{b}", [128, NB, 32], BF16)
                        wg = [sbt(st, f"s_wg{b}_{i}", [128, KC, 768], BF16) for i in range(2)]
                        raw = [sbt(st, f"s_raw{b}_{i}", [128, 515], F32) for i in range(4)]
                        accb = [sbt(st, f"s_acc{b}_{i}", [128, 512], F32) for i in range(4)]
                        thb = [sbt(st, f"s_th{b}_{i}", [128, 512], F32) for i in range(2)]
                        xbcT = [sbt(st, f"s_xbcT{b}_{i}", [128, 4, 512], BF16) for i in range(2)]
                        zs = [sbt(st, f"s_zs{b}_{i}", [128, 2, 512], BF16) for i in range(2)]
                        xbtok = [sbt(st, f"s_xbtok{b}_{i}", [128, 4, 384], BF16) for i in range(2)]
                        Xdt = [sbt(st, f"s_Xdt{b}_{i}", [128, 4, 256], BF16) for i in range(2)]
                        xsD = [sbt(st, f"s_xsD{b}_{i}", [128, 2, 512], F32) for i in range(2)]
                        Rh = [sbt(st, f"s_Rh{b}_{i}", [128, 512], BF16) for i in range(4)]
                        Rl = [sbt(st, f"s_Rl{b}_{i}", [128, 512], BF16) for i in range(4)]
                        Eb = [sbt(st, f"s_E{b}_{i}", [128, 512], BF16) for i in range(2)]
                        EAb = [sbt(st, f"s_EA{b}_{i}", [128, 512], BF16) for i in range(2)]
                        Mp = [sbt(st, f"s_Mp{b}_{i}", [128, 512], BF16) for i in range(2)]
                        Cs = [sbt(st, f"s_Cs{b}_{i}", [128, 512], BF16) for i in range(2)]
                        CBm = [sbt(st, f"s_CBm{b}_{i}", [128, 128], BF16) for i in range(2)]
                        Xd = [sbt(st, f"s_Xd{b}_{i}", [128, 256], BF16) for i in range(2)]
                        ddc = [sbt(st, f"s_dd{b}_{i}", [128, 12], F32) for i in range(2)]
                        Sst = sbt(st, f"s_S{b}", [128, 256], F32)
                        Sbf = [sbt(st, f"s_Sbf{b}_{i}", [128, 256], BF16) for i in range(2)]
                        ybuf = [sbt(st, f"s_yb{b}_{i}", [128, 2, 512], F32) for i in range(2)]
                        sqw = sbt(st, f"s_sq{b}", [128, 2, 512], BF16)
                        lnw = sbt(st, f"s_lnw{b}", [128, 512], F32)
                        yo = [sbt(st, f"s_yo{b}_{i}", [128, 2, 512], BF16) for i in range(2)]
                        ps_p = [pst(st, f"s_psp{b}_{i}") for i in range(2)]
                        ps_seg = pst(st, f"s_seg{b}")
                        ps_acs = pst(st, f"s_acs{b}")
                        ps_cb = pst(st, f"s_cb{b}")
                        ps_st = pst(st, f"s_st{b}")
                        ps_y = pst(st, f"s_y{b}")
                        ps_tr = pst(st, f"s_tr{b}", BF16)
                        load_w(wdt, w_in_d[:, C_DT:C_DT + 32], KC, 32, scale=g1)
                        P.dma(dtb[:], dtb_d[:, :].partition_broadcast(128), "dtb")
                        P.dma(alog[:], alog_d[:, :].partition_broadcast(128), "alog")
                        for half in range(NB // 16):
                            pp = ps_p[half % 2]
                            for bl in range(16):
                                blk = half * 16 + bl
                                for kc in range(KC):
                                    P.mm(pp[:, bl * 32:(bl + 1) * 32], hT[:, kc, blk * 128:(blk + 1) * 128], wdt[:, kc, :],
                                         start=(bl == 0 and kc == 0))
                            P.tt("dve", xdt[:, half * 16:(half + 1) * 16, :], pp[:, :].rearrange("p (k h) -> p k h", k=16),
                                 dtb[:, :].unsqueeze(1).broadcast_to([128, 16, 32]), ALU.add)
                        P.act(tmpa[:], xdt[:], AF.Abs)
                        P.act(tmpa[:], tmpa[:], AF.Exp, scale=-1.0)
                        P.act(alog[:], alog[:], AF.Exp)
                        P.act(a32[:], tmpa[:], AF.Ln, bias=1.0)
                        P.stt("dve", dt_all[:], xdt[:], 0.0, a32[:], ALU.max, ALU.add)
                        P.tt("dve", a32[:], dt_all[:], alog[:, :].unsqueeze(1).broadcast_to([128, NB, 32]), ALU.mult)
                        P.ts("dve", a32[:], a32[:], -1.0, None, ALU.mult)
                        P.copy("dve", a_hi[:], a32[:])
                        P.tt("dve", tmpa[:], a32[:], a_hi[:], ALU.subtract)
                        P.copy("dve", a_lo[:], tmpa[:])
                        cwh = lambda i, c: pvh[:, PV_CW + i * 32 + c:PV_CW + i * 32 + c + 1]
                        cbh = lambda c: pvh[:, PV_CB + c:PV_CB + c + 1]
                        cnt = [0]
                        v4 = lambda a: a.rearrange("p (k l) -> p k l", k=4)

                        def load_group(g):
                            wgi = wg[g % 2]
                            load_w(wgi[:, :, 0:256], w_in_d[:, C_Z + g * 256:C_Z + (g + 1) * 256], KC, 256, scale=g1h)
                            load_w(wgi[:, :, 256:512], w_in_d[:, C_X + g * 256:C_X + (g + 1) * 256], KC, 256, scale=g1)
                            load_w(wgi[:, :, 512:640], w_in_d[:, C_B + g * 128:C_B + (g + 1) * 128], KC, 128, scale=g1)
                            load_w(wgi[:, :, 640:768], w_in_d[:, C_C + g * 128:C_C + (g + 1) * 128], KC, 128, scale=g1)

                        pc_pp = {}

                        def PCm(W, i):
                            g, w = divmod(W, NT)
                            wgi = wg[g % 2]
                            tsl = slice(512 * w, 512 * (w + 1))
                            pp = ps_p[cnt[0] % 2]
                            cnt[0] += 1
                            pc_pp[(W, i)] = pp
                            c0 = i * 128 if i < 2 else 256 + (i - 2) * 128
                            for kc in range(KC):
                                P.mm(pp[:, 0:512], wgi[:, kc, c0:c0 + 128], hT[:, kc, tsl], start=(kc == 0))

                        def PCe(W, i):
                            g, w = divmod(W, NT)
                            wi = W % 2
                            pp = pc_pp[(W, i)]
                            if i < 2:
                                P.act(thb[i][:], pp[:, 0:512], AF.Tanh)
                                P.stt("dve", zs[wi][:, i, :], thb[i][:], 1.0, pp[:, 0:512], ALU.add, ALU.mult)
                                return
                            ct = i - 2
                            cchunk = (2 * g, 2 * g + 1, 16 + g, 24 + g)
                            if w == 0:
                                P.memset("pool", raw[ct][:, 0:3], 0.0)
                            cc = cchunk[ct]
                            ab = accb[ct]
                            P.copy("act", raw[ct][:, 3:515], pp[:, 0:512])
                            P.act(ab[:], pp[:, 0:512], AF.Identity, scale=cwh(3, cc), bias=cbh(cc))
                            for k_ in range(3):
                                P.stt("dve", ab[:], raw[ct][:, k_:k_ + 512], cwh(k_, cc), ab[:], ALU.mult, ALU.add)
                            P.copy("pool", raw[ct][:, 0:3], raw[ct][:, 512:515])

                        def PCf(W, i):
                            wi = W % 2
                            pc_pp.pop((W, i))
                            if i < 2:
                                return
                            ct = i - 2
                            ab = accb[ct]
                            tb = thb[ct % 2]
                            P.act(tb[:], ab[:], AF.Tanh)
                            P.stt("dve", xbcT[wi][:, ct, :], tb[:], 1.0, ab[:], ALU.add, ALU.mult)

                        def PCt(W):
                            g, w = divmod(W, NT)
                            wi = W % 2
                            for bp in range(2):
                                for k2 in range(2):
                                    blk = 2 * bp + k2
                                    for (ci_, ct) in enumerate((0, 1, 2)):
                                        P.tr(ps_tr[:, k2 * 384 + ci_ * 128:k2 * 384 + (ci_ + 1) * 128],
                                             xbcT[wi][:, ct, blk * 128:(blk + 1) * 128], ident)
                                P.copy("act", xbtok[wi][:, 2 * bp:2 * bp + 2, :], ps_tr[:, 0:768].rearrange("p (k c) -> p k c", k=2))
                            P.tt("dve", Xdt[wi][:, :, :].rearrange("p b (k q) -> p b k q", k=4),
                                 xbtok[wi][:, :, 0:256].rearrange("p b (k q) -> p b k q", k=4),
                                 dt_all[:, 4 * w:4 * w + 4, 4 * g:4 * g + 4].unsqueeze(3).broadcast_to([128, 4, 4, 64]), ALU.mult)
                            for hc in range(2):
                                P.act(xsD[wi][:, hc, :], xbcT[wi][:, hc, :], AF.Copy, scale=pv[:, PV_D + 2 * g + hc:PV_D + 2 * g + hc + 1])

                        def PC(W):
                            for i in range(6):
                                PCm(W, i)
                                PCe(W, i)
                                PCf(W, i)
                            PCt(W)

                        def RG(n):
                            g, cidx = divmod(n, NB)
                            for (R_, asrc) in ((Rh[n % 4], a_hi),):
                                P.tt("pool", v4(R_[:, :]), maskU.unsqueeze(1).broadcast_to([128, 4, 128]),
                                     asrc[:, cidx, 4 * g:4 * g + 4].unsqueeze(2).broadcast_to([128, 4, 128]), ALU.mult)

                        def A(n):
                            g, cidx = divmod(n, NB)
                            W = n // 4
                            wi = W % 2
                            c4 = cidx % 4
                            ci = n % 2
                            tc = slice(128 * c4, 128 * (c4 + 1))
                            BT = xbcT[wi][:, 2, tc]
                            CT = xbcT[wi][:, 3, tc]
                            P.mm(ps_seg[:, 0:512], maskGT, Rh[n % 4][:, :], start=True)
                            P.mm(ps_acs[:, 0:512], ones, Rh[n % 4][:, :], start=True)
                            P.mm(ps_cb[:, 0:128], BT, CT, start=True)
                            dd = ddc[ci]
                            P.act(Eb[ci][:, :], ps_seg[:, 0:512], AF.Exp)
                            P.act(dd[:, 0:4], ps_seg[:, 127:512:128], AF.Exp)
                            P.act(EAb[ci][:, :], ps_acs[:, 0:512], AF.Exp)
                            P.act(dd[:, 4:8], ps_acs[:, 127:512:128], AF.Exp)
                            P.tt("dve", CBm[ci][:, :], ps_cb[:, 0:128], maskU, ALU.mult)
                            P.tt("dve", v4(Mp[ci][:, :]), v4(Eb[ci][:, :]), CBm[ci][:, :].unsqueeze(1).broadcast_to([128, 4, 128]), ALU.mult)
                            P.tt("pool", v4(Cs[ci][:, :]), CT.unsqueeze(1).broadcast_to([128, 4, 128]), v4(EAb[ci][:, :]), ALU.mult)
                            P.tt("dve", Xd[ci][:, :].rearrange("p (k q) -> p k q", k=4),
                                 Xdt[wi][:, c4, :].rearrange("p (k q) -> p k q", k=4),
                                 dd[:, 0:4].unsqueeze(2).broadcast_to([128, 4, 64]), ALU.mult)

                        def Bk(n):
                            g, cidx = divmod(n, NB)
                            W = n // 4
                            wi = W % 2
                            c4 = cidx % 4
                            ci = n % 2
                            tc = slice(128 * c4, 128 * (c4 + 1))
                            dd = ddc[ci]
                            sb_old = Sbf[n % 2]
                            sb_new = Sbf[(n + 1) % 2]
                            firstq = [True, True]
                            for hc in range(2):
                                for hh in range(2):
                                    k = 2 * hc + hh
                                    kw = {} if hh == 0 else {"tile_position": (0, 64)}
                                    o = ps_y[64 * hh:64 * hh + 64, hc * 128:(hc + 1) * 128]
                                    P.mm(o, Xdt[wi][:, c4, k * 64:(k + 1) * 64], v4(Mp[ci][:, :])[:, k, :], start=firstq[hh], **kw)
                                    firstq[hh] = False
                                    if cidx > 0:
                                        P.mm(o, sb_old[:, k * 64:(k + 1) * 64], v4(Cs[ci][:, :])[:, k, :], start=False, **kw)
                            P.mm(ps_st[:, 0:256], xbtok[wi][:, c4, 256:384], Xd[ci][:, :], start=True)
                            if cidx == 0:
                                P.copy("dve", Sst[:, :], ps_st[:, 0:256])
                            else:
                                s3 = Sst[:, :].rearrange("p (k q) -> p k q", k=4)
                                P.tt("dve", s3, s3, dd[:, 4:8].unsqueeze(2).broadcast_to([128, 4, 64]), ALU.mult)
                                P.tt("dve", Sst[:, :], Sst[:, :], ps_st[:, 0:256], ALU.add)
                            P.copy("act", sb_new[:, :], Sst[:, :])
                            P.tt("dve", ybuf[wi][:, :, tc], ps_y[:, 0:256].rearrange("p (c q) -> p c q", c=2), xsD[wi][:, :, tc], ALU.add)

                        def WN(W):
                            g, w = divmod(W, NT)
                            wi = W % 2
                            tsl = slice(512 * w, 512 * (w + 1))
                            P.tt("dve", ybuf[wi][:], ybuf[wi][:], zs[wi][:], ALU.mult)
                            P.act(sqw[:], ybuf[wi][:], AF.Square)
                            pp = ps_p[cnt[0] % 2]
                            cnt[0] += 1
                            P.mm(pp[:, 0:512], ones, sqw[:, 0, :], start=True)
                            P.mm(pp[:, 0:512], ones, sqw[:, 1, :], start=False)
                            P.act(lnw[:], pp[:, 0:512], AF.Ln, scale=1.0 / 256, bias=EPS)
                            P.act(lnw[:], lnw[:], AF.Exp, scale=-0.5)
                            P.tt("dve", yo[wi][:], ybuf[wi][:], lnw[:, :].unsqueeze(1).broadcast_to([128, 2, 512]), ALU.mult)
                            P.dma(ysd_d[b, 2 * g:2 * g + 2, :, tsl].rearrange("c p t -> p c t"), yo[wi][:], f"yo{wi}")

                        NCH = 8 * NB
                        NW = 8 * NT
                        load_group(0)
                        PC(0)
                        RG(0)
                        RG(1)
                        for n in range(NCH):
                            g, cidx = divmod(n, NB)
                            W = n // 4
                            c4 = n % 4
                            if n + 2 < NCH:
                                RG(n + 2)
                            if n > 0:
                                Bk(n - 1)
                            A(n)
                            if n > 0 and c4 == 0:
                                WN(W - 1)
                            if cidx == 12 and g + 1 < 8:
                                load_group(g + 1)
                            if W + 1 < NW:
                                order = (2, 3, 4, 5, 0, 1)
                                o = order
                                if c4 == 0:
                                    PCm(W + 1, o[0]); PCm(W + 1, o[1])
                                elif c4 == 1:
                                    PCe(W + 1, o[0]); PCm(W + 1, o[2]); PCe(W + 1, o[1]); PCm(W + 1, o[3]); PCf(W + 1, o[0])
                                elif c4 == 2:
                                    PCe(W + 1, o[2]); PCf(W + 1, o[1]); PCm(W + 1, o[4]); PCe(W + 1, o[3]); PCf(W + 1, o[2])
                                    PCm(W + 1, o[5]); PCf(W + 1, o[3])
                                    PCt(W + 1)
                                else:
                                    PCe(W + 1, o[4]); PCe(W + 1, o[5]); PCf(W + 1, o[4]); PCf(W + 1, o[5])
                        Bk(NCH - 1)
                        WN(NW - 1)
            P.barrier()

        tiles = [(b, j) for b in range(NS) for j in range(NT)]
        if "merge" in phases:
            with ES() as st:
                Wssd = sbt(st, "m_wssd", [128, 16, D], BF16)
                Watt = sbt(st, "m_watt", [128, 4, D], BF16)
                Wgs = sbt(st, "m_wgs", [128, KC, D], BF16)
                Wga = sbt(st, "m_wga", [128, KC, D], BF16)
                Wout = sbt(st, "m_wout", [128, KC, D], BF16)
                ys = [sbt(st, f"m_ys{i}", [128, 16, 512], BF16) for i in range(2)]
                ya = [sbt(st, f"m_ya{i}", [128, 4, 512], BF16) for i in range(2)]
                hTt = [sbt(st, f"m_ht{i}", [128, KC, 512], BF16) for i in range(2)]
                xt = [sbt(st, f"m_xt{i}", [128, 4, D], F32) for i in range(2)]
                th = [sbt(st, f"m_th{i}", [128, 512], F32) for i in range(2)]
                m1 = [sbt(st, f"m_m1{i}", [128, 512], F32) for i in range(3)]
                mT = sbt(st, "m_mT", [128, KC, 512], BF16)
                psm = [pst(st, f"m_ps{i}") for i in range(8)]
                pcm = [0]

                def m_loads(t):
                    b, j = tiles[t]
                    i2 = t % 2
                    tsl = slice(512 * j, 512 * (j + 1))
                    P.dma(hTt[i2][:], hTs_d[b, :, :, tsl].rearrange("c p t -> p c t"), f"mht{i2}")
                    P.dma(ys[i2][:], ysd_d[b, :, :, tsl].rearrange("c p t -> p c t"), f"mys{i2}")
                    P.dma(ya[i2][:], yat_d[b, :, :, tsl].rearrange("c p t -> p c t"), f"mya{i2}")
                    P.dma(xt[i2][:], x_d[b, tsl, :].rearrange("(k p) d -> p k d", p=128), f"mxt{i2}")

                def m_compute(t):
                    b, j = tiles[t]
                    i2 = t % 2
                    tsl = slice(512 * j, 512 * (j + 1))
                    for oc in range(KC):
                        osl = slice(oc * 128, (oc + 1) * 128)
                        pc = pcm[0]
                        p1, pg1, p2, pg2 = psm[pc % 8], psm[(pc + 1) % 8], psm[(pc + 2) % 8], psm[(pc + 3) % 8]
                        pcm[0] += 4
                        e2 = 0
                        for kc in range(KC):
                            P.mm(pg1[:, 0:512], Wgs[:, kc, osl], hTt[i2][:, kc, :], start=(kc == 0))
                        for kc in range(16):
                            P.mm(p1[:, 0:512], Wssd[:, kc, osl], ys[i2][:, kc, :], start=(kc == 0))
                        for kc in range(KC):
                            P.mm(pg2[:, 0:512], Wga[:, kc, osl], hTt[i2][:, kc, :], start=(kc == 0))
                        for kc in range(4):
                            P.mm(p2[:, 0:512], Watt[:, kc, osl], ya[i2][:, kc, :], start=(kc == 0))
                        P.act(th[e2][:], pg1[:, 0:512], AF.Tanh)
                        P.stt("dve", m1[e2][:], th[e2][:], 1.0, p1[:, 0:512], ALU.add, ALU.mult)
                        P.act(th[e2 + 1][:], pg2[:, 0:512], AF.Tanh)
                        P.stt("dve", m1[e2 + 1][:], th[e2 + 1][:], 1.0, p2[:, 0:512], ALU.add, ALU.mult)
                        P.tt("pool", mT[:, oc, :], m1[e2][:], m1[e2 + 1][:], ALU.add)
                    for blk in range(4):
                        for half in range(2):
                            pp = psm[pcm[0] % 8]
                            pcm[0] += 1
                            hs = slice(half * 512, (half + 1) * 512)
                            for kc in range(KC):
                                P.mm(pp[:, 0:512], mT[:, kc, blk * 128:(blk + 1) * 128], Wout[:, kc, hs], start=(kc == 0))
                            P.tt("dve", xt[i2][:, blk, hs], xt[i2][:, blk, hs], pp[:, 0:512], ALU.add)
                    P.dma(out_d[b, tsl, :].rearrange("(k p) d -> p k d", p=128), xt[i2][:], f"mo{i2}")

                m_loads(0)
                load_w(Wgs, w_in_d[:, C_GS:C_GS + D], KC, D, scale=g1h)
                load_w(Wssd, w_ssd_d, 16, D, scale=pv[:, PV_GS:PV_GS + 16])
                load_w(Wga, w_in_d[:, C_GA:C_GA + D], KC, D, scale=g1h)
                load_w(Watt, w_att_d, 4, D)
                load_w(Wout, w_out_d, KC, D, scale=0.5)
                for t in range(len(tiles)):
                    if t + 1 < len(tiles):
                        m_loads(t + 1)
                    m_compute(t)
            P.barrier()

        if "ffn" in phases:
            with ES() as st:
                Wup = sbt(st, "f_wup", [128, KC, 2 * DFF], BF16)
                Wdn = sbt(st, "f_wdn", [128, NFC, D], BF16)
                xb = [sbt(st, f"f_xb{i}", [128, D], F32) for i in range(2)]
                h16 = sbt(st, "f_h16", [128, 1, D], BF16)
                ssq = [sbt(st, f"f_ss{i}", [128, 1], F32) for i in range(2)]
                lnv = [sbt(st, f"f_ln{i}", [128, 1], F32) for i in range(2)]
                rstd = [sbt(st, f"f_rs{i}", [128, 1], F32) for i in range(2)]
                h2T = [sbt(st, f"f_h2T{i}", [128, KC, 512], BF16) for i in range(2)]
                aT = sbt(st, "f_aT", [128, NFC, 512], BF16)
                tail = sbt(st, "f_tail", [128, 2 * NFC, 2], F32)
                rawf = [sbt(st, f"f_raw{i}", [128, 514], F32) for i in range(2)]
                accf = [sbt(st, f"f_acc{i}", [128, 512], F32) for i in range(4)]
                ostg = [sbt(st, f"f_os{i}", [128, 512], F32) for i in range(2)]
                pbf = [pst(st, f"f_pb{i}", BF16) for i in range(2)]
                psf = [pst(st, f"f_ps{i}") for i in range(6)]
                pcf = [0]
                blkc = [0]
                def fwc(i, ci):
                    src = pvh if ci < NFC else pv
                    return src[:, PV_FW + i * 44 + ci:PV_FW + i * 44 + ci + 1]

                def fbc(ci):
                    src = pvh if ci < NFC else pv
                    return src[:, PV_FB + ci:PV_FB + ci + 1]

                def f_norm(t):
                    b, j = tiles[t]
                    hb = h2T[t % 2]
                    for k in range(4):
                        q = blkc[0] % 2
                        blkc[0] += 1
                        r0 = 512 * j + 128 * k
                        P.dma(xb[q][:], out_d[b, r0:r0 + 128, :], f"fxb{q}")
                        P.memset("pool", ssq[q][:], 0.0)
                        P.act(h16[:, 0, :], xb[q][:], AF.Square, accum_out=ssq[q][:, 0:1])
                        P.act(lnv[q][:], ssq[q][:], AF.Ln, scale=1.0 / D, bias=EPS)
                        P.act(rstd[q][:], lnv[q][:], AF.Exp, scale=-0.5)
                        P.scale("dve" if k % 2 == 0 else "act", h16[:, 0, :], xb[q][:], rstd[q][:, 0:1])
                        bank = pbf[q]
                        for kc in range(KC):
                            P.tr(bank[:, kc * 128:(kc + 1) * 128], h16[:, 0, kc * 128:(kc + 1) * 128], ident)
                        P.copy("act" if k % 2 == 0 else "dve", hb[:, :, k * 128:(k + 1) * 128], bank[:, :].rearrange("p (c t) -> p c t", c=KC))

                def f_up(t):
                    b, j = tiles[t]
                    hb = h2T[t % 2]
                    if j == 0:
                        P.memset("pool", tail[:], 0.0)

                    def up1(c):
                        res = []
                        for half in range(2):
                            ci = half * NFC + c
                            pp = psf[pcf[0] % 6]
                            pcf[0] += 1
                            col = half * DFF + c * 128
                            for kc in range(KC):
                                P.mm(pp[:, 0:512], Wup[:, kc, col:col + 128], hb[:, kc, :], start=(kc == 0))
                            rw = rawf[half]
                            ab = accf[(2 * c + half) % 4]
                            P.copy("pool", rw[:, 0:2], tail[:, ci, :])
                            P.copy("act", rw[:, 2:514], pp[:, 0:512])
                            P.act(ab[:], pp[:, 0:512], AF.Identity, scale=fwc(2, ci), bias=fbc(ci))
                            P.copy("pool", tail[:, ci, :], rw[:, 512:514])
                            for i in range(2):
                                P.stt("dve", ab[:], rw[:, i:i + 512], fwc(i, ci), ab[:], ALU.mult, ALU.add)
                            res.append(ab)
                        return res

                    def up2(c, res):
                        tf = rawf[0][:, 2:514]
                        P.act(tf, res[0][:], AF.Tanh)
                        P.stt("dve", res[0][:], tf, 1.0, res[0][:], ALU.add, ALU.mult)
                        P.tt("pool", aT[:, c, :], res[0][:], res[1][:], ALU.mult)

                    for c in range(NFC):
                        r_ = up1(c)
                        up2(c, r_)

                def f_down(t):
                    b, j = tiles[t]
                    for blk in range(4):
                        r0 = 512 * j + 128 * blk
                        for half in range(2):
                            pp = psf[pcf[0] % 6]
                            q = pcf[0] % 2
                            pcf[0] += 1
                            hs = slice(half * 512, (half + 1) * 512)
                            for c in range(NFC):
                                P.mm(pp[:, 0:512], aT[:, c, blk * 128:(blk + 1) * 128], Wdn[:, c, hs], start=(c == 0))
                            P.copy("act", ostg[q][:], pp[:, 0:512])
                            P.dma(out_d[b, r0:r0 + 128, hs], ostg[q][:], f"fo{q}", eng="pool", accum_op=ALU.add)

                f_norm(0)
                load_w(Wup, w_up_d, KC, 2 * DFF, scale=pv[:, PV_G2:PV_G2 + 8])
                load_w(Wdn, w_dn_d, NFC, D)
                nt_ = len(tiles)
                f_up(0)
                for t in range(nt_):
                    if t + 1 < nt_:
                        f_norm(t + 1)
                    f_down(t)
                    if t + 1 < nt_:
                        f_up(t + 1)
        P.barrier(final=True)
        P.emit()
    return nc, P


_CACHE = {}


def _core_inputs(inp, c, NS):
    x = np.ascontiguousarray(np.asarray(inp["x"], dtype=np.float32)[c * NS:(c + 1) * NS])
    return x


def kernel(**inputs):
    NCORES = 8
    NS = 2
    if "nc" not in _CACHE:
        _CACHE["nc"] = build(NS=NS)[0]
    nc = _CACHE["nc"]
    f = lambda k: np.ascontiguousarray(np.asarray(inputs[k], dtype=np.float32)[0])
    shared = {
        "w_in": f("w_in"), "w_ssd_proj": f("w_ssd_proj"), "w_attn_proj": f("w_attn_proj"), "w_out": f("w_out"),
        "w_up": f("w_up"), "w_down": f("w_down"), "pvec": pack_pvec(inputs), "cmask": make_cmask(),
        "dt_bias": np.asarray(inputs["dt_bias"], np.float32).reshape(1, 32),
        "a_log": np.asarray(inputs["a_log"], np.float32).reshape(1, 32),
    }
    in_maps = []
    for c in range(NCORES):
        m = dict(shared)
        m["x"] = _core_inputs(inputs, c, NS)
        in_maps.append(m)
    res = run_bass_kernel_spmd(nc, in_maps, core_ids=list(range(NCORES)))
    out = np.concatenate([np.asarray(r["out"]) for r in res.results], axis=0)
    return out.astype(np.float32)
```
